# Optimizing a Trainium2 kernel written in Bass

```python
import jax, jax.numpy as jnp
from jax import lax
import numpy as np

D_MODEL = 1024
BATCH = 16
SEQ = 4096
DEPTH = 2
DEC_BATCH = 8
DEC_SEQ = 64
PAST_LEN = 2048

CHUNK = 64
EPS = 1e-6
SSD_HEADS = 16
SSD_HEAD_DIM = 64
SSD_INNER = SSD_HEADS * SSD_HEAD_DIM
SSD_GROUPS = 2
SSD_STATE = 128
SSD_CONV = 4
SSD_CONV_DIM = SSD_INNER + 2 * SSD_GROUPS * SSD_STATE
SC_WIDTH = 1024
SC_CONV = 3
AB_PROJ = SSD_INNER + SSD_CONV_DIM + SSD_HEADS + 3 * SC_WIDTH
AB_OUT = SSD_INNER + SC_WIDTH
LRU_WIDTH = 1024
LRU_HEADS = 4
LRU_BW = LRU_WIDTH // LRU_HEADS
LRU_CONV = 4
LRU_C = 8.0
D_FF = 2816
FFN_CONV = 3

kernel_name = "hybrid_ssd_shortconv_rglru_stream_step"


def rmsnorm(x, g):
    xf = x.astype(jnp.float32)
    y = xf * lax.rsqrt(jnp.mean(xf * xf, axis=-1, keepdims=True) + EPS)
    return (y * g.astype(jnp.float32)).astype(x.dtype)


def causal_dwconv(x, w, b, buf):
    K = w.shape[0]
    L = x.shape[1]
    xp = jnp.concatenate([buf.astype(x.dtype), x], axis=1)
    y = b + sum(xp[:, k:k + L] * w[k] for k in range(K))
    return y, xp[:, L:]


def ssd_scan(x, dt, A, Bm, Cm, h0):
    b, L, H, P = x.shape
    G, N = Bm.shape[2], Bm.shape[3]
    K = H // G
    cl = CHUNK if L % CHUNK == 0 else L
    nc = L // cl
    X = (x * dt[..., None]).reshape(b, nc, cl, G, K, P)
    dA = (dt * A).reshape(b, nc, cl, G, K)
    Bc = Bm.reshape(b, nc, cl, G, N)
    Cc = Cm.reshape(b, nc, cl, G, N)
    At = jnp.moveaxis(jnp.cumsum(dA, axis=2), 2, -1)
    mask = jnp.tril(jnp.ones((cl, cl), dtype=bool))
    seg = At[..., :, None] - At[..., None, :]
    Lmat = jnp.where(mask, jnp.exp(jnp.where(mask, seg, 0.0)), 0.0)
    CB = jnp.einsum('bclgn,bcsgn->bcgls', Cc, Bc)
    y_diag = jnp.einsum('bcgls,bcgkls,bcsgkp->bclgkp', CB, Lmat, X)
    decay_states = jnp.exp(At[..., -1:] - At)
    states = jnp.einsum('bclgn,bcgkl,bclgkp->bcgkpn', Bc, decay_states, X)
    chunk_decay = jnp.exp(At[..., -1])

    def step(h, inp):
        dec, s = inp
        return dec[..., None, None] * h + s, h

    h_fin, h_enter = lax.scan(step, h0.reshape(b, G, K, P, N),
                              (jnp.moveaxis(chunk_decay, 1, 0), jnp.moveaxis(states, 1, 0)))
    h_enter = jnp.moveaxis(h_enter, 0, 1)
    y_off = jnp.einsum('bclgn,bcgkpn,bcgkl->bclgkp', Cc, h_enter, jnp.exp(At))
    y = (y_diag + y_off).reshape(b, L, H, P)
    return y, h_fin.reshape(b, H, P, N)


def mixer_ab(h, st_conv, st_ssd, st_sc, w_in, conv_w, conv_b, dt_bias, a_log, d_skip, gnorm,
             sc_conv_w, sc_conv_b, w_out):
    b, L, _ = h.shape
    dtype = h.dtype
    proj = h @ w_in
    s1 = SSD_INNER
    s2 = s1 + SSD_CONV_DIM
    s3 = s2 + SSD_HEADS
    s4 = s3 + SC_WIDTH
    s5 = s4 + SC_WIDTH
    z, xbc, dt_raw = proj[..., :s1], proj[..., s1:s2], proj[..., s2:s3]
    g_b, g_c, sh = proj[..., s3:s4], proj[..., s4:s5], proj[..., s5:]
    xbc, new_conv = causal_dwconv(xbc, conv_w, conv_b, st_conv)
    xbc = jax.nn.silu(xbc).astype(jnp.float32)
    gn = SSD_GROUPS * SSD_STATE
    xs = xbc[..., :SSD_INNER].reshape(b, L, SSD_HEADS, SSD_HEAD_DIM)
    Bm = xbc[..., SSD_INNER:SSD_INNER + gn].reshape(b, L, SSD_GROUPS, SSD_STATE)
    Cm = xbc[..., SSD_INNER + gn:].reshape(b, L, SSD_GROUPS, SSD_STATE)
    dt = jax.nn.softplus(dt_raw.astype(jnp.float32) + dt_bias.astype(jnp.float32))
    A = -jnp.exp(a_log.astype(jnp.float32))
    y, new_ssd = ssd_scan(xs, dt, A, Bm, Cm, st_ssd.astype(jnp.float32))
    y = (y + d_skip.astype(jnp.float32)[:, None] * xs).reshape(b, L, SSD_INNER)
    y = rmsnorm((y * jax.nn.silu(z.astype(jnp.float32))), gnorm).astype(dtype)
    u, new_sc = causal_dwconv(g_c * sh, sc_conv_w, sc_conv_b, st_sc)
    ysc = g_b * u
    out = jnp.concatenate([y, ysc], axis=-1) @ w_out
    return out, new_conv, new_ssd.astype(dtype), new_sc


def mixer_c(h, st_conv, st_lru, w_in, conv_w, conv_b, wa, ba, wx, bx, lam, w_out):
    b, L, _ = h.shape
    dtype = h.dtype
    proj = h @ w_in
    gate, xb = proj[..., :LRU_WIDTH], proj[..., LRU_WIDTH:]
    xb, new_conv = causal_dwconv(xb, conv_w, conv_b, st_conv)
    xh = xb.reshape(b, L, LRU_HEADS, LRU_BW)
    r = jax.nn.sigmoid(jnp.einsum('blhi,hij->blhj', xh, wa) + ba).reshape(b, L, LRU_WIDTH)
    i = jax.nn.sigmoid(jnp.einsum('blhi,hij->blhj', xh, wx) + bx).reshape(b, L, LRU_WIDTH)
    log_a = -LRU_C * r.astype(jnp.float32) * jax.nn.softplus(-lam.astype(jnp.float32))
    a = jnp.exp(log_a)
    mult = jnp.sqrt(-jnp.expm1(2.0 * log_a))
    u = mult * (i * xb).astype(jnp.float32)
    u = u.at[:, 0].add(a[:, 0] * st_lru.astype(jnp.float32))

    def comb(left, right):
        a1, b1 = left
        a2, b2 = right
        return a1 * a2, a2 * b1 + b2

    _, hs = lax.associative_scan(comb, (a, u), axis=1)
    out = (jax.nn.gelu(gate) * hs.astype(dtype)) @ w_out
    return out, new_conv, hs[:, -1].astype(dtype)


def conv_ffn(h, st, w_in, conv_w, conv_b, w_out):
    gu = h @ w_in
    g, u = gu[..., :D_FF], gu[..., D_FF:]
    g, new_st = causal_dwconv(g, conv_w, conv_b, st)
    return (jax.nn.gelu(g) * u) @ w_out, new_st


def setup_inputs(seed: int = 0) -> dict:
    key = jax.random.key(seed)
    it = iter(list(jax.random.split(key, 48)))
    f32 = jnp.float32

    def nrm(shape, s):
        return jax.random.normal(next(it), shape, f32) * s

    def uni(shape, lo, hi):
        return jax.random.uniform(next(it), shape, f32, lo, hi)

    dt0 = jnp.exp(uni((SSD_HEADS,), float(np.log(1e-3)), float(np.log(1e-1))))
    a0 = uni((LRU_WIDTH,), 0.9, 0.999)
    return {
        "x_prompt": nrm((BATCH, SEQ, D_MODEL), 1.0),
        "x_sample": nrm((DEC_BATCH, DEC_SEQ, D_MODEL), 1.0),
        "state_ssd_conv": nrm((DEC_BATCH, SSD_CONV - 1, SSD_CONV_DIM), 1.0),
        "state_ssd": nrm((DEC_BATCH, SSD_HEADS, SSD_HEAD_DIM, SSD_STATE), 0.1),
        "state_sconv": nrm((DEC_BATCH, SC_CONV - 1, SC_WIDTH), 1.0),
        "state_lru_conv": nrm((DEC_BATCH, LRU_CONV - 1, LRU_WIDTH), 1.0),
        "state_lru": nrm((DEC_BATCH, LRU_WIDTH), 0.5),
        "state_ffn_conv": nrm((DEPTH, DEC_BATCH, FFN_CONV - 1, D_FF), 1.0),
        "norm_mix": 1.0 + nrm((DEPTH, D_MODEL), 0.02),
        "norm_ffn": 1.0 + nrm((DEPTH, D_MODEL), 0.02),
        "norm_final": 1.0 + nrm((D_MODEL,), 0.02),
        "ab_w_in": nrm((D_MODEL, AB_PROJ), D_MODEL ** -0.5),
        "ssd_conv_w": nrm((SSD_CONV, SSD_CONV_DIM), SSD_CONV ** -0.5),
        "ssd_conv_b": nrm((SSD_CONV_DIM,), 0.02),
        "ssd_dt_bias": dt0 + jnp.log(-jnp.expm1(-dt0)),
        "ssd_a_log": jnp.log(uni((SSD_HEADS,), 1.0, 16.0)),
        "ssd_d": 1.0 + nrm((SSD_HEADS,), 0.1),
        "ssd_norm": 1.0 + nrm((SSD_INNER,), 0.02),
        "sc_conv_w": nrm((SC_CONV, SC_WIDTH), SC_CONV ** -0.5),
        "sc_conv_b": nrm((SC_WIDTH,), 0.02),
        "ab_w_out": nrm((AB_OUT, D_MODEL), AB_OUT ** -0.5),
        "lru_w_in": nrm((D_MODEL, 2 * LRU_WIDTH), D_MODEL ** -0.5),
        "lru_conv_w": nrm((LRU_CONV, LRU_WIDTH), LRU_CONV ** -0.5),
        "lru_conv_b": nrm((LRU_WIDTH,), 0.02),
        "lru_wa": nrm((LRU_HEADS, LRU_BW, LRU_BW), LRU_BW ** -0.5),
        "lru_ba": nrm((LRU_HEADS, LRU_BW), 0.02),
        "lru_wx": nrm((LRU_HEADS, LRU_BW, LRU_BW), LRU_BW ** -0.5),
        "lru_bx": nrm((LRU_HEADS, LRU_BW), 0.02),
        "lru_lambda": jnp.log(a0) - jnp.log1p(-a0),
        "lru_w_out": nrm((LRU_WIDTH, D_MODEL), LRU_WIDTH ** -0.5),
        "ffn_w_in": nrm((DEPTH, D_MODEL, 2 * D_FF), D_MODEL ** -0.5),
        "ffn_conv_w": nrm((DEPTH, FFN_CONV, D_FF), FFN_CONV ** -0.5),
        "ffn_conv_b": nrm((DEPTH, D_FF), 0.02),
        "ffn_w_out": nrm((DEPTH, D_FF, D_MODEL), D_FF ** -0.5),
    }


def reference(x_prompt, x_sample, state_ssd_conv, state_ssd, state_sconv, state_lru_conv, state_lru,
              state_ffn_conv, norm_mix, norm_ffn, norm_final, ab_w_in, ssd_conv_w, ssd_conv_b,
              ssd_dt_bias, ssd_a_log, ssd_d, ssd_norm, sc_conv_w, sc_conv_b, ab_w_out, lru_w_in,
              lru_conv_w, lru_conv_b, lru_wa, lru_ba, lru_wx, lru_bx, lru_lambda, lru_w_out,
              ffn_w_in, ffn_conv_w, ffn_conv_b, ffn_w_out):

    def run(x, s_ssd_conv, s_ssd, s_sc, s_lru_conv, s_lru, s_ffn):
        ffn_new = []
        for l in range(DEPTH):
            hn = rmsnorm(x, norm_mix[l])
            if l % 2 == 0:
                m, n_ssd_conv, n_ssd, n_sc = mixer_ab(
                    hn, s_ssd_conv, s_ssd, s_sc, ab_w_in, ssd_conv_w, ssd_conv_b, ssd_dt_bias,
                    ssd_a_log, ssd_d, ssd_norm, sc_conv_w, sc_conv_b, ab_w_out)
            else:
                m, n_lru_conv, n_lru = mixer_c(
                    hn, s_lru_conv, s_lru, lru_w_in, lru_conv_w, lru_conv_b, lru_wa, lru_ba,
                    lru_wx, lru_bx, lru_lambda, lru_w_out)
            x = x + m
            f, nf = conv_ffn(rmsnorm(x, norm_ffn[l]), s_ffn[l], ffn_w_in[l], ffn_conv_w[l],
                             ffn_conv_b[l], ffn_w_out[l])
            x = x + f
            ffn_new.append(nf)
        return (rmsnorm(x, norm_final), n_ssd_conv, n_ssd, n_sc, n_lru_conv, n_lru,
                jnp.stack(ffn_new, axis=0))

    bp = x_prompt.shape[0]
    dtp = x_prompt.dtype
    y_prompt, p_ssd_conv, p_ssd, p_sconv, p_lru_conv, p_lru, p_ffn_conv = run(
        x_prompt,
        jnp.zeros((bp, SSD_CONV - 1, SSD_CONV_DIM), dtp),
        jnp.zeros((bp, SSD_HEADS, SSD_HEAD_DIM, SSD_STATE), dtp),
        jnp.zeros((bp, SC_CONV - 1, SC_WIDTH), dtp),
        jnp.zeros((bp, LRU_CONV - 1, LRU_WIDTH), dtp),
        jnp.zeros((bp, LRU_WIDTH), dtp),
        jnp.zeros((DEPTH, bp, FFN_CONV - 1, D_FF), dtp))
    y_sample, s_ssd_conv, s_ssd, s_sconv, s_lru_conv, s_lru, s_ffn_conv = run(
        x_sample, state_ssd_conv, state_ssd, state_sconv, state_lru_conv, state_lru,
        state_ffn_conv)
    return (y_prompt, y_sample, p_ssd_conv, p_ssd, p_sconv, p_lru_conv, p_lru, p_ffn_conv,
            s_ssd_conv, s_ssd, s_sconv, s_lru_conv, s_lru, s_ffn_conv)
```

```python
import numpy as np
from contextlib import ExitStack
import concourse.bass as bass
import concourse.mybir as mybir
from concourse.alu_op_type import AluOpType as ALU
from concourse.bass_utils import run_bass_kernel_spmd

F32 = mybir.dt.float32
BF16 = mybir.dt.bfloat16
AF = mybir.ActivationFunctionType
NCORES = 8
D = 1024
DFF = 2816
NJF = 22
EPS = 1e-6
USZ = 4096
import os
STAGE = int(os.environ.get('KSTAGE', '9'))
SUB = int(os.environ.get('KSUB', '9'))
LAZY = os.environ.get('KLAZY', '1') == '1'
LZQ = os.environ.get('KLZQ', 'act')


class Buf:
    __slots__ = ("name", "w", "rl", "x")

    def __init__(self, name):
        self.name = name
        self.w = None
        self.rl = []
        self.x = False


class Rec:
    def __init__(self):
        self.calls = []

    def __getattr__(self, name):
        def m(*a, **k):
            self.calls.append((name, a, k))
            return None
        return m


def _free(ap):
    n = 1
    for d in ap.shape[1:]:
        n *= d
    return n


STALLDBG = None


class Sched:
    ENGS = ("pe", "dve", "act", "pool", "sp")
    WIN = {"pe": int(os.environ.get("KWPE", "40")), "dve": int(os.environ.get("KWV", "24")), "act": int(os.environ.get("KWV", "24")),
           "pool": int(os.environ.get("KWV", "24")), "sp": 1}
    TBL = float(os.environ.get("KTBL", "1300"))
    SCL = {e: float(os.environ.get("KS_" + e, "1")) for e in ("pe", "dve", "act", "pool", "sp")}

    def __init__(self, nc, stack):
        self.nc = nc
        self.stack = stack
        self.nodes = []
        self.nbuf = 0

    def buf(self, name=None):
        self.nbuf += 1
        return Buf(name or ("b%d" % self.nbuf))

    def bufs(self, n, name="b"):
        return [self.buf("%s%d" % (name, i)) for i in range(n)]

    def _est(self, en, calls):
        t = 0.0
        for name, a, k in calls:
            if name == "dma_start":
                o = k["out"]
                t = max(t, 2000.0 + o.shape[0] * _free(o) * (2 if o.dtype == BF16 else 4) / 160.0)
            elif en == "pe":
                if name == "matmul":
                    rhs = k["rhs"]
                    t += (_free(rhs) * 0.45 + 14) * (4.0 if rhs.dtype == F32 else 1.0)
                else:
                    t += 220.0 if k["in_"].dtype == F32 else 70.0
            elif name == "dma_start":
                o = k["out"]
                t = max(t, 2000.0 + o.shape[0] * _free(o) * (2 if o.dtype == BF16 else 4) / 160.0)
            else:
                o = k.get("out", a[0] if a else None)
                n = _free(o) if o is not None else 64
                if en == "dve":
                    t += 130 + n * (2.1 if name == "tensor_tensor_scan" else (6.0 if name == "reciprocal" else 1.04))
                elif en == "act":
                    t += 230 + n * 0.83
                else:
                    t += 300 + n * 1.35
        return t

    def _add(self, en, calls, reads, writes, dkey=None):
        nid = len(self.nodes)
        deps = {}

        def need(d, sem):
            if d is None:
                return
            deps[d] = deps.get(d, False) or sem
        for b in reads:
            need(b.w, True)
            if b.x:
                for r in b.rl:
                    if self.nodes[r]["eng"] != en or self.nodes[r]["dkey"]:
                        need(r, True)
        for b in writes:
            if b.w is not None:
                need(b.w, True)
            for r in b.rl:
                need(r, True)
        for b in reads:
            b.rl.append(nid)
        for b in writes:
            b.w = nid
            b.rl = []
        aset = None
        if en == "act":
            fn = calls[0][2].get("func")
            aset = {AF.Exp: "exp", AF.Ln: "ln", AF.Silu: "silu", AF.Gelu_apprx_tanh: "gelu", AF.Sigmoid: "sig", AF.Sqrt: "sqrt",
                    AF.Tanh: "exp"}.get(fn)
        self.nodes.append(dict(eng=en, calls=calls, deps=list(deps.items()), dkey=dkey, ndma=len(calls) if dkey else 0,
                               dur=self._est(en, calls) * self.SCL[en], aset=aset))
        return nid

    def op(self, en, emit, reads=(), writes=()):
        self.ops(en, [emit], reads, writes)

    def ops(self, en, emits, reads=(), writes=()):
        rec = Rec()
        for e in emits:
            e(rec)
        assert len(rec.calls) >= 1
        self._add(en, rec.calls, reads, writes)

    def dma(self, qn, key, emit, reads=(), writes=(), n=None):
        rec = Rec()
        emit(rec)
        assert len(rec.calls) >= 1
        self._add(qn, rec.calls, reads, writes, dkey=key)

    def _schedule(self):
        nodes = self.nodes
        q = {e: [] for e in self.ENGS}
        for i, nd in enumerate(nodes):
            q[nd["eng"]].append(i)
        head = {e: 0 for e in self.ENGS}
        emitted = [False] * len(nodes)
        fin = [0.0] * len(nodes)
        etime = {e: 0.0 for e in self.ENGS}
        order = {e: [] for e in self.ENGS}
        remaining = len(nodes)
        cur_set = [None]
        nswitch = [0]
        self.nswitch = nswitch
        NOSCHED = os.environ.get("KNOSCHED", "0") == "1"
        while remaining:
            best = None
            for e in self.ENGS:
                lst = q[e]
                h = head[e]
                while h < len(lst) and emitted[lst[h]]:
                    h += 1
                head[e] = h
                if h >= len(lst):
                    continue
                cand = None
                cnt = 0
                i = h
                W = 1 if NOSCHED else self.WIN[e]
                while i < len(lst) and cnt < W:
                    nid = lst[i]
                    i += 1
                    if emitted[nid]:
                        continue
                    cnt += 1
                    ok = True
                    rdy = 0.0
                    for d, sem in nodes[nid]["deps"]:
                        if not emitted[d]:
                            ok = False
                            break
                        lat = 60.0 if not sem else (150.0 if nodes[d]["eng"] == e and not nodes[d]["dkey"] else 300.0)
                        fd = fin[d] + lat if sem else 0.0
                        if fd > rdy:
                            rdy = fd
                    if not ok:
                        continue
                    pen = 0.0
                    if e == "act" and nodes[nid]["aset"] is not None and nodes[nid]["aset"] != cur_set[0]:
                        pen = self.TBL
                    eff = max(rdy, etime[e]) + pen
                    if cand is None or eff < cand[0] - 1e-9:
                        cand = (eff, nid, pen)
                    if eff <= etime[e]:
                        break
                if cand is None:
                    continue
                start = cand[0]
                if best is None or start < best[0]:
                    best = (start, cand[1], e)
            assert best is not None, "scheduler deadlock"
            start, nid, e = best
            if STALLDBG is not None:
                crit = None
                cf = -1.0
                for d, sem in nodes[nid]["deps"]:
                    if sem and fin[d] > cf:
                        cf = fin[d]
                        crit = d
                STALLDBG.append((e, nid, start, etime[e], crit))
            emitted[nid] = True
            nd = nodes[nid]
            if e == "act" and nd["aset"] is not None:
                if nd["aset"] != cur_set[0]:
                    nswitch[0] += 1
                cur_set[0] = nd["aset"]
            if nd["dkey"]:
                etime[e] = start + 60.0
                fin[nid] = start + nd["dur"]
            else:
                fin[nid] = start + nd["dur"]
                etime[e] = fin[nid]
            order[e].append(nid)
            remaining -= 1
        self.makespan = max(fin) if fin else 0.0
        return order

    def emit_all(self, final_q="sp"):
        nc = self.nc
        nodes = self.nodes
        order = self._schedule()
        sems = {e: self.stack.enter_context(nc.semaphore("s_" + e)) for e in self.ENGS}
        dsems = {}
        ev = [None] * len(nodes)
        cnt = {e: 0 for e in self.ENGS}
        dcnt = {}
        for e in self.ENGS:
            for nid in order[e]:
                nd = nodes[nid]
                if nd["dkey"]:
                    k = nd["dkey"]
                    if k not in dsems:
                        dsems[k] = self.stack.enter_context(nc.semaphore("d_" + k))
                        dcnt[k] = 0
                    dcnt[k] += 16 * nd["ndma"]
                    ev[nid] = ("d_" + k, dsems[k], dcnt[k])
                else:
                    cnt[e] += 1
                    ev[nid] = ("s_" + e, sems[e], cnt[e])
        progs = {}
        for e in self.ENGS:
            prog = []
            waited = {}
            for nid in order[e]:
                nd = nodes[nid]
                for d, sem in nd["deps"]:
                    if not sem:
                        continue
                    sname, sh, val = ev[d]
                    if waited.get(sname, 0) >= val:
                        continue
                    waited[sname] = val
                    prog.append(("w", sh, val))
                prog.append(("i", nd["calls"], ev[nid][1], 16 if nd["dkey"] else 1, bool(nd["dkey"])))
            progs[e] = prog
        waited = {}
        fin_waits = []
        for k, sh in dsems.items():
            fin_waits.append(("w", sh, dcnt[k]))
        for e in ("pe", "dve", "act", "pool"):
            if cnt[e]:
                fin_waits.append(("w", sems[e], cnt[e]))
        progs[final_q] = progs[final_q] + fin_waits
        with nc.Block() as block:
            for n, attr in (("sp", "sync"), ("pe", "tensor"), ("dve", "vector"), ("act", "scalar"), ("pool", "gpsimd")):
                prog = progs[n]
                if not prog:
                    continue

                def body(eng, prog=prog):
                    for it in prog:
                        if it[0] == "w":
                            eng.wait_ge(it[1], it[2])
                        else:
                            _, calls, sem, inc, all_inc = it
                            for idx, (name, a, k) in enumerate(calls):
                                inst = getattr(eng, name)(*a, **k)
                                if all_inc or idx == len(calls) - 1:
                                    inst.then_inc(sem, inc)
                getattr(block, attr)(body)


VEC_SPECS = [
    ("norm_mix", 16), ("norm_ffn", 16), ("norm_final", 8), ("ssd_norm", 8), ("sc_conv_b", 8),
    ("lru_conv_b", 8), ("lru_lambda", 8), ("lru_ba", 8), ("lru_bx", 8), ("ssd_conv_w", 48),
    ("ssd_conv_b", 12), ("sc_conv_w", 24), ("lru_conv_w", 32), ("ffn_conv_w", 132), ("ffn_conv_b", 44),
]
W_SHAPES = {
    "norm_mix": [2, D], "norm_ffn": [2, D], "norm_final": [D], "ab_w_in": [D, 5648], "ssd_conv_w": [4, 1536],
    "ssd_conv_b": [1536], "ssd_dt_bias": [16], "ssd_a_log": [16], "ssd_d": [16], "ssd_norm": [D],
    "sc_conv_w": [3, D], "sc_conv_b": [D], "ab_w_out": [2048, D], "lru_w_in": [D, 2048], "lru_conv_w": [4, D],
    "lru_conv_b": [D], "lru_wa": [4, 256, 256], "lru_ba": [4, 256], "lru_wx": [4, 256, 256], "lru_bx": [4, 256],
    "lru_lambda": [D], "lru_w_out": [D, D], "ffn_w_in": [2, D, 2 * DFF], "ffn_conv_w": [2, 3, DFF],
    "ffn_conv_b": [2, DFF], "ffn_w_out": [2, DFF, D],
}
CAR_SPECS = [("ssd_conv", 36), ("sc", 16), ("lc", 24), ("l", 8), ("f", 88)]


def build_nc(NPS, LP, LS):
    nc = bass.Bass("TRN2", target_bir_lowering=False)
    NS = NPS + 1
    din = {}

    def DI(name, shape):
        din[name] = nc.dram_tensor(name, list(shape), F32, kind="ExternalInput").ap()
        return din[name]

    def DO(name, shape):
        return nc.dram_tensor(name, list(shape), F32, kind="ExternalOutput").ap()

    xp = DI("xp", [NPS, LP, D])
    xs_in = DI("xs", [1, LS, D])
    st_in = {"ssd_conv": DI("st_ssd_conv", [3, 1536]), "sc": DI("st_sc", [2, D]), "lc": DI("st_lc", [3, D]),
             "l": DI("st_l", [D]), "f": DI("st_f", [2, 2, DFF])}
    st_ssd = DI("st_ssd", [16, 64, 128])
    Wd = {k: DI(k, v) for k, v in W_SHAPES.items()}
    yp = DO("yp", [NPS, LP, D])
    ys = DO("ys", [1, LS, D])
    st_out = {"ssd_conv": DO("o_ssd_conv", [NS, 3, 1536]), "sc": DO("o_sc", [NS, 2, D]), "lc": DO("o_lc", [NS, 3, D]),
              "l": DO("o_l", [NS, D]), "f": DO("o_f", [2, NS, 2, DFF])}
    o_ssd = DO("o_ssd", [NS, 16, 64, 128])

    units = []

    def add_unit(nk, ncols, pieces, tag):
        units.append(dict(nk=nk, ncols=ncols, pieces=pieces, tag=tag))
        return len(units) - 1

    def colsrc(w2d, c0, cw):
        return lambda k0, nk: w2d[k0 * 128:(k0 + nk) * 128, c0:c0 + cw].rearrange("(k p) n -> p k n", p=128)

    abin = Wd["ab_w_in"]
    U = {}
    U["dt"] = add_unit(8, 48, [(colsrc(abin, 2560, 16), 0, 16), (colsrc(abin, 2560, 16), 32, 16)], "dt")
    U["xbc"] = [add_unit(8, 512, [(colsrc(abin, 1024 + 512 * i, 512), 0, 512)], "xbc") for i in range(3)]
    U["z"] = [add_unit(8, 512, [(colsrc(abin, 512 * i, 512), 0, 512)], "z") for i in range(2)]
    U["sc"] = [add_unit(8, 384, [(colsrc(abin, 3600 + 128 * j, 128), 0, 128), (colsrc(abin, 4624 + 128 * j, 128), 128, 128),
                                 (colsrc(abin, 2576 + 128 * j, 128), 256, 128)], "sc") for j in range(8)]
    U["abo"] = [add_unit(16, 256, [(colsrc(Wd["ab_w_out"], 256 * i, 256), 0, 256)], "abo") for i in range(4)]
    U["fin"] = []
    U["fout"] = []
    for l in range(2):
        wi = Wd["ffn_w_in"][l]
        U["fin"].append([add_unit(8, 512, [(colsrc(wi, 256 * i, 128), 0, 128), (colsrc(wi, DFF + 256 * i, 128), 128, 128),
                                           (colsrc(wi, 256 * i + 128, 128), 256, 128), (colsrc(wi, DFF + 256 * i + 128, 128), 384, 128)],
                                  "fin") for i in range(11)])
        U["fout"].append([add_unit(22, 128, [(colsrc(Wd["ffn_w_out"][l], 128 * m, 128), 0, 128)], "fout") for m in range(8)])
    U["lin"] = [add_unit(8, 512, [(colsrc(Wd["lru_w_in"], 512 * i, 512), 0, 512)], "lin") for i in range(4)]
    wa2 = Wd["lru_wa"].rearrange("h k n -> (h k) n")
    wx2 = Wd["lru_wx"].rearrange("h k n -> (h k) n")
    U["ax"] = [add_unit(8, 256, [("ax", wa2, wx2, hb)], "ax") for hb in (0, 2)]
    U["lout"] = [add_unit(8, 512, [(colsrc(Wd["lru_w_out"], 512 * i, 512), 0, 512)], "lout") for i in range(2)]
    NU = len(units)
    WS = nc.dram_tensor("WS", [NU, 128, USZ], BF16, kind="Internal").ap()

    tile_seq = []
    if STAGE >= 3:
        tile_seq += [U["dt"]]
        if SUB >= 2:
            tile_seq += U["xbc"] + U["z"]
        if SUB >= 5:
            tile_seq += U["sc"]
        if SUB >= 6:
            tile_seq += U["abo"]
    if STAGE >= 4:
        tile_seq += U["fin"][0] + U["fout"][0]
    if STAGE >= 5:
        tile_seq += [U["lin"][2], U["lin"][3], U["ax"][0], U["lin"][0], U["ax"][1], U["lin"][1]] + U["lout"]
    if STAGE >= 6:
        tile_seq += U["fin"][1] + U["fout"][1]

    seqs = [("p", i, LP) for i in range(NPS)] + [("s", 0, LS)]
    tiles = []
    for si, (kind, idx, L) in enumerate(seqs):
        T = min(512, L)
        assert L % T == 0
        for ti in range(L // T):
            tiles.append(dict(si=si, kind=kind, idx=idx, t0=ti * T, T=T, first=(ti == 0), last=(ti == L // T - 1)))
    full_seq = tile_seq * len(tiles)

    with ExitStack() as st:
        S = Sched(nc, st)

        def sb(name, shape, dt):
            return st.enter_context(nc.sbuf_tensor(name, shape, dt))

        def psum(name, shape, dt):
            return st.enter_context(nc.psum_tensor(name, shape, dt))

        NRING = int(os.environ.get('KNRING', '3'))
        ring = [sb("ring%d" % i, [128, USZ], BF16) for i in range(NRING)]
        Bring = S.bufs(NRING, "ring")
        x_res = sb("x_res", [128, 8, 512], F32)
        Bx = S.bufs(8, "xres")
        xin = [sb("xin%d" % i, [128, D], F32) for i in range(4)]
        Bxin = S.bufs(4, "xin")
        yout = [sb("yout%d" % i, [128, D], F32) for i in range(2)]
        Byout = S.bufs(2, "yout")
        BF8 = [sb("bf8_%d" % i, [128, 8, 512], BF16) for i in range(5)]
        BBF8 = [S.bufs(8, "bf8_%d_" % i) for i in range(5)]
        NFT = int(os.environ.get('KNFT', '11'))
        F32T = [sb("f32t%d" % i, [128, 516], F32) for i in range(NFT)]
        BFT = S.bufs(NFT, "f32t")
        BC = sb("BC", [128, 4, 512], BF16)
        BBC = S.bufs(4, "BC")
        YG = sb("YG", [128, 8, 512], F32)
        BYG = S.bufs(8, "YG")
        sqb = [sb("sqb%d" % i, [128, 512], BF16) for i in range(4)]
        Bsqb = S.bufs(4, "sqb")
        t_e = sb("t_e", [48, 512], F32)
        t_dt = sb("t_dt", [48, 512], F32)
        t_ln = t_e
        t_dA = sb("t_dA", [48, 512], F32)
        t_At = sb("t_At", [48, 512], F32)
        t_w = t_dA
        LT = sb("LT", [48, 512], F32)
        Bte, Btdt, Btln, BtdA, BtAt, Btw, BLT = S.bufs(7, "ssdsm")
        Btln = Bte
        Btw = BtdA
        RH = [sb("RH%d" % i, [48, 1024], F32) for i in range(2)]
        BRH = S.bufs(2, "RH")
        segm = sb("segm", [128, 1024], F32)
        Bsegm = S.buf("segm")
        Lb = sb("Lb", [128, 1024], BF16)
        BLb = S.buf("Lb")
        Mb = [sb("Mb%d" % i, [128, 1024], BF16) for i in range(2)]
        BMb = S.bufs(2, "Mb")
        xpad = sb("xpad", [128, 2048], BF16)
        Bxpad = S.buf("xpad")
        Xdd = sb("Xdd", [128, 1024], BF16)
        BXdd = S.buf("Xdd")
        Btm = sb("Btm", [128, 256], BF16)
        BBtm = S.buf("Btm")
        CBs = sb("CBs", [128, 256], BF16)
        BCBs = S.buf("CBs")
        wtm = sb("wtm", [128, 16], F32)
        Bwtm = S.buf("wtm")
        cd = sb("cd", [128, 8, 4], F32)
        Bcd = S.buf("cd")
        hst = sb("hst", [128, 8, 128], F32)
        Bhst = S.buf("hst")
        hT = sb("hT", [128, 1024], BF16)
        BhT = S.buf("hT")
        hl = sb("hl", [128, 8], F32)
        Bhl = S.bufs(8, "hl")
        rstd = sb("rstd", [128, 512], F32)
        Brstd = S.buf("rstd")
        identf = sb("identf", [128, 128], F32)
        identb = sb("identb", [128, 128], BF16)
        ones_bf = sb("ones_bf", [128, 128], BF16)
        negmask = sb("negmask", [128, 128], F32)
        rmask = sb("rmask", [48, 512], F32)
        SelHP = sb("SelHP", [16, 8, 128], F32)
        NVEC = sum(r for _, r in VEC_SPECS)
        CV = sb("CV", [128, 384], F32)
        CAR = sb("CAR", [128, 176], F32)
        rows = sb("rows", [128, 5, 128], F32)
        Brows = S.buf("rows")
        A48 = sb("A48", [48, 1], F32)
        dtb48 = sb("dtb48", [48, 1], F32)
        dsk = sb("dsk", [128, 8], F32)
        clc = sb("clc", [128, 8], F32)
        cl2 = sb("cl2", [128, 8], F32)
        clh = sb("clh", [128, 8], F32)
        hb = sb("hb", [128, 16], F32)
        Bconst = S.buf("const")
        BCV = S.buf("CV")

        psF = psum("psF", [128, 6, 512], F32)
        BpsF = S.bufs(6, "psF")
        psX = psum("psX", [128, 1024], BF16)
        BpsX = S.buf("psX")
        psM = psum("psM", [128, 512], F32)
        BpsM = S.buf("psM")
        psB = psM[:, 384:512].bitcast(BF16)
        print("psB", psB.shape, psB.ap, psB.offset)
        for b_ in BpsF + [BpsX, BpsM]:
            b_.x = True
        BpsB = BpsM

        voff = {}
        o = 0
        for name, r in VEC_SPECS:
            voff[name] = o
            o += r

        def cvc(name, idx):
            c = voff[name] + idx
            return CV[:, c:c + 1]

        coff = {}
        o = 0
        for name, r in CAR_SPECS:
            coff[name] = o
            o += r
        Bcar = {"ssd_conv": S.bufs(12, "c_xbc"), "sc": S.bufs(8, "c_sc"), "lc": S.bufs(8, "c_lc"), "l": Bhl,
                "f": [S.bufs(22, "c_f0_"), S.bufs(22, "c_f1_")]}

        def car_cols(name, nk, nj, j, base=0):
            c0 = coff[name] + base + j
            return CAR[:, c0:c0 + (nk - 1) * nj + 1:nj]

        bank_ctr = [0]

        def next_bank():
            b = bank_ctr[0] % 6
            bank_ctr[0] += 1
            return b

        ft_ctr = [0]

        def next_ft():
            i = ft_ctr[0] % NFT
            ft_ctr[0] += 1
            return i

        evac_ctr = [0]

        def evac_eng():
            evac_ctr[0] += 1
            return "act" if evac_ctr[0] % 2 else "dve"

        def copy_op(en, out, in_):
            if en == "act":
                return lambda e: e.activation(out=out, in_=in_, func=AF.Copy)
            return lambda e: e.tensor_copy(out=out, in_=in_)

        S.op("pool", lambda e: e.memset(identf[:], 1.0), writes=[Bconst])
        S.op("pool", lambda e: e.affine_select(out=identf[:], in_=identf[:], pattern=[[-1, 128]], compare_op=ALU.is_equal,
                                               fill=0.0, base=0, channel_multiplier=1), reads=[Bconst], writes=[Bconst])
        S.op("pool", lambda e: e.tensor_copy(out=identb[:], in_=identf[:]), reads=[Bconst], writes=[Bconst])
        S.op("pool", lambda e: e.memset(ones_bf[:], 1.0), writes=[Bconst])
        S.op("pool", lambda e: e.memset(negmask[:], 0.0), writes=[Bconst])
        S.op("pool", lambda e: e.affine_select(out=negmask[:], in_=negmask[:], pattern=[[1, 128]], compare_op=ALU.is_ge,
                                               fill=-1.0e5, base=0, channel_multiplier=-1), reads=[Bconst], writes=[Bconst])
        S.op("pool", lambda e: e.memset(rmask[:], 1.0), writes=[Bconst])
        S.op("pool", lambda e: e.memset(rmask[:, 0:512:128], 0.0), reads=[Bconst], writes=[Bconst])
        S.op("pool", lambda e: e.memset(SelHP[:], 1.0), writes=[Bconst])
        S.op("pool", lambda e: e.affine_select(out=SelHP[:], in_=SelHP[:], pattern=[[-2, 8], [-1, 2], [0, 64]],
                                               compare_op=ALU.is_equal, fill=0.0, base=0, channel_multiplier=1),
             reads=[Bconst], writes=[Bconst])
        S.op("pool", lambda e: e.memset(LT[:], 1.0), writes=[BLT])
        rh_cl = [0]

        def init_RH(CL):
            if rh_cl[0] == CL:
                return
            rh_cl[0] = CL
            for hh in range(2):
                S.op("pool", lambda e, hh=hh: e.memset(RH[hh][:], 0.0), writes=[BRH[hh]])
                S.op("pool", lambda e, hh=hh: e.memset(RH[hh][32:48, 0:8 * CL], 1.0), reads=[BRH[hh]], writes=[BRH[hh]])
                S.op("pool", lambda e, hh=hh: e.affine_select(
                    out=RH[hh][32:48, 0:8 * CL].rearrange("p (h l) -> p h l", h=8),
                    in_=RH[hh][32:48, 0:8 * CL].rearrange("p (h l) -> p h l", h=8),
                    pattern=[[-1, 8], [0, CL]], compare_op=ALU.is_equal, fill=0.0, base=-8 * hh, channel_multiplier=1),
                    reads=[BRH[hh]], writes=[BRH[hh]])
        S.op("pool", lambda e: e.memset(xpad[:], 0.0), writes=[Bxpad])
        S.op("pool", lambda e: e.memset(CAR[:], 0.0), writes=[Bconst])

        Bvst = S.buf("vst")
        vst = [x_res[:, 0, 0:128], x_res[:, 1, 0:128], x_res[:, 2, 0:128]]

        def flat_rows(ap):
            n = len(ap.shape)
            if n == 1:
                return ap.rearrange("(r c) -> r c", c=128)
            if n == 2:
                return ap.rearrange("a (r c) -> (a r) c", c=128)
            return ap.rearrange("a b (r c) -> (a b r) c", c=128)

        S.op("pool", lambda e: e.memset(x_res[:, 0:3, 0:128], 0.0), writes=[Bvst])
        ndm = 0
        emits = []
        for name, r in VEC_SPECS:
            src = flat_rows(Wd[name])
            o0 = voff[name]
            done = 0
            while done < r:
                g = (o0 + done) // 128
                p0 = (o0 + done) % 128
                n = min(r - done, 128 - p0)
                emits.append((x_res[p0:p0 + n, g, 0:128], src[done:done + n, :]))
                done += n
        S.dma("sp", "vec", lambda e: [e.dma_start(out=a, in_=b) for a, b in emits], reads=[Bvst], writes=[Bvst], n=len(emits))
        for g in range(3):
            S.ops("pe", [lambda e, g=g: e.transpose(psF[:, g, 0:128], in_=vst[g], identity=identf[:])],
                  reads=[Bvst, Bconst], writes=[BpsF[g]])
            S.op("dve", lambda e, g=g: e.tensor_copy(out=CV[:, g * 128:(g + 1) * 128], in_=psF[:, g, 0:128]),
                 reads=[BpsF[g]], writes=[BCV])
        S.op("pool", lambda e: e.memset(A48[:], 0.0), writes=[Bconst])
        S.op("pool", lambda e: e.memset(dtb48[:], 0.0), reads=[Bconst], writes=[Bconst])
        al = Wd["ssd_a_log"].rearrange("(h o) -> h o", o=1)
        db = Wd["ssd_dt_bias"].rearrange("(h o) -> h o", o=1)
        S.dma("sp", "c1", lambda e: [e.dma_start(out=A48[0:16, :], in_=al), e.dma_start(out=A48[32:48, :], in_=al),
                                     e.dma_start(out=dtb48[0:16, :], in_=db), e.dma_start(out=dtb48[32:48, :], in_=db)],
              reads=[Bconst], writes=[Bconst], n=4)
        S.op("act", lambda e: e.activation(out=A48[:], in_=A48[:], func=AF.Exp), reads=[Bconst], writes=[Bconst])
        S.op("dve", lambda e: e.tensor_scalar(out=A48[:], in0=A48[:], scalar1=-1.0, scalar2=None, op0=ALU.mult),
             reads=[Bconst], writes=[Bconst])
        dsrc = Wd["ssd_d"]
        S.dma("sp", "c2", lambda e: [
            e.dma_start(out=dsk[0:64, :], in_=bass.AP(dsrc.tensor, 0, [[0, 64], [2, 8]]), allow_slow_non_contiguous=True),
            e.dma_start(out=dsk[64:128, :], in_=bass.AP(dsrc.tensor, 1, [[0, 64], [2, 8]]), allow_slow_non_contiguous=True)],
            reads=[Bconst], writes=[Bconst], n=2)
        lamc = CV[:, voff["lru_lambda"]:voff["lru_lambda"] + 8]
        S.op("act", lambda e: e.activation(out=clc[:], in_=lamc, func=AF.Exp, scale=-1.0), reads=[BCV], writes=[Bconst])
        S.op("act", lambda e: e.activation(out=clc[:], in_=clc[:], func=AF.Ln, bias=1.0), reads=[Bconst], writes=[Bconst])
        S.op("dve", lambda e: e.tensor_scalar(out=cl2[:], in0=clc[:], scalar1=-16.0, scalar2=None, op0=ALU.mult),
             reads=[Bconst], writes=[Bconst])
        S.op("dve", lambda e: e.tensor_scalar(out=clh[:], in0=clc[:], scalar1=-4.0, scalar2=None, op0=ALU.mult),
             reads=[Bconst], writes=[Bconst])
        S.op("dve", lambda e: e.tensor_scalar(out=clc[:], in0=clc[:], scalar1=-8.0, scalar2=None, op0=ALU.mult),
             reads=[Bconst], writes=[Bconst])
        S.op("dve", lambda e: e.tensor_scalar(out=hb[:, 0:8], in0=CV[:, voff["lru_ba"]:voff["lru_ba"] + 8], scalar1=0.5, scalar2=None,
                                              op0=ALU.mult), reads=[BCV], writes=[Bconst])
        S.op("dve", lambda e: e.tensor_scalar(out=hb[:, 8:16], in0=CV[:, voff["lru_bx"]:voff["lru_bx"] + 8], scalar1=0.5, scalar2=None,
                                              op0=ALU.mult), reads=[BCV], writes=[Bconst])

        BWS = S.bufs(NU, "WS")
        NSTG = 4
        stg32 = [x_res[:, 0:4, :].rearrange("p a b -> p (a b)"), x_res[:, 4:8, :].rearrange("p a b -> p (a b)"),
                 YG[:, 0:4, :].rearrange("p a b -> p (a b)"), YG[:, 4:8, :].rearrange("p a b -> p (a b)")]
        stg16 = [BF8[0][:, 0:4, :].rearrange("p a b -> p (a b)"), BF8[0][:, 4:8, :].rearrange("p a b -> p (a b)"),
                 BF8[1][:, 0:4, :].rearrange("p a b -> p (a b)"), BF8[1][:, 4:8, :].rearrange("p a b -> p (a b)")]
        Bs32 = [Bvst, S.buf("s32b"), S.buf("s32c"), S.buf("s32d")]
        Bs16 = S.bufs(4, "s16")
        ceng = ["dve", "act", "pool"]
        jobs = []
        for u, un in enumerate(units if STAGE >= 1 else []):
            nk, ncols = un["nk"], un["ncols"]
            hk = nk // 2
            for half in range(2):
                jobs.append((u, un, hk, ncols, half))
        DEPTH = 2

        def cv_in(i):
            u, un, hk, ncols, half = jobs[i]
            s = i % NSTG
            n_el = hk * ncols
            s32 = stg32[s][:, 0:n_el].rearrange("p (k n) -> p k n", k=hk)
            if un["tag"] == "ax":
                _, a2, x2 = un["pieces"][0]
                srcm = a2 if half == 0 else x2
                pcs = [(s32, srcm.rearrange("(k p) n -> p k n", p=128))]
            else:
                pcs = [(s32[:, :, c0:c0 + cw], fn(half * hk, hk)) for fn, c0, cw in un["pieces"]]
            if un["tag"] == "dt":
                S.op("pool", lambda e: e.memset(s32, 0.0), reads=[Bs32[s]], writes=[Bs32[s]])
            S.dma("sp", "cvi%d" % s, lambda e: [e.dma_start(out=a_, in_=b_, allow_slow_non_contiguous=True) for a_, b_ in pcs],
                  reads=[Bs32[s]], writes=[Bs32[s]])

        def cv_out(i):
            u, un, hk, ncols, half = jobs[i]
            s = i % NSTG
            n_el = hk * ncols
            en = ceng[i % 3]
            S.op(en, copy_op(en, stg16[s][:, 0:n_el], stg32[s][:, 0:n_el]), reads=[Bs32[s]], writes=[Bs16[s]])
            S.dma("sp", "cvo%d" % s, lambda e: e.dma_start(out=WS[u, :, half * n_el:(half + 1) * n_el], in_=stg16[s][:, 0:n_el]),
                  reads=[Bs16[s]], writes=[BWS[u]])

        for i in range((len(jobs) + DEPTH) if not LAZY else 0):
            if i < len(jobs):
                cv_in(i)
            if i >= DEPTH:
                cv_out(i - DEPTH)
        def inherit(dsts, srcs):
            for b in dsts:
                for sb_ in srcs:
                    if sb_.w is not None:
                        b.rl.append(sb_.w)
                    b.rl.extend(sb_.rl)
        if not LAZY:
            inherit(Bx, Bs32[0:2])
            inherit(BYG, Bs32[2:4])
            inherit(BBF8[0], Bs16[0:2])
            inherit(BBF8[1], Bs16[2:4])

        wstate = dict(issued=0, cur=0)

        lazy_ctr = [0]
        lz_eng = ["act", "dve"]
        lz_stg = [(yout[0], Byout[0]), (xin[0], Bxin[0]), (xin[1], Bxin[1]), (yout[1], Byout[1]), (xin[2], Bxin[2]), (xin[3], Bxin[3])]

        def w_issue():
            i = wstate["issued"]
            if i >= len(full_seq):
                return
            u = full_seq[i]
            s = i % NRING
            un = units[u]
            nk, ncols = un["nk"], un["ncols"]
            n_el = nk * ncols
            if LAZY and i < len(tile_seq):
                kper = max(1, min(nk, 1024 // ncols))
                k0 = 0
                while k0 < nk:
                    kn = min(kper, nk - k0)
                    q = lazy_ctr[0] % len(lz_stg)
                    ne = kn * ncols
                    stg_t, stg_b = lz_stg[q]
                    s32 = stg_t[:, 0:ne].rearrange("p (k n) -> p k n", k=kn)
                    if un["tag"] == "ax":
                        _, a2, x2, hb = un["pieces"][0]
                        srcm = a2 if k0 < 4 else x2
                        kk = (k0 % 4) + hb * 2
                        pcs = [(s32, srcm[kk * 128:(kk + kn) * 128, :].rearrange("(k p) n -> p k n", p=128))]
                    else:
                        pcs = [(s32[:, :, c0:c0 + cw], fn(k0, kn)) for fn, c0, cw in un["pieces"]]
                    if un["tag"] == "dt":
                        S.op("pool", lambda e, s32=s32: e.memset(s32, 0.0), reads=[stg_b], writes=[stg_b])
                    S.dma("sp", "lz%d" % q, lambda e, pcs=pcs: [e.dma_start(out=a_, in_=b_, allow_slow_non_contiguous=True) for a_, b_ in pcs],
                          reads=[stg_b], writes=[stg_b])
                    en = lz_eng[lazy_ctr[0] % 2]
                    S.op(en, copy_op(en, ring[s][:, k0 * ncols:k0 * ncols + ne], stg_t[:, 0:ne]), reads=[stg_b], writes=[Bring[s]])
                    lazy_ctr[0] += 1
                    k0 += kn
                S.dma(LZQ, "lzo%d" % s, lambda e, u=u, s=s, n_el=n_el: e.dma_start(out=WS[u, :, 0:n_el], in_=ring[s][:, 0:n_el]),
                      reads=[Bring[s]], writes=[BWS[u]])
            else:
                S.dma("sp", "w%d" % s, lambda e, u=u, s=s, n_el=n_el: e.dma_start(out=ring[s][:, 0:n_el], in_=WS[u, :, 0:n_el]),
                      reads=[BWS[u]], writes=[Bring[s]])
            wstate["issued"] += 1

        def w_acquire(u_expected):
            i = wstate["cur"]
            assert full_seq[i] == u_expected, (i, full_seq[i], u_expected)
            while wstate["issued"] < min(i + NRING, len(full_seq)):
                w_issue()
            wstate["cur"] += 1
            s = i % NRING
            return ring[s], Bring[s]

        def w_release():
            while wstate["issued"] < min(wstate["cur"] + NRING - 1, len(full_seq)):
                w_issue()

        def rms_norm(T, src_aps, src_bufs, gname, gbase, dst_aps, dst_bufs):
            b = next_bank()
            for j in range(8):
                q = j % 4
                if False:
                    S.op("pool", lambda e, j=j, q=q: e.tensor_tensor(out=sqb[q][:, 0:T], in0=src_aps[j], in1=src_aps[j], op=ALU.mult),
                         reads=[src_bufs[j]], writes=[Bsqb[q]])
                else:
                    S.op("act", lambda e, j=j, q=q: e.activation(out=sqb[q][:, 0:T], in_=src_aps[j], func=AF.Square),
                         reads=[src_bufs[j]], writes=[Bsqb[q]])
                S.ops("pe", [lambda e, j=j, q=q, b=b: e.matmul(psF[:, b, 0:T], lhsT=ones_bf[:], rhs=sqb[q][:, 0:T],
                                                             start=(j == 0), stop=(j == 7))],
                      reads=[Bsqb[q], Bconst], writes=[BpsF[b]])
            S.op("act", lambda e, b=b: e.activation(out=rstd[:, 0:T], in_=psF[:, b, 0:T], func=AF.Sqrt, bias=EPS, scale=1.0 / D),
                 reads=[BpsF[b]], writes=[Brstd])
            S.op("dve", lambda e: e.reciprocal(out=rstd[:, 0:T], in_=rstd[:, 0:T]), reads=[Brstd], writes=[Brstd])
            for j in range(8):
                if False:
                    it = next_ft()
                    S.op("pool", lambda e, j=j, it=it: e.tensor_scalar(out=F32T[it][:, 0:T], in0=src_aps[j], scalar1=cvc(gname, gbase + j),
                                                                      scalar2=None, op0=ALU.mult), reads=[src_bufs[j], BCV], writes=[BFT[it]])
                    S.op("pool", lambda e, j=j, it=it: e.tensor_tensor(out=dst_aps[j], in0=F32T[it][:, 0:T], in1=rstd[:, 0:T], op=ALU.mult),
                         reads=[BFT[it], Brstd], writes=[dst_bufs[j]])
                else:
                    S.op("dve", lambda e, j=j: e.scalar_tensor_tensor(out=dst_aps[j], in0=src_aps[j], scalar=cvc(gname, gbase + j),
                                                                     in1=rstd[:, 0:T], op0=ALU.mult, op1=ALU.mult),
                         reads=[src_bufs[j], Brstd, BCV], writes=[dst_bufs[j]])

        def proj_chunks(u, T, rhs_fn, rhs_bufs, nk, ncols, M, consume, fine=False):
            slot, Bslot = w_acquire(u)
            noc = max(1, ncols // 128)
            rb = list(rhs_bufs)
            for oc in range(noc):
                b = next_bank()
                ems = []
                for kc in range(nk):
                    c0 = kc * ncols + oc * 128
                    ems.append(lambda e, kc=kc, c0=c0, b=b: e.matmul(psF[0:M, b, 0:T], lhsT=slot[:, c0:c0 + M], rhs=rhs_fn(kc),
                                                                      start=(kc == 0), stop=(kc == nk - 1)))
                if fine and oc == 0 and len(rb) == nk:
                    for kc in range(nk):
                        S.ops("pe", [ems[kc]], reads=[Bslot, rb[kc]], writes=[BpsF[b]])
                else:
                    S.ops("pe", ems, reads=[Bslot] + rb, writes=[BpsF[b]])
                consume(oc, psF[0:M, b, 0:T], BpsF[b])
            w_release()

        def conv_chunk(T, ps_ap, Bps, K, cname, nj, j, carbuf, wname, wbase, bname, bidx, in_mul=None, car_base=0, wstride=None):
            H = K - 1
            it = next_ft()
            tmp = F32T[it]
            cc = car_cols(cname, H, nj, j, car_base)
            S.op("pool", lambda e: e.tensor_copy(out=tmp[:, 0:H], in_=cc), reads=[carbuf], writes=[BFT[it]])
            if in_mul is None:
                S.op("act", lambda e: e.activation(out=tmp[:, H:H + T], in_=ps_ap, func=AF.Copy), reads=[Bps], writes=[BFT[it]])
            else:
                g_ap, g_buf = in_mul
                S.op("dve", lambda e: e.tensor_tensor(out=tmp[:, H:H + T], in0=g_ap, in1=ps_ap, op=ALU.mult),
                     reads=[Bps, g_buf], writes=[BFT[it]])
            S.op("pool", lambda e: e.tensor_copy(out=cc, in_=tmp[:, T:T + H]), reads=[BFT[it]], writes=[carbuf])
            ia = next_ft()
            acc = F32T[ia]
            ws = wstride if wstride is not None else nj
            S.op("dve", lambda e: e.tensor_scalar(out=acc[:, 0:T], in0=tmp[:, 0:T], scalar1=cvc(wname, wbase + j),
                                                  scalar2=cvc(bname, bidx), op0=ALU.mult, op1=ALU.add),
                 reads=[BFT[it], BCV], writes=[BFT[ia]])
            for k in range(1, K):
                S.op("dve", lambda e, k=k: e.scalar_tensor_tensor(out=acc[:, 0:T], in0=tmp[:, k:k + T], scalar=cvc(wname, wbase + k * ws + j),
                                                                 in1=acc[:, 0:T], op0=ALU.mult, op1=ALU.add),
                     reads=[BFT[it], BFT[ia], BCV], writes=[BFT[ia]])
            return acc, ia

        def ffn(l, T):
            hn, Bhn = BF8[0], BBF8[0]
            rms_norm(T, [x_res[:, j, 0:T] for j in range(8)], Bx, "norm_ffn", 8 * l, [hn[:, j, 0:T] for j in range(8)], Bhn)

            def a_ap(j):
                return BF8[1 + j // 8][:, j % 8, 0:T], BBF8[1 + j // 8][j % 8]
            for i in range(11):
                stt = {}

                def consume(oc, ps_ap, Bps, i=i, stt=stt):
                    j = 2 * i + oc // 2
                    if oc % 2 == 0:
                        acc, ia = conv_chunk(T, ps_ap, Bps, 3, "f", 22, j, Bcar["f"][l][j], "ffn_conv_w", l * 66, "ffn_conv_b", l * 22 + j,
                                             car_base=l * 44)
                        S.op("act", lambda e: e.activation(out=acc[:, 0:T], in_=acc[:, 0:T], func=AF.Gelu_apprx_tanh),
                             reads=[BFT[ia]], writes=[BFT[ia]])
                        stt["g"] = (acc, ia)
                    else:
                        acc, ia = stt["g"]
                        da, db_ = a_ap(j)
                        S.op("dve", lambda e: e.tensor_tensor(out=da, in0=acc[:, 0:T], in1=ps_ap, op=ALU.mult),
                             reads=[BFT[ia], Bps], writes=[db_])
                proj_chunks(U["fin"][l][i], T, lambda kc: hn[:, kc, 0:T], Bhn, 8, 512, 128, consume, fine=(i == 0))
            allA = [a_ap(j)[1] for j in range(22)]
            for m in range(8):
                def consume(oc, ps_ap, Bps, m=m):
                    S.op("dve", lambda e: e.tensor_tensor(out=x_res[:, m, 0:T], in0=x_res[:, m, 0:T], in1=ps_ap, op=ALU.add),
                         reads=[Bx[m], Bps], writes=[Bx[m]])
                proj_chunks(U["fout"][l][m], T, lambda kc: a_ap(kc)[0], allA, 22, 128, 128, consume, fine=(m == 0))

        def mixer_ab(T):
            CL = min(128, T)
            NCH = T // CL
            init_RH(CL)
            hn, Bhn = BF8[0], BBF8[0]
            zs, Bzs = BF8[1], BBF8[1]
            xs_, Bxs = BF8[2], BBF8[2]
            yc, Byc = BF8[3], BBF8[3]
            EBt, BEB = BF8[4], BBF8[4]
            ysc, Bysc = BF8[4], BBF8[4]
            rms_norm(T, [x_res[:, j, 0:T] for j in range(8)], Bx, "norm_mix", 0, [hn[:, j, 0:T] for j in range(8)], Bhn)
            rhs_fn = lambda kc: hn[:, kc, 0:T]

            def consume_dt(oc, ps_ap, Bps):
                S.op("act", lambda e: e.activation(out=t_e[:, 0:T], in_=ps_ap, func=AF.Exp, bias=dtb48[:, 0:1]),
                     reads=[Bps, Bconst], writes=[Bte])
                S.op("act", lambda e: e.activation(out=t_dt[:, 0:T], in_=t_e[:, 0:T], func=AF.Ln, bias=1.0), reads=[Bte], writes=[Btdt])
                S.op("act", lambda e: e.activation(out=t_ln[:, 0:T], in_=t_dt[:, 0:T], func=AF.Ln), reads=[Btdt], writes=[Btln])
                S.op("dve", lambda e: e.tensor_scalar(out=t_dA[:, 0:T], in0=t_dt[:, 0:T], scalar1=A48[:, 0:1], scalar2=None, op0=ALU.mult),
                     reads=[Btdt, Bconst], writes=[BtdA])
                S.op("dve", lambda e: e.tensor_tensor_scan(out=t_At[:, 0:T], data0=rmask[:, 0:T], data1=t_dA[:, 0:T], initial=0.0,
                                                          op0=ALU.mult, op1=ALU.add), reads=[BtdA, Bconst], writes=[BtAt])
                S.op("dve", lambda e: e.tensor_tensor(out=LT[32:48, 0:T], in0=t_ln[32:48, 0:T], in1=t_At[32:48, 0:T], op=ALU.subtract),
                     reads=[Btln, BtAt], writes=[BLT])
                At3 = t_At[0:16, 0:T].rearrange("p (c l) -> p c l", c=NCH)
                S.op("dve", lambda e: e.tensor_tensor(out=t_w[0:16, 0:T].rearrange("p (c l) -> p c l", c=NCH),
                                                      in0=At3[:, :, CL - 1:CL].broadcast_to([16, NCH, CL]), in1=At3, op=ALU.subtract),
                     reads=[BtAt], writes=[Btw])
                S.op("act", lambda e: e.activation(out=t_w[0:16, 0:T], in_=t_w[0:16, 0:T], func=AF.Exp), reads=[Btw], writes=[Btw])
                S.op("dve", lambda e: e.tensor_tensor(out=t_w[0:16, 0:T], in0=t_w[0:16, 0:T], in1=t_dt[0:16, 0:T], op=ALU.mult),
                     reads=[Btw, Btdt], writes=[Btw])
            proj_chunks(U["dt"], T, rhs_fn, Bhn, 8, 48, 48, consume_dt, fine=True)

            if SUB < 2:
                return
            for i in range(3):
                def consume(oc, ps_ap, Bps, i=i):
                    j = 4 * i + oc
                    acc, ia = conv_chunk(T, ps_ap, Bps, 4, "ssd_conv", 12, j, Bcar["ssd_conv"][j], "ssd_conv_w", 0, "ssd_conv_b", j)
                    if j < 8:
                        dst, dbf = xs_[:, j, 0:T], Bxs[j]
                    else:
                        dst, dbf = BC[:, j - 8, 0:T], BBC[j - 8]
                    S.op("act", lambda e: e.activation(out=dst, in_=acc[:, 0:T], func=AF.Silu), reads=[BFT[ia]], writes=[dbf])
                proj_chunks(U["xbc"][i], T, rhs_fn, Bhn, 8, 512, 128, consume)
            for i in range(2):
                def consume(oc, ps_ap, Bps, i=i):
                    j = 4 * i + oc
                    S.op("act", lambda e: e.activation(out=zs[:, j, 0:T], in_=ps_ap, func=AF.Silu), reads=[Bps], writes=[Bzs[j]])
                proj_chunks(U["z"][i], T, rhs_fn, Bhn, 8, 512, 128, consume)

            if SUB < 3:
                return
            for j in range(8):
                b = next_bank()
                S.ops("pe", [lambda e, j=j, b=b: e.matmul(psF[:, b, 0:T], lhsT=SelHP[0:16, j, :], rhs=t_At[0:16, 0:T], start=True, stop=True)],
                      reads=[BtAt, Bconst], writes=[BpsF[b]])
                S.op("act", lambda e, j=j, b=b: e.activation(out=EBt[:, j, 0:T], in_=psF[:, b, 0:T], func=AF.Exp),
                     reads=[BpsF[b]], writes=[BEB[j]])
                S.op("act", lambda e, j=j, b=b: e.activation(out=cd[:, j, 0:NCH], in_=psF[:, b, CL - 1:T:CL], func=AF.Exp),
                     reads=[BpsF[b]], writes=[Bcd])
            for c in range(NCH):
                tk = slice(c * CL, (c + 1) * CL)
                S.ops("pe", [lambda e, j=j: e.transpose(psX[0:CL, j * 128:(j + 1) * 128], in_=xs_[:, j, tk], identity=identb[:])
                             for j in range(8)], reads=Bxs + [Bconst], writes=[BpsX])
                S.ops("pe", [lambda e, g=g: e.transpose(psB[0:CL, g * 128:(g + 1) * 128], in_=BC[:, g, tk], identity=identb[:])
                             for g in range(2)], reads=[BBC[0], BBC[1], Bconst], writes=[BpsB])
                S.ops("pe", [lambda e: e.transpose(psM[0:CL, 256:272], in_=t_w[0:16, tk], identity=identf[0:16, 0:16])] +
                      [lambda e, g=g: e.matmul(psM[0:CL, g * 128:g * 128 + CL], lhsT=BC[:, g, tk], rhs=BC[:, 2 + g, tk], start=True, stop=True)
                       for g in range(2)], reads=[Btw, Bconst] + BBC, writes=[BpsM])
                S.op("act", lambda e: e.activation(out=wtm[0:CL, :], in_=psM[0:CL, 256:272], func=AF.Copy), reads=[BpsM], writes=[Bwtm])
                S.op("dve", lambda e: e.tensor_copy(
                    out=bass.AP(xpad[:].tensor, xpad[:].offset, [[xpad[:].ap[0][0], CL], [256, 8], [192, 2], [1, 64]]),
                    in_=psX[0:CL, :].rearrange("p (j e d) -> p j e d", j=8, e=2)), reads=[BpsX], writes=[Bxpad])
                S.op("dve", lambda e: e.tensor_tensor(out=Xdd[0:CL, :].rearrange("p (h d) -> p h d", h=16),
                                                      in0=psX[0:CL, :].rearrange("p (h d) -> p h d", h=16),
                                                      in1=wtm[0:CL, :].unsqueeze(2).broadcast_to([CL, 16, 64]), op=ALU.mult),
                     reads=[BpsX, Bwtm], writes=[BXdd])
                S.op("act", lambda e: e.activation(out=Btm[0:CL, :], in_=psB[0:CL, :], func=AF.Copy), reads=[BpsB], writes=[BBtm])
                S.op("act", lambda e: e.activation(out=CBs[0:CL, :], in_=psM[0:CL, 0:256], func=AF.Copy), reads=[BpsM], writes=[BCBs])
                nq = (8 * CL) // 512
                for hh in range(2):
                    S.op("pool", lambda e, hh=hh: e.affine_select(
                        out=RH[hh][0:16, 0:8 * CL].rearrange("p (h l) -> p h l", h=8),
                        in_=t_At[0:16, tk].unsqueeze(1).broadcast_to([16, 8, CL]),
                        pattern=[[-1, 8], [0, CL]], compare_op=ALU.is_equal, fill=0.0, base=-8 * hh, channel_multiplier=1),
                        reads=[BtAt], writes=[BRH[hh]])
                    for q in range(nq):
                        S.ops("pe", [lambda e, hh=hh, q=q: e.matmul(psF[0:CL, 2 * hh + q, :], lhsT=LT[0:48, tk],
                                                                     rhs=RH[hh][0:48, q * 512:(q + 1) * 512], start=True, stop=True)],
                              reads=[BLT, BRH[hh]], writes=[BpsF[2 * hh + q]])
                    seg_ap = psF[0:CL, 2 * hh:2 * hh + nq, :].rearrange("p q (h l) -> p (q h) l", l=CL)
                    S.op("dve", lambda e, seg_ap=seg_ap: e.tensor_tensor(
                        out=segm[0:CL, 0:8 * CL].rearrange("p (h l) -> p h l", h=8), in0=seg_ap,
                        in1=negmask[0:CL, 0:CL].unsqueeze(1).broadcast_to([CL, 8, CL]), op=ALU.add),
                        reads=[BpsF[2 * hh + q] for q in range(nq)] + [Bconst], writes=[Bsegm])
                    S.op("act", lambda e: e.activation(out=Lb[0:CL, 0:8 * CL], in_=segm[0:CL, 0:8 * CL], func=AF.Exp),
                         reads=[Bsegm], writes=[BLb])
                    S.op("dve", lambda e, hh=hh: e.tensor_tensor(
                        out=Mb[hh][0:CL, 0:8 * CL].rearrange("p (h l) -> p h l", h=8),
                        in0=Lb[0:CL, 0:8 * CL].rearrange("p (h l) -> p h l", h=8),
                        in1=CBs[0:CL, hh * 128:hh * 128 + CL].unsqueeze(1).broadcast_to([CL, 8, CL]), op=ALU.mult),
                        reads=[BLb, BCBs], writes=[BMb[hh]])
                for jg in range(2):
                    bY, bO = 2 * jg, 2 * jg + 1
                    ems = []
                    for jj in range(4):
                        j = 4 * jg + jj
                        for e2 in range(2):
                            ems.append(lambda e, jj=jj, j=j, e2=e2, jg=jg, bY=bY: e.matmul(
                                psF[:, bY, jj * 128:jj * 128 + CL], lhsT=xpad[0:CL, j * 256 + e2 * 128:j * 256 + e2 * 128 + 128],
                                rhs=Mb[jg][0:CL, (2 * jj + e2) * CL:(2 * jj + e2 + 1) * CL], start=(e2 == 0), stop=(e2 == 1)))
                    S.ops("pe", ems, reads=[Bxpad, BMb[jg]], writes=[BpsF[bY]])
                    S.ops("pe", [lambda e, jj=jj, jg=jg, bO=bO: e.matmul(
                        psF[:, bO, jj * 128:jj * 128 + CL], lhsT=hT[:, (4 * jg + jj) * 128:(4 * jg + jj + 1) * 128], rhs=BC[:, 2 + jg, tk],
                        start=True, stop=True) for jj in range(4)], reads=[BhT, BBC[2 + jg]], writes=[BpsF[bO]])
                    it = next_ft()
                    tmp = F32T[it]
                    t3 = tmp[:, 0:4 * CL].rearrange("p (j l) -> p j l", j=4)
                    pO = psF[:, bO, :].rearrange("p (j l) -> p j l", j=4)[:, :, 0:CL]
                    pY = psF[:, bY, :].rearrange("p (j l) -> p j l", j=4)[:, :, 0:CL]
                    S.op("dve", lambda e, t3=t3, pO=pO, jg=jg: e.tensor_tensor(out=t3, in0=pO, in1=EBt[:, 4 * jg:4 * jg + 4, tk], op=ALU.mult),
                         reads=[BpsF[bO]] + BEB[4 * jg:4 * jg + 4], writes=[BFT[it]])
                    S.op("dve", lambda e, t3=t3, pY=pY, jg=jg: e.tensor_tensor(out=YG[:, 4 * jg:4 * jg + 4, tk], in0=pY, in1=t3, op=ALU.add),
                         reads=[BpsF[bY], BFT[it]], writes=BYG[4 * jg:4 * jg + 4])
                S.ops("pe", [lambda e, j=j: e.matmul(psF[:, 4 + j // 4, (j % 4) * 128:(j % 4 + 1) * 128], lhsT=Xdd[0:CL, j * 128:(j + 1) * 128],
                                                     rhs=Btm[0:CL, (j // 4) * 128:(j // 4 + 1) * 128], start=True, stop=True)
                             for j in range(8)], reads=[BXdd, BBtm], writes=[BpsF[4], BpsF[5]])
                S.op("dve", lambda e, c=c: e.tensor_tensor(out=hst[:], in0=hst[:], in1=cd[:, :, c:c + 1].broadcast_to([128, 8, 128]), op=ALU.mult),
                     reads=[Bhst, Bcd], writes=[Bhst])
                S.op("dve", lambda e: e.tensor_tensor(out=hst[:], in0=hst[:], in1=psF[:, 4:6, :].rearrange("p b (j n) -> p (b j) n", n=128),
                                                      op=ALU.add), reads=[Bhst, BpsF[4], BpsF[5]], writes=[Bhst])
                ssd_state_T()
            if SUB < 4:
                return
            for j in range(8):
                S.op("dve", lambda e, j=j: e.scalar_tensor_tensor(out=YG[:, j, 0:T], in0=xs_[:, j, 0:T], scalar=dsk[:, j:j + 1], in1=YG[:, j, 0:T],
                                                                 op0=ALU.mult, op1=ALU.add), reads=[Bxs[j], BYG[j], Bconst], writes=[BYG[j]])
                S.op("dve", lambda e, j=j: e.tensor_tensor(out=YG[:, j, 0:T], in0=YG[:, j, 0:T], in1=zs[:, j, 0:T], op=ALU.mult),
                     reads=[BYG[j], Bzs[j]], writes=[BYG[j]])
            rms_norm(T, [YG[:, j, 0:T] for j in range(8)], BYG, "ssd_norm", 0, [yc[:, j, 0:T] for j in range(8)], Byc)

            if SUB < 5:
                return
            for j in range(8):
                stt = {}

                def consume(oc, ps_ap, Bps, j=j, stt=stt):
                    if oc == 0:
                        ig = next_ft()
                        S.op("act", lambda e: e.activation(out=F32T[ig][:, 0:T], in_=ps_ap, func=AF.Copy), reads=[Bps], writes=[BFT[ig]])
                        stt["g"] = ig
                    elif oc == 1:
                        ig = stt["g"]
                        acc, ia = conv_chunk(T, ps_ap, Bps, 3, "sc", 8, j, Bcar["sc"][j], "sc_conv_w", 0, "sc_conv_b", j,
                                             in_mul=(F32T[ig][:, 0:T], BFT[ig]))
                        stt["u"] = (acc, ia)
                    else:
                        acc, ia = stt["u"]
                        S.op("dve", lambda e: e.tensor_tensor(out=ysc[:, j, 0:T], in0=acc[:, 0:T], in1=ps_ap, op=ALU.mult),
                             reads=[BFT[ia], Bps], writes=[Bysc[j]])
                proj_chunks(U["sc"][j], T, rhs_fn, Bhn, 8, 384, 128, consume)
            if SUB < 6:
                return
            for i in range(4):
                def consume(oc, ps_ap, Bps, i=i):
                    m = 2 * i + oc
                    S.op("dve", lambda e: e.tensor_tensor(out=x_res[:, m, 0:T], in0=x_res[:, m, 0:T], in1=ps_ap, op=ALU.add),
                         reads=[Bx[m], Bps], writes=[Bx[m]])
                proj_chunks(U["abo"][i], T, lambda kc: (yc[:, kc, 0:T] if kc < 8 else ysc[:, kc - 8, 0:T]), Byc + Bysc, 16, 256, 128, consume, fine=(i == 0))

        def ssd_state_T():
            S.ops("pe", [lambda e, j=j: e.transpose(psF[:, 4 + j // 4, (j % 4) * 128:(j % 4 + 1) * 128], in_=hst[:, j, :], identity=identf[:])
                         for j in range(8)], reads=[Bhst, Bconst], writes=[BpsF[4], BpsF[5]])
            S.op("act", lambda e: e.activation(out=hT[:].rearrange("p (b n) -> p b n", b=2), in_=psF[:, 4:6, :], func=AF.Copy),
                 reads=[BpsF[4], BpsF[5]], writes=[BhT])

        def mixer_c(T):
            hn, Bhn = BF8[0], BBF8[0]
            gg, Bgg = BF8[1], BBF8[1]
            xbb, Bxbb = BF8[2], BBF8[2]
            yl, Byl = BF8[3], BBF8[3]
            rms_norm(T, [x_res[:, j, 0:T] for j in range(8)], Bx, "norm_mix", 8, [hn[:, j, 0:T] for j in range(8)], Bhn)
            rhs_fn = lambda kc: hn[:, kc, 0:T]
            def gate_unit(i):
                def consume(oc, ps_ap, Bps, i=i):
                    j = 4 * i + oc
                    S.op("act", lambda e: e.activation(out=gg[:, j, 0:T], in_=ps_ap, func=AF.Gelu_apprx_tanh), reads=[Bps], writes=[Bgg[j]])
                proj_chunks(U["lin"][i], T, rhs_fn, Bhn, 8, 512, 128, consume)
            for i in range(2):
                def consume(oc, ps_ap, Bps, i=i):
                    j = 4 * i + oc
                    acc, ia = conv_chunk(T, ps_ap, Bps, 4, "lc", 8, j, Bcar["lc"][j], "lru_conv_w", 0, "lru_conv_b", j)
                    S.op("act", lambda e: e.activation(out=YG[:, j, 0:T], in_=acc[:, 0:T], func=AF.Copy), reads=[BFT[ia]], writes=[BYG[j]])
                    S.op("pool", lambda e: e.tensor_copy(out=xbb[:, j, 0:T], in_=acc[:, 0:T]), reads=[BFT[ia]], writes=[Bxbb[j]])
                proj_chunks(U["lin"][2 + i], T, rhs_fn, Bhn, 8, 512, 128, consume, fine=(i == 0))
            def head_chain(h, slot, Bslot):
                for oc in range(2):
                    j = 2 * h + oc
                    bR, bI = next_bank(), next_bank()
                    for mat, bb in ((0, bR), (1, bI)):
                        S.ops("pe", [lambda e, kc=kc, mat=mat, bb=bb, h=h, oc=oc: e.matmul(
                            psF[:, bb, 0:T], lhsT=slot[:, ((mat * 2 + h % 2) * 2 + kc) * 256 + oc * 128:((mat * 2 + h % 2) * 2 + kc) * 256 + oc * 128 + 128],
                            rhs=xbb[:, 2 * h + kc, 0:T], start=(kc == 0), stop=(kc == 1)) for kc in range(2)],
                            reads=[Bslot, Bxbb[2 * h], Bxbb[2 * h + 1]], writes=[BpsF[bb]])
                    ir, ii, ia_, im, iu = [next_ft() for _ in range(5)]
                    r_, i_, a_, m_, u_ = [F32T[k][:, 0:T] for k in (ir, ii, ia_, im, iu)]
                    S.op("act", lambda e, r_=r_, bR=bR, j=j: e.activation(out=r_, in_=psF[:, bR, 0:T], func=AF.Tanh, bias=hb[:, j:j + 1], scale=0.5),
                         reads=[BpsF[bR], Bconst], writes=[BFT[ir]])
                    S.op("act", lambda e, i_=i_, bI=bI, j=j: e.activation(out=i_, in_=psF[:, bI, 0:T], func=AF.Tanh, bias=hb[:, 8 + j:9 + j], scale=0.5),
                         reads=[BpsF[bI], Bconst], writes=[BFT[ii]])
                    S.op("act", lambda e, a_=a_, r_=r_, j=j: e.activation(out=a_, in_=r_, func=AF.Exp, scale=clh[:, j:j + 1], bias=clh[:, j:j + 1]),
                         reads=[BFT[ir], Bconst], writes=[BFT[ia_]])
                    S.op("pool", lambda e, m_=m_, a_=a_: e.tensor_tensor(out=m_, in0=a_, in1=a_, op=ALU.mult),
                         reads=[BFT[ia_]], writes=[BFT[im]])
                    S.op("dve", lambda e, m_=m_: e.tensor_scalar(out=m_, in0=m_, scalar1=-0.25, scalar2=0.25, op0=ALU.mult, op1=ALU.add),
                         reads=[BFT[im]], writes=[BFT[im]])
                    S.op("act", lambda e, m_=m_: e.activation(out=m_, in_=m_, func=AF.Sqrt), reads=[BFT[im]], writes=[BFT[im]])
                    S.op("dve", lambda e, u_=u_, i_=i_, j=j: e.scalar_tensor_tensor(out=u_, in0=i_, scalar=1.0, in1=YG[:, j, 0:T],
                                                                                  op0=ALU.add, op1=ALU.mult),
                         reads=[BFT[ii], BYG[j]], writes=[BFT[iu]])
                    S.op("dve", lambda e, u_=u_, m_=m_: e.tensor_tensor(out=u_, in0=u_, in1=m_, op=ALU.mult),
                         reads=[BFT[iu], BFT[im]], writes=[BFT[iu]])
                    S.op("dve", lambda e, a_=a_, u_=u_, j=j: e.tensor_tensor_scan(out=YG[:, j, 0:T], data0=a_, data1=u_, initial=hl[:, j:j + 1],
                                                                               op0=ALU.mult, op1=ALU.add),
                         reads=[BFT[ia_], BFT[iu], Bhl[j]], writes=[BYG[j]])
                    S.op("pool", lambda e, j=j: e.tensor_copy(out=hl[:, j:j + 1], in_=YG[:, j, T - 1:T]), reads=[BYG[j]], writes=[Bhl[j]])

            def yl_ops(js):
                for j in js:
                    S.op("dve", lambda e, j=j: e.tensor_tensor(out=yl[:, j, 0:T], in0=YG[:, j, 0:T], in1=gg[:, j, 0:T], op=ALU.mult),
                         reads=[BYG[j], Bgg[j]], writes=[Byl[j]])

            sl, Bsl = w_acquire(U["ax"][0])
            head_chain(0, sl, Bsl)
            head_chain(1, sl, Bsl)
            gate_unit(0)
            yl_ops(range(0, 4))
            sl, Bsl = w_acquire(U["ax"][1])
            head_chain(2, sl, Bsl)
            head_chain(3, sl, Bsl)
            gate_unit(1)
            yl_ops(range(4, 8))
            w_release()
            for i in range(2):
                def consume(oc, ps_ap, Bps, i=i):
                    m = 4 * i + oc
                    S.op("dve", lambda e: e.tensor_tensor(out=x_res[:, m, 0:T], in0=x_res[:, m, 0:T], in1=ps_ap, op=ALU.add),
                         reads=[Bx[m], Bps], writes=[Bx[m]])
                proj_chunks(U["lout"][i], T, lambda kc: yl[:, kc, 0:T], Byl, 8, 512, 128, consume, fine=(i == 0))

        allcar = Bcar["ssd_conv"] + Bcar["sc"] + Bcar["lc"] + Bcar["f"][0] + Bcar["f"][1]

        def car_rows_src(name, ap):
            return flat_rows(ap)

        def init_state(kind):
            if kind == "p":
                S.op("pool", lambda e: e.memset(CAR[:], 0.0), writes=allcar + Bhl)
                S.op("pool", lambda e: e.memset(hl[:], 0.0), writes=Bhl)
                S.op("pool", lambda e: e.memset(hst[:], 0.0), writes=[Bhst])
                S.op("pool", lambda e: e.memset(hT[:], 0.0), writes=[BhT])
            else:
                ems = []
                for gi, (name, r) in enumerate(CAR_SPECS):
                    ems.append((rows[0:r, gi, :], flat_rows(st_in[name])))
                S.dma("sp", "strow", lambda e: [e.dma_start(out=a, in_=b) for a, b in ems], writes=[Brows], n=len(ems))
                for gi, (name, r) in enumerate(CAR_SPECS):
                    b = next_bank()
                    S.ops("pe", [lambda e, gi=gi, r=r, b=b: e.transpose(psF[:, b, 0:r], in_=rows[0:r, gi, :], identity=identf[0:r, 0:r])],
                          reads=[Brows, Bconst], writes=[BpsF[b]])
                    if name == "l":
                        S.op("dve", lambda e, b=b: e.tensor_copy(out=hl[:], in_=psF[:, b, 0:8]), reads=[BpsF[b]], writes=Bhl)
                    else:
                        S.op("dve", lambda e, b=b, r=r, name=name: e.tensor_copy(out=CAR[:, coff[name]:coff[name] + r], in_=psF[:, b, 0:r]),
                             reads=[BpsF[b]], writes=allcar)
                S.dma("sp", "sth", lambda e: e.dma_start(out=hst[0:64, :, :], in_=st_ssd.rearrange("(j two) p n -> two p j n", two=2)[0]),
                      writes=[Bhst])
                S.dma("sp", "sth", lambda e: e.dma_start(out=hst[64:128, :, :], in_=st_ssd.rearrange("(j two) p n -> two p j n", two=2)[1]),
                      reads=[Bhst], writes=[Bhst])
                ssd_state_T()

        def out_state(si):
            S.op("pool", lambda e: e.tensor_copy(out=CAR[:, coff["l"]:coff["l"] + 8], in_=hl[:]), reads=Bhl, writes=Bhl)
            for gi, (name, r) in enumerate(CAR_SPECS):
                b = next_bank()
                S.ops("pe", [lambda e, name=name, r=r, b=b: e.transpose(psF[0:r, b, 0:128], in_=CAR[:, coff[name]:coff[name] + r], identity=identf[:])],
                      reads=allcar + Bhl + [Bconst], writes=[BpsF[b]])
                S.op("dve", lambda e, gi=gi, r=r, b=b: e.tensor_copy(out=rows[0:r, gi, :], in_=psF[0:r, b, 0:128]), reads=[BpsF[b]], writes=[Brows])
            ems = []
            for gi, (name, r) in enumerate(CAR_SPECS):
                if name == "f":
                    for l in range(2):
                        ems.append((flat_rows(st_out["f"][l, si]), rows[44 * l:44 * l + 44, gi, :]))
                else:
                    ems.append((flat_rows(st_out[name][si]), rows[0:r, gi, :]))
            S.dma("sp", "ostrow", lambda e: [e.dma_start(out=a, in_=b) for a, b in ems], reads=[Brows], writes=[Brows], n=len(ems))
            o3 = o_ssd[si].rearrange("(j two) p n -> two p j n", two=2)
            S.dma("sp", "osth", lambda e: [e.dma_start(out=o3[0], in_=hst[0:64, :, :]), e.dma_start(out=o3[1], in_=hst[64:128, :, :])],
                  reads=[Bhst], writes=[Bhst], n=2)

        def x_load(tl):
            T_ = tl["T"]
            src_ = xp[tl["idx"]] if tl["kind"] == "p" else xs_in[0]
            ntok_ = min(128, T_)
            for tb in range(max(1, T_ // 128)):
                r0 = tl["t0"] + tb * 128
                S.dma("sp", "xin%d" % tb, lambda e, tb=tb, r0=r0: e.dma_start(out=xin[tb][0:ntok_, :], in_=src_[r0:r0 + ntok_, :]),
                      writes=[Bxin[tb]])

        def x_transposes(tlx, dx, dB):
            Tx = tlx["T"]
            ntk = min(128, Tx)
            for tb in range(max(1, Tx // 128)):
                for jh in range(2):
                    b = next_bank()
                    S.ops("pe", [lambda e, jj=jj, tb=tb, b=b, jh=jh: e.transpose(
                        psF[:, b, jj * 128:jj * 128 + ntk], in_=xin[tb][0:ntk, (4 * jh + jj) * 128:(4 * jh + jj + 1) * 128], identity=identf[0:ntk, 0:ntk])
                        for jj in range(4)], reads=[Bxin[tb], Bconst], writes=[BpsF[b]])
                    en = evac_eng()
                    S.op(en, copy_op(en, dx[:, 4 * jh:4 * jh + 4, tb * 128:tb * 128 + ntk],
                                     psF[:, b, :].rearrange("p (j t) -> p j t", j=4)[:, :, 0:ntk]),
                         reads=[BpsF[b]], writes=dB[4 * jh:4 * jh + 4])

        for tli, tl in enumerate(tiles if STAGE >= 2 else []):
            T = tl["T"]
            dst = yp[tl["idx"]] if tl["kind"] == "p" else ys[0]
            has_next = tli + 1 < len(tiles)
            early = has_next and not (LAZY and tli == 0)
            if tli == 0:
                x_load(tl)
            if tl["first"]:
                init_state(tl["kind"])
            NTB = max(1, T // 128)
            ntok = min(128, T)
            if tli == 0:
                x_transposes(tl, x_res, Bx)
            if early:
                x_load(tiles[tli + 1])
            if STAGE >= 3:
                mixer_ab(T)
            if STAGE >= 4:
                ffn(0, T)
            if STAGE >= 5:
                mixer_c(T)
            if early:
                x_transposes(tiles[tli + 1], YG, BYG)
            if STAGE >= 6:
                ffn(1, T)
            if has_next and not early:
                x_load(tiles[tli + 1])
                x_transposes(tiles[tli + 1], YG, BYG)
            rms_norm(T, [x_res[:, j, 0:T] for j in range(8)], Bx, "norm_final", 0, [x_res[:, j, 0:T] for j in range(8)], Bx)
            for tb in range(NTB):
                q = tb % 2
                for jh in range(2):
                    b = next_bank()
                    S.ops("pe", [lambda e, jj=jj, b=b, jh=jh, tb=tb: e.transpose(
                        psF[0:ntok, b, jj * 128:(jj + 1) * 128], in_=x_res[:, 4 * jh + jj, tb * 128:tb * 128 + ntok], identity=identf[:])
                        for jj in range(4)], reads=Bx[4 * jh:4 * jh + 4] + [Bconst], writes=[BpsF[b]])
                    en = evac_eng()
                    S.op(en, copy_op(en, yout[q][0:ntok, jh * 512:(jh + 1) * 512], psF[0:ntok, b, :]), reads=[BpsF[b]], writes=[Byout[q]])
                r0 = tl["t0"] + tb * 128
                S.dma("sp", "yout%d" % q, lambda e, q=q, r0=r0: e.dma_start(out=dst[r0:r0 + ntok, :], in_=yout[q][0:ntok, :]),
                      reads=[Byout[q]], writes=[Byout[q]])
            if tl["last"]:
                out_state(tl["si"])
            x_res, YG = YG, x_res
            Bx, BYG = BYG, Bx
        S.emit_all()
    return nc


_IN_ORDER = ["x_prompt", "x_sample", "state_ssd_conv", "state_ssd", "state_sconv", "state_lru_conv", "state_lru", "state_ffn_conv"]


def run(inputs, NPS, LP, LS, ncores):
    nc = build_nc(NPS, LP, LS)
    f = lambda a: np.ascontiguousarray(np.asarray(a, dtype=np.float32))
    in_maps = []
    for c in range(ncores):
        m = {"xp": f(inputs["x_prompt"][c * NPS:(c + 1) * NPS]), "xs": f(inputs["x_sample"][c:c + 1]),
             "st_ssd_conv": f(inputs["state_ssd_conv"][c]), "st_ssd": f(inputs["state_ssd"][c]), "st_sc": f(inputs["state_sconv"][c]),
             "st_lc": f(inputs["state_lru_conv"][c]), "st_l": f(inputs["state_lru"][c]), "st_f": f(inputs["state_ffn_conv"][:, c])}
        for k in W_SHAPES:
            m[k] = f(inputs[k])
        in_maps.append(m)
    res = run_bass_kernel_spmd(nc, in_maps, core_ids=list(range(ncores)))
    R = res.results
    cat = lambda k, sl: np.concatenate([r[k][sl] for r in R], axis=0)
    P, Sm = slice(0, NPS), slice(NPS, NPS + 1)
    y_p = cat("yp", slice(None))
    y_s = cat("ys", slice(None))
    outs = [y_p, y_s]
    for sl in (P, Sm):
        outs += [cat("o_ssd_conv", sl), cat("o_ssd", sl), cat("o_sc", sl), cat("o_lc", sl), cat("o_l", sl),
                 np.concatenate([r["o_f"][:, sl] for r in R], axis=1)]
    return tuple(np.ascontiguousarray(o.astype(np.float32)) for o in outs)


def kernel(**inputs):
    return run(inputs, 2, 4096, 64, NCORES)
```

```python
import numpy as np
from contextlib import ExitStack
import concourse.bass as bass
import concourse.mybir as mybir
from concourse.alu_op_type import AluOpType as ALU
from concourse.bass_utils import run_bass_kernel_spmd

F32 = mybir.dt.float32
BF16 = mybir.dt.bfloat16
AF = mybir.ActivationFunctionType
NCORES = 8
D = 1024
DFF = 2816
NJF = 22
EPS = 1e-6
USZ = 4096
import os
STAGE = int(os.environ.get('KSTAGE', '9'))
SUB = int(os.environ.get('KSUB', '9'))
LAZY = os.environ.get('KLAZY', '1') == '1'
LZQ = os.environ.get('KLZQ', 'act')
ACTTAP = os.environ.get('KACTTAP', '1') == '1'


class Buf:
    __slots__ = ("name", "w", "rl", "x")

    def __init__(self, name):
        self.name = name
        self.w = None
        self.rl = []
        self.x = False


class Rec:
    def __init__(self):
        self.calls = []

    def __getattr__(self, name):
        def m(*a, **k):
            self.calls.append((name, a, k))
            return None
        return m


def _free(ap):
    n = 1
    for d in ap.shape[1:]:
        n *= d
    return n


STALLDBG = None


class Sched:
    ENGS = ("pe", "dve", "act", "pool", "sp")
    WIN = {"pe": int(os.environ.get("KWPE", "40")), "dve": int(os.environ.get("KWV", "24")), "act": int(os.environ.get("KWV", "24")),
           "pool": int(os.environ.get("KWV", "24")), "sp": 1}
    TBL = float(os.environ.get("KTBL", "1300"))
    SCL = {e: float(os.environ.get("KS_" + e, "1")) for e in ("pe", "dve", "act", "pool", "sp")}

    def __init__(self, nc, stack):
        self.nc = nc
        self.stack = stack
        self.nodes = []
        self.nbuf = 0

    def buf(self, name=None):
        self.nbuf += 1
        return Buf(name or ("b%d" % self.nbuf))

    def bufs(self, n, name="b"):
        return [self.buf("%s%d" % (name, i)) for i in range(n)]

    def _est(self, en, calls):
        t = 0.0
        for name, a, k in calls:
            if name == "dma_start":
                o = k["out"]
                t = max(t, 2000.0 + o.shape[0] * _free(o) * (2 if o.dtype == BF16 else 4) / 160.0)
            elif en == "pe":
                if name == "matmul":
                    rhs = k["rhs"]
                    t += (_free(rhs) * 0.45 + 14) * (4.0 if rhs.dtype == F32 else 1.0)
                else:
                    t += 220.0 if k["in_"].dtype == F32 else 70.0
            elif name == "dma_start":
                o = k["out"]
                t = max(t, 2000.0 + o.shape[0] * _free(o) * (2 if o.dtype == BF16 else 4) / 160.0)
            else:
                o = k.get("out", a[0] if a else None)
                n = _free(o) if o is not None else 64
                if en == "dve":
                    t += 130 + n * (2.1 if name == "tensor_tensor_scan" else (6.0 if name == "reciprocal" else 1.04))
                elif en == "act":
                    t += 230 + n * 0.83
                else:
                    t += 300 + n * 1.35
        return t

    def _add(self, en, calls, reads, writes, dkey=None):
        nid = len(self.nodes)
        deps = {}

        def need(d, sem):
            if d is None:
                return
            deps[d] = deps.get(d, False) or sem
        for b in reads:
            need(b.w, True)
            if b.x:
                for r in b.rl:
                    if self.nodes[r]["eng"] != en or self.nodes[r]["dkey"]:
                        need(r, True)
        for b in writes:
            if b.w is not None:
                need(b.w, True)
            for r in b.rl:
                need(r, True)
        for b in reads:
            b.rl.append(nid)
        for b in writes:
            b.w = nid
            b.rl = []
        aset = None
        if en == "act":
            fn = calls[0][2].get("func")
            aset = {AF.Exp: "exp", AF.Ln: "ln", AF.Silu: "silu", AF.Gelu_apprx_tanh: "gelu", AF.Sigmoid: "sig", AF.Sqrt: "sqrt",
                    AF.Tanh: "exp"}.get(fn)
        self.nodes.append(dict(eng=en, calls=calls, deps=list(deps.items()), dkey=dkey, ndma=len(calls) if dkey else 0,
                               dur=self._est(en, calls) * self.SCL[en], aset=aset))
        return nid

    def op(self, en, emit, reads=(), writes=()):
        self.ops(en, [emit], reads, writes)

    def ops(self, en, emits, reads=(), writes=()):
        rec = Rec()
        for e in emits:
            e(rec)
        assert len(rec.calls) >= 1
        self._add(en, rec.calls, reads, writes)

    def dma(self, qn, key, emit, reads=(), writes=(), n=None):
        rec = Rec()
        emit(rec)
        assert len(rec.calls) >= 1
        self._add(qn, rec.calls, reads, writes, dkey=key)

    def _schedule(self):
        nodes = self.nodes
        q = {e: [] for e in self.ENGS}
        for i, nd in enumerate(nodes):
            q[nd["eng"]].append(i)
        head = {e: 0 for e in self.ENGS}
        emitted = [False] * len(nodes)
        fin = [0.0] * len(nodes)
        etime = {e: 0.0 for e in self.ENGS}
        order = {e: [] for e in self.ENGS}
        remaining = len(nodes)
        cur_set = [None]
        nswitch = [0]
        self.nswitch = nswitch
        NOSCHED = os.environ.get("KNOSCHED", "0") == "1"
        while remaining:
            best = None
            for e in self.ENGS:
                lst = q[e]
                h = head[e]
                while h < len(lst) and emitted[lst[h]]:
                    h += 1
                head[e] = h
                if h >= len(lst):
                    continue
                cand = None
                cnt = 0
                i = h
                W = 1 if NOSCHED else self.WIN[e]
                while i < len(lst) and cnt < W:
                    nid = lst[i]
                    i += 1
                    if emitted[nid]:
                        continue
                    cnt += 1
                    ok = True
                    rdy = 0.0
                    for d, sem in nodes[nid]["deps"]:
                        if not emitted[d]:
                            ok = False
                            break
                        lat = 60.0 if not sem else (150.0 if nodes[d]["eng"] == e and not nodes[d]["dkey"] else 300.0)
                        fd = fin[d] + lat if sem else 0.0
                        if fd > rdy:
                            rdy = fd
                    if not ok:
                        continue
                    pen = 0.0
                    if e == "act" and nodes[nid]["aset"] is not None and nodes[nid]["aset"] != cur_set[0]:
                        pen = self.TBL
                    eff = max(rdy, etime[e]) + pen
                    if cand is None or eff < cand[0] - 1e-9:
                        cand = (eff, nid, pen)
                    if eff <= etime[e]:
                        break
                if cand is None:
                    continue
                start = cand[0]
                if best is None or start < best[0]:
                    best = (start, cand[1], e)
            assert best is not None, "scheduler deadlock"
            start, nid, e = best
            if STALLDBG is not None:
                crit = None
                cf = -1.0
                for d, sem in nodes[nid]["deps"]:
                    if sem and fin[d] > cf:
                        cf = fin[d]
                        crit = d
                STALLDBG.append((e, nid, start, etime[e], crit))
            emitted[nid] = True
            nd = nodes[nid]
            if e == "act" and nd["aset"] is not None:
                if nd["aset"] != cur_set[0]:
                    nswitch[0] += 1
                cur_set[0] = nd["aset"]
            if nd["dkey"]:
                etime[e] = start + 60.0
                fin[nid] = start + nd["dur"]
            else:
                fin[nid] = start + nd["dur"]
                etime[e] = fin[nid]
            order[e].append(nid)
            remaining -= 1
        self.makespan = max(fin) if fin else 0.0
        return order

    def emit_all(self, final_q="sp"):
        nc = self.nc
        nodes = self.nodes
        order = self._schedule()
        sems = {e: self.stack.enter_context(nc.semaphore("s_" + e)) for e in self.ENGS}
        dsems = {}
        ev = [None] * len(nodes)
        cnt = {e: 0 for e in self.ENGS}
        dcnt = {}
        for e in self.ENGS:
            for nid in order[e]:
                nd = nodes[nid]
                if nd["dkey"]:
                    k = nd["dkey"]
                    if k not in dsems:
                        dsems[k] = self.stack.enter_context(nc.semaphore("d_" + k))
                        dcnt[k] = 0
                    dcnt[k] += 16 * nd["ndma"]
                    ev[nid] = ("d_" + k, dsems[k], dcnt[k])
                else:
                    cnt[e] += 1
                    ev[nid] = ("s_" + e, sems[e], cnt[e])
        progs = {}
        for e in self.ENGS:
            prog = []
            waited = {}
            for nid in order[e]:
                nd = nodes[nid]
                for d, sem in nd["deps"]:
                    if not sem:
                        continue
                    sname, sh, val = ev[d]
                    if waited.get(sname, 0) >= val:
                        continue
                    waited[sname] = val
                    prog.append(("w", sh, val))
                prog.append(("i", nd["calls"], ev[nid][1], 16 if nd["dkey"] else 1, bool(nd["dkey"])))
            progs[e] = prog
        waited = {}
        fin_waits = []
        for k, sh in dsems.items():
            fin_waits.append(("w", sh, dcnt[k]))
        for e in ("pe", "dve", "act", "pool"):
            if cnt[e]:
                fin_waits.append(("w", sems[e], cnt[e]))
        progs[final_q] = progs[final_q] + fin_waits
        with nc.Block() as block:
            for n, attr in (("sp", "sync"), ("pe", "tensor"), ("dve", "vector"), ("act", "scalar"), ("pool", "gpsimd")):
                prog = progs[n]
                if not prog:
                    continue

                def body(eng, prog=prog):
                    for it in prog:
                        if it[0] == "w":
                            eng.wait_ge(it[1], it[2])
                        else:
                            _, calls, sem, inc, all_inc = it
                            for idx, (name, a, k) in enumerate(calls):
                                inst = getattr(eng, name)(*a, **k)
                                if all_inc or idx == len(calls) - 1:
                                    inst.then_inc(sem, inc)
                getattr(block, attr)(body)


VEC_SPECS = [
    ("norm_mix", 16), ("norm_ffn", 16), ("norm_final", 8), ("ssd_norm", 8), ("sc_conv_b", 8),
    ("lru_conv_b", 8), ("lru_lambda", 8), ("lru_ba", 8), ("lru_bx", 8), ("ssd_conv_w", 48),
    ("ssd_conv_b", 12), ("sc_conv_w", 24), ("lru_conv_w", 32), ("ffn_conv_w", 132), ("ffn_conv_b", 44),
]
W_SHAPES = {
    "norm_mix": [2, D], "norm_ffn": [2, D], "norm_final": [D], "ab_w_in": [D, 5648], "ssd_conv_w": [4, 1536],
    "ssd_conv_b": [1536], "ssd_dt_bias": [16], "ssd_a_log": [16], "ssd_d": [16], "ssd_norm": [D],
    "sc_conv_w": [3, D], "sc_conv_b": [D], "ab_w_out": [2048, D], "lru_w_in": [D, 2048], "lru_conv_w": [4, D],
    "lru_conv_b": [D], "lru_wa": [4, 256, 256], "lru_ba": [4, 256], "lru_wx": [4, 256, 256], "lru_bx": [4, 256],
    "lru_lambda": [D], "lru_w_out": [D, D], "ffn_w_in": [2, D, 2 * DFF], "ffn_conv_w": [2, 3, DFF],
    "ffn_conv_b": [2, DFF], "ffn_w_out": [2, DFF, D],
}
CAR_SPECS = [("ssd_conv", 36), ("sc", 16), ("lc", 24), ("l", 8), ("f", 88)]


def build_nc(NPS, LP, LS):
    nc = bass.Bass("TRN2", target_bir_lowering=False)
    NS = NPS + 1
    din = {}

    def DI(name, shape):
        din[name] = nc.dram_tensor(name, list(shape), F32, kind="ExternalInput").ap()
        return din[name]

    def DO(name, shape):
        return nc.dram_tensor(name, list(shape), F32, kind="ExternalOutput").ap()

    xp = DI("xp", [NPS, LP, D])
    xs_in = DI("xs", [1, LS, D])
    st_in = {"ssd_conv": DI("st_ssd_conv", [3, 1536]), "sc": DI("st_sc", [2, D]), "lc": DI("st_lc", [3, D]),
             "l": DI("st_l", [D]), "f": DI("st_f", [2, 2, DFF])}
    st_ssd = DI("st_ssd", [16, 64, 128])
    Wd = {k: DI(k, v) for k, v in W_SHAPES.items()}
    yp = DO("yp", [NPS, LP, D])
    ys = DO("ys", [1, LS, D])
    st_out = {"ssd_conv": DO("o_ssd_conv", [NS, 3, 1536]), "sc": DO("o_sc", [NS, 2, D]), "lc": DO("o_lc", [NS, 3, D]),
              "l": DO("o_l", [NS, D]), "f": DO("o_f", [2, NS, 2, DFF])}
    o_ssd = DO("o_ssd", [NS, 16, 64, 128])

    units = []

    def add_unit(nk, ncols, pieces, tag):
        units.append(dict(nk=nk, ncols=ncols, pieces=pieces, tag=tag))
        return len(units) - 1

    def colsrc(w2d, c0, cw):
        return lambda k0, nk: w2d[k0 * 128:(k0 + nk) * 128, c0:c0 + cw].rearrange("(k p) n -> p k n", p=128)

    abin = Wd["ab_w_in"]
    U = {}
    U["dt"] = add_unit(8, 48, [(colsrc(abin, 2560, 16), 0, 16), (colsrc(abin, 2560, 16), 32, 16)], "dt")
    U["xbc"] = [add_unit(8, 512, [(colsrc(abin, 1024 + 512 * i, 512), 0, 512)], "xbc") for i in range(3)]
    U["z"] = [add_unit(8, 512, [(colsrc(abin, 512 * i, 512), 0, 512)], "z") for i in range(2)]
    U["sc"] = [add_unit(8, 384, [(colsrc(abin, 3600 + 128 * j, 128), 0, 128), (colsrc(abin, 4624 + 128 * j, 128), 128, 128),
                                 (colsrc(abin, 2576 + 128 * j, 128), 256, 128)], "sc") for j in range(8)]
    U["abo"] = [add_unit(16, 256, [(colsrc(Wd["ab_w_out"], 256 * i, 256), 0, 256)], "abo") for i in range(4)]
    U["fin"] = []
    U["fout"] = []
    for l in range(2):
        wi = Wd["ffn_w_in"][l]
        U["fin"].append([add_unit(8, 512, [(colsrc(wi, 256 * i, 128), 0, 128), (colsrc(wi, DFF + 256 * i, 128), 128, 128),
                                           (colsrc(wi, 256 * i + 128, 128), 256, 128), (colsrc(wi, DFF + 256 * i + 128, 128), 384, 128)],
                                  "fin") for i in range(11)])
        U["fout"].append([add_unit(22, 128, [(colsrc(Wd["ffn_w_out"][l], 128 * m, 128), 0, 128)], "fout") for m in range(8)])
    U["lin"] = [add_unit(8, 512, [(colsrc(Wd["lru_w_in"], 512 * i, 512), 0, 512)], "lin") for i in range(4)]
    wa2 = Wd["lru_wa"].rearrange("h k n -> (h k) n")
    wx2 = Wd["lru_wx"].rearrange("h k n -> (h k) n")
    U["ax"] = [add_unit(8, 256, [("ax", wa2, wx2, hb)], "ax") for hb in (0, 2)]
    U["lout"] = [add_unit(8, 512, [(colsrc(Wd["lru_w_out"], 512 * i, 512), 0, 512)], "lout") for i in range(2)]
    NU = len(units)
    WS = nc.dram_tensor("WS", [NU, 128, USZ], BF16, kind="Internal").ap()

    tile_seq = []
    if STAGE >= 3:
        tile_seq += [U["dt"]]
        if SUB >= 2:
            tile_seq += U["xbc"] + U["z"]
        if SUB >= 5:
            tile_seq += U["sc"]
        if SUB >= 6:
            tile_seq += U["abo"]
    if STAGE >= 4:
        tile_seq += U["fin"][0] + U["fout"][0]
    if STAGE >= 5:
        tile_seq += [U["lin"][2], U["lin"][3], U["ax"][0], U["lin"][0], U["ax"][1], U["lin"][1]] + U["lout"]
    if STAGE >= 6:
        tile_seq += U["fin"][1] + U["fout"][1]

    seqs = [("p", i, LP) for i in range(NPS)] + [("s", 0, LS)]
    tiles = []
    for si, (kind, idx, L) in enumerate(seqs):
        T = min(512, L)
        assert L % T == 0
        for ti in range(L // T):
            tiles.append(dict(si=si, kind=kind, idx=idx, t0=ti * T, T=T, first=(ti == 0), last=(ti == L // T - 1)))
    full_seq = tile_seq * len(tiles)

    with ExitStack() as st:
        S = Sched(nc, st)

        def sb(name, shape, dt):
            return st.enter_context(nc.sbuf_tensor(name, shape, dt))

        def psum(name, shape, dt):
            return st.enter_context(nc.psum_tensor(name, shape, dt))

        NRING = int(os.environ.get('KNRING', '3'))
        ring = [sb("ring%d" % i, [128, USZ], BF16) for i in range(NRING)]
        Bring = S.bufs(NRING, "ring")
        x_res = sb("x_res", [128, 8, 512], F32)
        Bx = S.bufs(8, "xres")
        xin = [sb("xin%d" % i, [128, D], F32) for i in range(4)]
        Bxin = S.bufs(4, "xin")
        yout = [sb("yout%d" % i, [128, D], F32) for i in range(2)]
        Byout = S.bufs(2, "yout")
        BF8 = [sb("bf8_%d" % i, [128, 8, 512], BF16) for i in range(5)]
        BBF8 = [S.bufs(8, "bf8_%d_" % i) for i in range(5)]
        NFT = int(os.environ.get('KNFT', '11'))
        F32T = [sb("f32t%d" % i, [128, 516], F32) for i in range(NFT)]
        BFT = S.bufs(NFT, "f32t")
        BC = sb("BC", [128, 4, 512], BF16)
        BBC = S.bufs(4, "BC")
        YG = sb("YG", [128, 8, 512], F32)
        BYG = S.bufs(8, "YG")
        sqb = [sb("sqb%d" % i, [128, 512], BF16) for i in range(4)]
        Bsqb = S.bufs(4, "sqb")
        t_e = sb("t_e", [48, 512], F32)
        t_dt = sb("t_dt", [48, 512], F32)
        t_ln = t_e
        t_dA = sb("t_dA", [48, 512], F32)
        t_At = sb("t_At", [48, 512], F32)
        t_w = t_dA
        LT = sb("LT", [48, 512], F32)
        Bte, Btdt, Btln, BtdA, BtAt, Btw, BLT = S.bufs(7, "ssdsm")
        Btln = Bte
        Btw = BtdA
        RH = [sb("RH%d" % i, [48, 1024], F32) for i in range(2)]
        BRH = S.bufs(2, "RH")
        segm = sb("segm", [128, 1024], F32)
        Bsegm = S.buf("segm")
        Lb = sb("Lb", [128, 1024], BF16)
        BLb = S.buf("Lb")
        Mb = [sb("Mb%d" % i, [128, 1024], BF16) for i in range(2)]
        BMb = S.bufs(2, "Mb")
        xpad = sb("xpad", [128, 2048], BF16)
        Bxpad = S.buf("xpad")
        Xdd = sb("Xdd", [128, 1024], BF16)
        BXdd = S.buf("Xdd")
        Btm = sb("Btm", [128, 256], BF16)
        BBtm = S.buf("Btm")
        CBs = sb("CBs", [128, 256], BF16)
        BCBs = S.buf("CBs")
        wtm = sb("wtm", [128, 16], F32)
        Bwtm = S.buf("wtm")
        cd = sb("cd", [128, 8, 4], F32)
        Bcd = S.buf("cd")
        hst = sb("hst", [128, 8, 128], F32)
        Bhst = S.buf("hst")
        hT = sb("hT", [128, 1024], BF16)
        BhT = S.buf("hT")
        hl = sb("hl", [128, 8], F32)
        Bhl = S.bufs(8, "hl")
        rstd = sb("rstd", [128, 512], F32)
        Brstd = S.buf("rstd")
        identf = sb("identf", [128, 128], F32)
        identb = sb("identb", [128, 128], BF16)
        ones_bf = sb("ones_bf", [128, 128], BF16)
        negmask = sb("negmask", [128, 128], F32)
        rmask = sb("rmask", [48, 512], F32)
        SelHP = sb("SelHP", [16, 8, 128], F32)
        NVEC = sum(r for _, r in VEC_SPECS)
        CV = sb("CV", [128, 384], F32)
        CAR = sb("CAR", [128, 176], F32)
        rows = sb("rows", [128, 5, 128], F32)
        Brows = S.buf("rows")
        A48 = sb("A48", [48, 1], F32)
        dtb48 = sb("dtb48", [48, 1], F32)
        dsk = sb("dsk", [128, 8], F32)
        clc = sb("clc", [128, 8], F32)
        cl2 = sb("cl2", [128, 8], F32)
        clh = sb("clh", [128, 8], F32)
        hb = sb("hb", [128, 16], F32)
        Bconst = S.buf("const")
        BCV = S.buf("CV")

        psF = psum("psF", [128, 6, 512], F32)
        BpsF = S.bufs(6, "psF")
        psX = psum("psX", [128, 1024], BF16)
        BpsX = S.buf("psX")
        psM = psum("psM", [128, 512], F32)
        BpsM = S.buf("psM")
        psB = psM[:, 384:512].bitcast(BF16)
        print("psB", psB.shape, psB.ap, psB.offset)
        for b_ in BpsF + [BpsX, BpsM]:
            b_.x = True
        BpsB = BpsM

        voff = {}
        o = 0
        for name, r in VEC_SPECS:
            voff[name] = o
            o += r

        def cvc(name, idx):
            c = voff[name] + idx
            return CV[:, c:c + 1]

        coff = {}
        o = 0
        for name, r in CAR_SPECS:
            coff[name] = o
            o += r
        Bcar = {"ssd_conv": S.bufs(12, "c_xbc"), "sc": S.bufs(8, "c_sc"), "lc": S.bufs(8, "c_lc"), "l": Bhl,
                "f": [S.bufs(22, "c_f0_"), S.bufs(22, "c_f1_")]}

        def car_cols(name, nk, nj, j, base=0):
            c0 = coff[name] + base + j
            return CAR[:, c0:c0 + (nk - 1) * nj + 1:nj]

        bank_ctr = [0]

        def next_bank():
            b = bank_ctr[0] % 6
            bank_ctr[0] += 1
            return b

        ft_ctr = [0]

        def next_ft():
            i = ft_ctr[0] % NFT
            ft_ctr[0] += 1
            return i

        evac_ctr = [0]

        def evac_eng():
            evac_ctr[0] += 1
            return "act" if evac_ctr[0] % 2 else "dve"

        def copy_op(en, out, in_):
            if en == "act":
                return lambda e: e.activation(out=out, in_=in_, func=AF.Copy)
            return lambda e: e.tensor_copy(out=out, in_=in_)

        S.op("pool", lambda e: e.memset(identf[:], 1.0), writes=[Bconst])
        S.op("pool", lambda e: e.affine_select(out=identf[:], in_=identf[:], pattern=[[-1, 128]], compare_op=ALU.is_equal,
                                               fill=0.0, base=0, channel_multiplier=1), reads=[Bconst], writes=[Bconst])
        S.op("pool", lambda e: e.tensor_copy(out=identb[:], in_=identf[:]), reads=[Bconst], writes=[Bconst])
        S.op("pool", lambda e: e.memset(ones_bf[:], 1.0), writes=[Bconst])
        S.op("pool", lambda e: e.memset(negmask[:], 0.0), writes=[Bconst])
        S.op("pool", lambda e: e.affine_select(out=negmask[:], in_=negmask[:], pattern=[[1, 128]], compare_op=ALU.is_ge,
                                               fill=-1.0e5, base=0, channel_multiplier=-1), reads=[Bconst], writes=[Bconst])
        S.op("pool", lambda e: e.memset(rmask[:], 1.0), writes=[Bconst])
        S.op("pool", lambda e: e.memset(rmask[:, 0:512:128], 0.0), reads=[Bconst], writes=[Bconst])
        S.op("pool", lambda e: e.memset(SelHP[:], 1.0), writes=[Bconst])
        S.op("pool", lambda e: e.affine_select(out=SelHP[:], in_=SelHP[:], pattern=[[-2, 8], [-1, 2], [0, 64]],
                                               compare_op=ALU.is_equal, fill=0.0, base=0, channel_multiplier=1),
             reads=[Bconst], writes=[Bconst])
        S.op("pool", lambda e: e.memset(LT[:], 1.0), writes=[BLT])
        rh_cl = [0]

        def init_RH(CL):
            if rh_cl[0] == CL:
                return
            rh_cl[0] = CL
            for hh in range(2):
                S.op("pool", lambda e, hh=hh: e.memset(RH[hh][:], 0.0), writes=[BRH[hh]])
                S.op("pool", lambda e, hh=hh: e.memset(RH[hh][32:48, 0:8 * CL], 1.0), reads=[BRH[hh]], writes=[BRH[hh]])
                S.op("pool", lambda e, hh=hh: e.affine_select(
                    out=RH[hh][32:48, 0:8 * CL].rearrange("p (h l) -> p h l", h=8),
                    in_=RH[hh][32:48, 0:8 * CL].rearrange("p (h l) -> p h l", h=8),
                    pattern=[[-1, 8], [0, CL]], compare_op=ALU.is_equal, fill=0.0, base=-8 * hh, channel_multiplier=1),
                    reads=[BRH[hh]], writes=[BRH[hh]])
        S.op("pool", lambda e: e.memset(xpad[:], 0.0), writes=[Bxpad])
        S.op("pool", lambda e: e.memset(CAR[:], 0.0), writes=[Bconst])

        Bvst = S.buf("vst")
        vst = [x_res[:, 0, 0:128], x_res[:, 1, 0:128], x_res[:, 2, 0:128]]

        def flat_rows(ap):
            n = len(ap.shape)
            if n == 1:
                return ap.rearrange("(r c) -> r c", c=128)
            if n == 2:
                return ap.rearrange("a (r c) -> (a r) c", c=128)
            return ap.rearrange("a b (r c) -> (a b r) c", c=128)

        S.op("pool", lambda e: e.memset(x_res[:, 0:3, 0:128], 0.0), writes=[Bvst])
        ndm = 0
        emits = []
        for name, r in VEC_SPECS:
            src = flat_rows(Wd[name])
            o0 = voff[name]
            done = 0
            while done < r:
                g = (o0 + done) // 128
                p0 = (o0 + done) % 128
                n = min(r - done, 128 - p0)
                emits.append((x_res[p0:p0 + n, g, 0:128], src[done:done + n, :]))
                done += n
        S.dma("sp", "vec", lambda e: [e.dma_start(out=a, in_=b) for a, b in emits], reads=[Bvst], writes=[Bvst], n=len(emits))
        for g in range(3):
            S.ops("pe", [lambda e, g=g: e.transpose(psF[:, g, 0:128], in_=vst[g], identity=identf[:])],
                  reads=[Bvst, Bconst], writes=[BpsF[g]])
            S.op("dve", lambda e, g=g: e.tensor_copy(out=CV[:, g * 128:(g + 1) * 128], in_=psF[:, g, 0:128]),
                 reads=[BpsF[g]], writes=[BCV])
        S.op("pool", lambda e: e.memset(A48[:], 0.0), writes=[Bconst])
        S.op("pool", lambda e: e.memset(dtb48[:], 0.0), reads=[Bconst], writes=[Bconst])
        al = Wd["ssd_a_log"].rearrange("(h o) -> h o", o=1)
        db = Wd["ssd_dt_bias"].rearrange("(h o) -> h o", o=1)
        S.dma("sp", "c1", lambda e: [e.dma_start(out=A48[0:16, :], in_=al), e.dma_start(out=A48[32:48, :], in_=al),
                                     e.dma_start(out=dtb48[0:16, :], in_=db), e.dma_start(out=dtb48[32:48, :], in_=db)],
              reads=[Bconst], writes=[Bconst], n=4)
        S.op("act", lambda e: e.activation(out=A48[:], in_=A48[:], func=AF.Exp), reads=[Bconst], writes=[Bconst])
        S.op("dve", lambda e: e.tensor_scalar(out=A48[:], in0=A48[:], scalar1=-1.0, scalar2=None, op0=ALU.mult),
             reads=[Bconst], writes=[Bconst])
        dsrc = Wd["ssd_d"]
        S.dma("sp", "c2", lambda e: [
            e.dma_start(out=dsk[0:64, :], in_=bass.AP(dsrc.tensor, 0, [[0, 64], [2, 8]]), allow_slow_non_contiguous=True),
            e.dma_start(out=dsk[64:128, :], in_=bass.AP(dsrc.tensor, 1, [[0, 64], [2, 8]]), allow_slow_non_contiguous=True)],
            reads=[Bconst], writes=[Bconst], n=2)
        lamc = CV[:, voff["lru_lambda"]:voff["lru_lambda"] + 8]
        S.op("act", lambda e: e.activation(out=clc[:], in_=lamc, func=AF.Exp, scale=-1.0), reads=[BCV], writes=[Bconst])
        S.op("act", lambda e: e.activation(out=clc[:], in_=clc[:], func=AF.Ln, bias=1.0), reads=[Bconst], writes=[Bconst])
        S.op("dve", lambda e: e.tensor_scalar(out=cl2[:], in0=clc[:], scalar1=-16.0, scalar2=None, op0=ALU.mult),
             reads=[Bconst], writes=[Bconst])
        S.op("dve", lambda e: e.tensor_scalar(out=clh[:], in0=clc[:], scalar1=-4.0, scalar2=None, op0=ALU.mult),
             reads=[Bconst], writes=[Bconst])
        S.op("dve", lambda e: e.tensor_scalar(out=clc[:], in0=clc[:], scalar1=-8.0, scalar2=None, op0=ALU.mult),
             reads=[Bconst], writes=[Bconst])
        S.op("dve", lambda e: e.tensor_scalar(out=hb[:, 0:8], in0=CV[:, voff["lru_ba"]:voff["lru_ba"] + 8], scalar1=0.5, scalar2=None,
                                              op0=ALU.mult), reads=[BCV], writes=[Bconst])
        S.op("dve", lambda e: e.tensor_scalar(out=hb[:, 8:16], in0=CV[:, voff["lru_bx"]:voff["lru_bx"] + 8], scalar1=0.5, scalar2=None,
                                              op0=ALU.mult), reads=[BCV], writes=[Bconst])

        BWS = S.bufs(NU, "WS")
        NSTG = 4
        stg32 = [x_res[:, 0:4, :].rearrange("p a b -> p (a b)"), x_res[:, 4:8, :].rearrange("p a b -> p (a b)"),
                 YG[:, 0:4, :].rearrange("p a b -> p (a b)"), YG[:, 4:8, :].rearrange("p a b -> p (a b)")]
        stg16 = [BF8[0][:, 0:4, :].rearrange("p a b -> p (a b)"), BF8[0][:, 4:8, :].rearrange("p a b -> p (a b)"),
                 BF8[1][:, 0:4, :].rearrange("p a b -> p (a b)"), BF8[1][:, 4:8, :].rearrange("p a b -> p (a b)")]
        Bs32 = [Bvst, S.buf("s32b"), S.buf("s32c"), S.buf("s32d")]
        Bs16 = S.bufs(4, "s16")
        ceng = ["dve", "act", "pool"]
        jobs = []
        for u, un in enumerate(units if STAGE >= 1 else []):
            nk, ncols = un["nk"], un["ncols"]
            hk = nk // 2
            for half in range(2):
                jobs.append((u, un, hk, ncols, half))
        DEPTH = 2

        def cv_in(i):
            u, un, hk, ncols, half = jobs[i]
            s = i % NSTG
            n_el = hk * ncols
            s32 = stg32[s][:, 0:n_el].rearrange("p (k n) -> p k n", k=hk)
            if un["tag"] == "ax":
                _, a2, x2 = un["pieces"][0]
                srcm = a2 if half == 0 else x2
                pcs = [(s32, srcm.rearrange("(k p) n -> p k n", p=128))]
            else:
                pcs = [(s32[:, :, c0:c0 + cw], fn(half * hk, hk)) for fn, c0, cw in un["pieces"]]
            if un["tag"] == "dt":
                S.op("pool", lambda e: e.memset(s32, 0.0), reads=[Bs32[s]], writes=[Bs32[s]])
            S.dma("sp", "cvi%d" % s, lambda e: [e.dma_start(out=a_, in_=b_, allow_slow_non_contiguous=True) for a_, b_ in pcs],
                  reads=[Bs32[s]], writes=[Bs32[s]])

        def cv_out(i):
            u, un, hk, ncols, half = jobs[i]
            s = i % NSTG
            n_el = hk * ncols
            en = ceng[i % 3]
            S.op(en, copy_op(en, stg16[s][:, 0:n_el], stg32[s][:, 0:n_el]), reads=[Bs32[s]], writes=[Bs16[s]])
            S.dma("sp", "cvo%d" % s, lambda e: e.dma_start(out=WS[u, :, half * n_el:(half + 1) * n_el], in_=stg16[s][:, 0:n_el]),
                  reads=[Bs16[s]], writes=[BWS[u]])

        for i in range((len(jobs) + DEPTH) if not LAZY else 0):
            if i < len(jobs):
                cv_in(i)
            if i >= DEPTH:
                cv_out(i - DEPTH)
        def inherit(dsts, srcs):
            for b in dsts:
                for sb_ in srcs:
                    if sb_.w is not None:
                        b.rl.append(sb_.w)
                    b.rl.extend(sb_.rl)
        if not LAZY:
            inherit(Bx, Bs32[0:2])
            inherit(BYG, Bs32[2:4])
            inherit(BBF8[0], Bs16[0:2])
            inherit(BBF8[1], Bs16[2:4])

        wstate = dict(issued=0, cur=0)

        lazy_ctr = [0]
        lz_eng = ["act", "dve"]
        lz_stg = [(yout[0], Byout[0]), (xin[0], Bxin[0]), (xin[1], Bxin[1]), (yout[1], Byout[1]), (xin[2], Bxin[2]), (xin[3], Bxin[3])]

        def w_issue():
            i = wstate["issued"]
            if i >= len(full_seq):
                return
            u = full_seq[i]
            s = i % NRING
            un = units[u]
            nk, ncols = un["nk"], un["ncols"]
            n_el = nk * ncols
            if LAZY and i < len(tile_seq):
                kper = max(1, min(nk, 1024 // ncols))
                k0 = 0
                while k0 < nk:
                    kn = min(kper, nk - k0)
                    q = lazy_ctr[0] % len(lz_stg)
                    ne = kn * ncols
                    stg_t, stg_b = lz_stg[q]
                    s32 = stg_t[:, 0:ne].rearrange("p (k n) -> p k n", k=kn)
                    if un["tag"] == "ax":
                        _, a2, x2, hb = un["pieces"][0]
                        srcm = a2 if k0 < 4 else x2
                        kk = (k0 % 4) + hb * 2
                        pcs = [(s32, srcm[kk * 128:(kk + kn) * 128, :].rearrange("(k p) n -> p k n", p=128))]
                    else:
                        pcs = [(s32[:, :, c0:c0 + cw], fn(k0, kn)) for fn, c0, cw in un["pieces"]]
                    if un["tag"] == "dt":
                        S.op("pool", lambda e, s32=s32: e.memset(s32, 0.0), reads=[stg_b], writes=[stg_b])
                    S.dma("sp", "lz%d" % q, lambda e, pcs=pcs: [e.dma_start(out=a_, in_=b_, allow_slow_non_contiguous=True) for a_, b_ in pcs],
                          reads=[stg_b], writes=[stg_b])
                    en = lz_eng[lazy_ctr[0] % 2]
                    S.op(en, copy_op(en, ring[s][:, k0 * ncols:k0 * ncols + ne], stg_t[:, 0:ne]), reads=[stg_b], writes=[Bring[s]])
                    lazy_ctr[0] += 1
                    k0 += kn
                S.dma(LZQ, "lzo%d" % s, lambda e, u=u, s=s, n_el=n_el: e.dma_start(out=WS[u, :, 0:n_el], in_=ring[s][:, 0:n_el]),
                      reads=[Bring[s]], writes=[BWS[u]])
            else:
                S.dma("sp", "w%d" % s, lambda e, u=u, s=s, n_el=n_el: e.dma_start(out=ring[s][:, 0:n_el], in_=WS[u, :, 0:n_el]),
                      reads=[BWS[u]], writes=[Bring[s]])
            wstate["issued"] += 1

        def w_acquire(u_expected):
            i = wstate["cur"]
            assert full_seq[i] == u_expected, (i, full_seq[i], u_expected)
            while wstate["issued"] < min(i + NRING, len(full_seq)):
                w_issue()
            wstate["cur"] += 1
            s = i % NRING
            return ring[s], Bring[s]

        def w_release():
            while wstate["issued"] < min(wstate["cur"] + NRING - 1, len(full_seq)):
                w_issue()

        def rms_norm(T, src_aps, src_bufs, gname, gbase, dst_aps, dst_bufs):
            b = next_bank()
            for j in range(8):
                q = j % 4
                if False:
                    S.op("pool", lambda e, j=j, q=q: e.tensor_tensor(out=sqb[q][:, 0:T], in0=src_aps[j], in1=src_aps[j], op=ALU.mult),
                         reads=[src_bufs[j]], writes=[Bsqb[q]])
                else:
                    S.op("act", lambda e, j=j, q=q: e.activation(out=sqb[q][:, 0:T], in_=src_aps[j], func=AF.Square),
                         reads=[src_bufs[j]], writes=[Bsqb[q]])
                S.ops("pe", [lambda e, j=j, q=q, b=b: e.matmul(psF[:, b, 0:T], lhsT=ones_bf[:], rhs=sqb[q][:, 0:T],
                                                             start=(j == 0), stop=(j == 7))],
                      reads=[Bsqb[q], Bconst], writes=[BpsF[b]])
            S.op("act", lambda e, b=b: e.activation(out=rstd[:, 0:T], in_=psF[:, b, 0:T], func=AF.Sqrt, bias=EPS, scale=1.0 / D),
                 reads=[BpsF[b]], writes=[Brstd])
            S.op("dve", lambda e: e.reciprocal(out=rstd[:, 0:T], in_=rstd[:, 0:T]), reads=[Brstd], writes=[Brstd])
            for j in range(8):
                if False:
                    it = next_ft()
                    S.op("pool", lambda e, j=j, it=it: e.tensor_scalar(out=F32T[it][:, 0:T], in0=src_aps[j], scalar1=cvc(gname, gbase + j),
                                                                      scalar2=None, op0=ALU.mult), reads=[src_bufs[j], BCV], writes=[BFT[it]])
                    S.op("pool", lambda e, j=j, it=it: e.tensor_tensor(out=dst_aps[j], in0=F32T[it][:, 0:T], in1=rstd[:, 0:T], op=ALU.mult),
                         reads=[BFT[it], Brstd], writes=[dst_bufs[j]])
                else:
                    S.op("dve", lambda e, j=j: e.scalar_tensor_tensor(out=dst_aps[j], in0=src_aps[j], scalar=cvc(gname, gbase + j),
                                                                     in1=rstd[:, 0:T], op0=ALU.mult, op1=ALU.mult),
                         reads=[src_bufs[j], Brstd, BCV], writes=[dst_bufs[j]])

        def proj_chunks(u, T, rhs_fn, rhs_bufs, nk, ncols, M, consume, fine=False):
            slot, Bslot = w_acquire(u)
            noc = max(1, ncols // 128)
            rb = list(rhs_bufs)
            for oc in range(noc):
                b = next_bank()
                ems = []
                for kc in range(nk):
                    c0 = kc * ncols + oc * 128
                    ems.append(lambda e, kc=kc, c0=c0, b=b: e.matmul(psF[0:M, b, 0:T], lhsT=slot[:, c0:c0 + M], rhs=rhs_fn(kc),
                                                                      start=(kc == 0), stop=(kc == nk - 1)))
                if fine and oc == 0 and len(rb) == nk:
                    for kc in range(nk):
                        S.ops("pe", [ems[kc]], reads=[Bslot, rb[kc]], writes=[BpsF[b]])
                else:
                    S.ops("pe", ems, reads=[Bslot] + rb, writes=[BpsF[b]])
                consume(oc, psF[0:M, b, 0:T], BpsF[b])
            w_release()

        def conv_chunk(T, ps_ap, Bps, K, cname, nj, j, carbuf, wname, wbase, bname, bidx, in_mul=None, car_base=0, wstride=None):
            H = K - 1
            it = next_ft()
            tmp = F32T[it]
            cc = car_cols(cname, H, nj, j, car_base)
            S.op("pool", lambda e: e.tensor_copy(out=tmp[:, 0:H], in_=cc), reads=[carbuf], writes=[BFT[it]])
            if in_mul is None:
                S.op("act", lambda e: e.activation(out=tmp[:, H:H + T], in_=ps_ap, func=AF.Copy), reads=[Bps], writes=[BFT[it]])
            else:
                g_ap, g_buf = in_mul
                S.op("dve", lambda e: e.tensor_tensor(out=tmp[:, H:H + T], in0=g_ap, in1=ps_ap, op=ALU.mult),
                     reads=[Bps, g_buf], writes=[BFT[it]])
            S.op("pool", lambda e: e.tensor_copy(out=cc, in_=tmp[:, T:T + H]), reads=[BFT[it]], writes=[carbuf])
            ia = next_ft()
            acc = F32T[ia]
            ws = wstride if wstride is not None else nj
            if in_mul is None and ACTTAP:
                S.op("act", lambda e: e.activation(out=acc[:, 0:T], in_=ps_ap, func=AF.Identity, scale=cvc(wname, wbase + H * ws + j),
                                                   bias=cvc(bname, bidx)), reads=[Bps, BCV], writes=[BFT[ia]])
                taps = range(0, H)
            else:
                S.op("dve", lambda e: e.tensor_scalar(out=acc[:, 0:T], in0=tmp[:, 0:T], scalar1=cvc(wname, wbase + j),
                                                      scalar2=cvc(bname, bidx), op0=ALU.mult, op1=ALU.add),
                     reads=[BFT[it], BCV], writes=[BFT[ia]])
                taps = range(1, K)
            for k in taps:
                S.op("dve", lambda e, k=k: e.scalar_tensor_tensor(out=acc[:, 0:T], in0=tmp[:, k:k + T], scalar=cvc(wname, wbase + k * ws + j),
                                                                 in1=acc[:, 0:T], op0=ALU.mult, op1=ALU.add),
                     reads=[BFT[it], BFT[ia], BCV], writes=[BFT[ia]])
            return acc, ia

        def ffn(l, T):
            hn, Bhn = BF8[0], BBF8[0]
            rms_norm(T, [x_res[:, j, 0:T] for j in range(8)], Bx, "norm_ffn", 8 * l, [hn[:, j, 0:T] for j in range(8)], Bhn)

            def a_ap(j):
                return BF8[1 + j // 8][:, j % 8, 0:T], BBF8[1 + j // 8][j % 8]
            for i in range(11):
                stt = {}

                def consume(oc, ps_ap, Bps, i=i, stt=stt):
                    j = 2 * i + oc // 2
                    if oc % 2 == 0:
                        acc, ia = conv_chunk(T, ps_ap, Bps, 3, "f", 22, j, Bcar["f"][l][j], "ffn_conv_w", l * 66, "ffn_conv_b", l * 22 + j,
                                             car_base=l * 44)
                        S.op("act", lambda e: e.activation(out=acc[:, 0:T], in_=acc[:, 0:T], func=AF.Gelu_apprx_tanh),
                             reads=[BFT[ia]], writes=[BFT[ia]])
                        stt["g"] = (acc, ia)
                    else:
                        acc, ia = stt["g"]
                        da, db_ = a_ap(j)
                        S.op("dve", lambda e: e.tensor_tensor(out=da, in0=acc[:, 0:T], in1=ps_ap, op=ALU.mult),
                             reads=[BFT[ia], Bps], writes=[db_])
                proj_chunks(U["fin"][l][i], T, lambda kc: hn[:, kc, 0:T], Bhn, 8, 512, 128, consume, fine=(i == 0))
            allA = [a_ap(j)[1] for j in range(22)]
            for m in range(8):
                def consume(oc, ps_ap, Bps, m=m):
                    S.op("dve", lambda e: e.tensor_tensor(out=x_res[:, m, 0:T], in0=x_res[:, m, 0:T], in1=ps_ap, op=ALU.add),
                         reads=[Bx[m], Bps], writes=[Bx[m]])
                proj_chunks(U["fout"][l][m], T, lambda kc: a_ap(kc)[0], allA, 22, 128, 128, consume, fine=(m == 0))

        def mixer_ab(T):
            CL = min(128, T)
            NCH = T // CL
            init_RH(CL)
            hn, Bhn = BF8[0], BBF8[0]
            zs, Bzs = BF8[1], BBF8[1]
            xs_, Bxs = BF8[2], BBF8[2]
            yc, Byc = BF8[3], BBF8[3]
            EBt, BEB = BF8[4], BBF8[4]
            ysc, Bysc = BF8[4], BBF8[4]
            rms_norm(T, [x_res[:, j, 0:T] for j in range(8)], Bx, "norm_mix", 0, [hn[:, j, 0:T] for j in range(8)], Bhn)
            rhs_fn = lambda kc: hn[:, kc, 0:T]

            def consume_dt(oc, ps_ap, Bps):
                S.op("act", lambda e: e.activation(out=t_e[:, 0:T], in_=ps_ap, func=AF.Exp, bias=dtb48[:, 0:1]),
                     reads=[Bps, Bconst], writes=[Bte])
                S.op("act", lambda e: e.activation(out=t_dt[:, 0:T], in_=t_e[:, 0:T], func=AF.Ln, bias=1.0), reads=[Bte], writes=[Btdt])
                S.op("act", lambda e: e.activation(out=t_ln[:, 0:T], in_=t_dt[:, 0:T], func=AF.Ln), reads=[Btdt], writes=[Btln])
                S.op("dve", lambda e: e.tensor_scalar(out=t_dA[:, 0:T], in0=t_dt[:, 0:T], scalar1=A48[:, 0:1], scalar2=None, op0=ALU.mult),
                     reads=[Btdt, Bconst], writes=[BtdA])
                S.op("dve", lambda e: e.tensor_tensor_scan(out=t_At[:, 0:T], data0=rmask[:, 0:T], data1=t_dA[:, 0:T], initial=0.0,
                                                          op0=ALU.mult, op1=ALU.add), reads=[BtdA, Bconst], writes=[BtAt])
                S.op("dve", lambda e: e.tensor_tensor(out=LT[32:48, 0:T], in0=t_ln[32:48, 0:T], in1=t_At[32:48, 0:T], op=ALU.subtract),
                     reads=[Btln, BtAt], writes=[BLT])
                At3 = t_At[0:16, 0:T].rearrange("p (c l) -> p c l", c=NCH)
                S.op("dve", lambda e: e.tensor_tensor(out=t_w[0:16, 0:T].rearrange("p (c l) -> p c l", c=NCH),
                                                      in0=At3[:, :, CL - 1:CL].broadcast_to([16, NCH, CL]), in1=At3, op=ALU.subtract),
                     reads=[BtAt], writes=[Btw])
                S.op("act", lambda e: e.activation(out=t_w[0:16, 0:T], in_=t_w[0:16, 0:T], func=AF.Exp), reads=[Btw], writes=[Btw])
                S.op("dve", lambda e: e.tensor_tensor(out=t_w[0:16, 0:T], in0=t_w[0:16, 0:T], in1=t_dt[0:16, 0:T], op=ALU.mult),
                     reads=[Btw, Btdt], writes=[Btw])
            proj_chunks(U["dt"], T, rhs_fn, Bhn, 8, 48, 48, consume_dt, fine=True)

            if SUB < 2:
                return
            for i in range(3):
                def consume(oc, ps_ap, Bps, i=i):
                    j = 4 * i + oc
                    acc, ia = conv_chunk(T, ps_ap, Bps, 4, "ssd_conv", 12, j, Bcar["ssd_conv"][j], "ssd_conv_w", 0, "ssd_conv_b", j)
                    if j < 8:
                        dst, dbf = xs_[:, j, 0:T], Bxs[j]
                    else:
                        dst, dbf = BC[:, j - 8, 0:T], BBC[j - 8]
                    S.op("act", lambda e: e.activation(out=dst, in_=acc[:, 0:T], func=AF.Silu), reads=[BFT[ia]], writes=[dbf])
                proj_chunks(U["xbc"][i], T, rhs_fn, Bhn, 8, 512, 128, consume)
            for i in range(2):
                def consume(oc, ps_ap, Bps, i=i):
                    j = 4 * i + oc
                    S.op("act", lambda e: e.activation(out=zs[:, j, 0:T], in_=ps_ap, func=AF.Silu), reads=[Bps], writes=[Bzs[j]])
                proj_chunks(U["z"][i], T, rhs_fn, Bhn, 8, 512, 128, consume)

            if SUB < 3:
                return
            for j in range(8):
                b = next_bank()
                S.ops("pe", [lambda e, j=j, b=b: e.matmul(psF[:, b, 0:T], lhsT=SelHP[0:16, j, :], rhs=t_At[0:16, 0:T], start=True, stop=True)],
                      reads=[BtAt, Bconst], writes=[BpsF[b]])
                S.op("act", lambda e, j=j, b=b: e.activation(out=EBt[:, j, 0:T], in_=psF[:, b, 0:T], func=AF.Exp),
                     reads=[BpsF[b]], writes=[BEB[j]])
                S.op("act", lambda e, j=j, b=b: e.activation(out=cd[:, j, 0:NCH], in_=psF[:, b, CL - 1:T:CL], func=AF.Exp),
                     reads=[BpsF[b]], writes=[Bcd])
            for c in range(NCH):
                tk = slice(c * CL, (c + 1) * CL)
                S.ops("pe", [lambda e, j=j: e.transpose(psX[0:CL, j * 128:(j + 1) * 128], in_=xs_[:, j, tk], identity=identb[:])
                             for j in range(8)], reads=Bxs + [Bconst], writes=[BpsX])
                S.ops("pe", [lambda e, g=g: e.transpose(psB[0:CL, g * 128:(g + 1) * 128], in_=BC[:, g, tk], identity=identb[:])
                             for g in range(2)], reads=[BBC[0], BBC[1], Bconst], writes=[BpsB])
                S.ops("pe", [lambda e: e.transpose(psM[0:CL, 256:272], in_=t_w[0:16, tk], identity=identf[0:16, 0:16])] +
                      [lambda e, g=g: e.matmul(psM[0:CL, g * 128:g * 128 + CL], lhsT=BC[:, g, tk], rhs=BC[:, 2 + g, tk], start=True, stop=True)
                       for g in range(2)], reads=[Btw, Bconst] + BBC, writes=[BpsM])
                S.op("act", lambda e: e.activation(out=wtm[0:CL, :], in_=psM[0:CL, 256:272], func=AF.Copy), reads=[BpsM], writes=[Bwtm])
                S.op("dve", lambda e: e.tensor_copy(
                    out=bass.AP(xpad[:].tensor, xpad[:].offset, [[xpad[:].ap[0][0], CL], [256, 8], [192, 2], [1, 64]]),
                    in_=psX[0:CL, :].rearrange("p (j e d) -> p j e d", j=8, e=2)), reads=[BpsX], writes=[Bxpad])
                S.op("dve", lambda e: e.tensor_tensor(out=Xdd[0:CL, :].rearrange("p (h d) -> p h d", h=16),
                                                      in0=psX[0:CL, :].rearrange("p (h d) -> p h d", h=16),
                                                      in1=wtm[0:CL, :].unsqueeze(2).broadcast_to([CL, 16, 64]), op=ALU.mult),
                     reads=[BpsX, Bwtm], writes=[BXdd])
                S.op("act", lambda e: e.activation(out=Btm[0:CL, :], in_=psB[0:CL, :], func=AF.Copy), reads=[BpsB], writes=[BBtm])
                S.op("act", lambda e: e.activation(out=CBs[0:CL, :], in_=psM[0:CL, 0:256], func=AF.Copy), reads=[BpsM], writes=[BCBs])
                nq = (8 * CL) // 512
                for hh in range(2):
                    S.op("pool", lambda e, hh=hh: e.affine_select(
                        out=RH[hh][0:16, 0:8 * CL].rearrange("p (h l) -> p h l", h=8),
                        in_=t_At[0:16, tk].unsqueeze(1).broadcast_to([16, 8, CL]),
                        pattern=[[-1, 8], [0, CL]], compare_op=ALU.is_equal, fill=0.0, base=-8 * hh, channel_multiplier=1),
                        reads=[BtAt], writes=[BRH[hh]])
                    for q in range(nq):
                        S.ops("pe", [lambda e, hh=hh, q=q: e.matmul(psF[0:CL, 2 * hh + q, :], lhsT=LT[0:48, tk],
                                                                     rhs=RH[hh][0:48, q * 512:(q + 1) * 512], start=True, stop=True)],
                              reads=[BLT, BRH[hh]], writes=[BpsF[2 * hh + q]])
                    seg_ap = psF[0:CL, 2 * hh:2 * hh + nq, :].rearrange("p q (h l) -> p (q h) l", l=CL)
                    S.op("dve", lambda e, seg_ap=seg_ap: e.tensor_tensor(
                        out=segm[0:CL, 0:8 * CL].rearrange("p (h l) -> p h l", h=8), in0=seg_ap,
                        in1=negmask[0:CL, 0:CL].unsqueeze(1).broadcast_to([CL, 8, CL]), op=ALU.add),
                        reads=[BpsF[2 * hh + q] for q in range(nq)] + [Bconst], writes=[Bsegm])
                    S.op("act", lambda e: e.activation(out=Lb[0:CL, 0:8 * CL], in_=segm[0:CL, 0:8 * CL], func=AF.Exp),
                         reads=[Bsegm], writes=[BLb])
                    S.op("dve", lambda e, hh=hh: e.tensor_tensor(
                        out=Mb[hh][0:CL, 0:8 * CL].rearrange("p (h l) -> p h l", h=8),
                        in0=Lb[0:CL, 0:8 * CL].rearrange("p (h l) -> p h l", h=8),
                        in1=CBs[0:CL, hh * 128:hh * 128 + CL].unsqueeze(1).broadcast_to([CL, 8, CL]), op=ALU.mult),
                        reads=[BLb, BCBs], writes=[BMb[hh]])
                for jg in range(2):
                    bY, bO = 2 * jg, 2 * jg + 1
                    ems = []
                    for jj in range(4):
                        j = 4 * jg + jj
                        for e2 in range(2):
                            ems.append(lambda e, jj=jj, j=j, e2=e2, jg=jg, bY=bY: e.matmul(
                                psF[:, bY, jj * 128:jj * 128 + CL], lhsT=xpad[0:CL, j * 256 + e2 * 128:j * 256 + e2 * 128 + 128],
                                rhs=Mb[jg][0:CL, (2 * jj + e2) * CL:(2 * jj + e2 + 1) * CL], start=(e2 == 0), stop=(e2 == 1)))
                    S.ops("pe", ems, reads=[Bxpad, BMb[jg]], writes=[BpsF[bY]])
                    S.ops("pe", [lambda e, jj=jj, jg=jg, bO=bO: e.matmul(
                        psF[:, bO, jj * 128:jj * 128 + CL], lhsT=hT[:, (4 * jg + jj) * 128:(4 * jg + jj + 1) * 128], rhs=BC[:, 2 + jg, tk],
                        start=True, stop=True) for jj in range(4)], reads=[BhT, BBC[2 + jg]], writes=[BpsF[bO]])
                    it = next_ft()
                    tmp = F32T[it]
                    t3 = tmp[:, 0:4 * CL].rearrange("p (j l) -> p j l", j=4)
                    pO = psF[:, bO, :].rearrange("p (j l) -> p j l", j=4)[:, :, 0:CL]
                    pY = psF[:, bY, :].rearrange("p (j l) -> p j l", j=4)[:, :, 0:CL]
                    S.op("dve", lambda e, t3=t3, pO=pO, jg=jg: e.tensor_tensor(out=t3, in0=pO, in1=EBt[:, 4 * jg:4 * jg + 4, tk], op=ALU.mult),
                         reads=[BpsF[bO]] + BEB[4 * jg:4 * jg + 4], writes=[BFT[it]])
                    S.op("dve", lambda e, t3=t3, pY=pY, jg=jg: e.tensor_tensor(out=YG[:, 4 * jg:4 * jg + 4, tk], in0=pY, in1=t3, op=ALU.add),
                         reads=[BpsF[bY], BFT[it]], writes=BYG[4 * jg:4 * jg + 4])
                S.ops("pe", [lambda e, j=j: e.matmul(psF[:, 4 + j // 4, (j % 4) * 128:(j % 4 + 1) * 128], lhsT=Xdd[0:CL, j * 128:(j + 1) * 128],
                                                     rhs=Btm[0:CL, (j // 4) * 128:(j // 4 + 1) * 128], start=True, stop=True)
                             for j in range(8)], reads=[BXdd, BBtm], writes=[BpsF[4], BpsF[5]])
                S.op("dve", lambda e, c=c: e.tensor_tensor(out=hst[:], in0=hst[:], in1=cd[:, :, c:c + 1].broadcast_to([128, 8, 128]), op=ALU.mult),
                     reads=[Bhst, Bcd], writes=[Bhst])
                S.op("dve", lambda e: e.tensor_tensor(out=hst[:], in0=hst[:], in1=psF[:, 4:6, :].rearrange("p b (j n) -> p (b j) n", n=128),
                                                      op=ALU.add), reads=[Bhst, BpsF[4], BpsF[5]], writes=[Bhst])
                ssd_state_T()
            if SUB < 4:
                return
            for j in range(8):
                S.op("dve", lambda e, j=j: e.scalar_tensor_tensor(out=YG[:, j, 0:T], in0=xs_[:, j, 0:T], scalar=dsk[:, j:j + 1], in1=YG[:, j, 0:T],
                                                                 op0=ALU.mult, op1=ALU.add), reads=[Bxs[j], BYG[j], Bconst], writes=[BYG[j]])
                S.op("dve", lambda e, j=j: e.tensor_tensor(out=YG[:, j, 0:T], in0=YG[:, j, 0:T], in1=zs[:, j, 0:T], op=ALU.mult),
                     reads=[BYG[j], Bzs[j]], writes=[BYG[j]])
            rms_norm(T, [YG[:, j, 0:T] for j in range(8)], BYG, "ssd_norm", 0, [yc[:, j, 0:T] for j in range(8)], Byc)

            if SUB < 5:
                return
            for j in range(8):
                stt = {}

                def consume(oc, ps_ap, Bps, j=j, stt=stt):
                    if oc == 0:
                        ig = next_ft()
                        S.op("act", lambda e: e.activation(out=F32T[ig][:, 0:T], in_=ps_ap, func=AF.Copy), reads=[Bps], writes=[BFT[ig]])
                        stt["g"] = ig
                    elif oc == 1:
                        ig = stt["g"]
                        acc, ia = conv_chunk(T, ps_ap, Bps, 3, "sc", 8, j, Bcar["sc"][j], "sc_conv_w", 0, "sc_conv_b", j,
                                             in_mul=(F32T[ig][:, 0:T], BFT[ig]))
                        stt["u"] = (acc, ia)
                    else:
                        acc, ia = stt["u"]
                        S.op("dve", lambda e: e.tensor_tensor(out=ysc[:, j, 0:T], in0=acc[:, 0:T], in1=ps_ap, op=ALU.mult),
                             reads=[BFT[ia], Bps], writes=[Bysc[j]])
                proj_chunks(U["sc"][j], T, rhs_fn, Bhn, 8, 384, 128, consume)
            if SUB < 6:
                return
            for i in range(4):
                def consume(oc, ps_ap, Bps, i=i):
                    m = 2 * i + oc
                    S.op("dve", lambda e: e.tensor_tensor(out=x_res[:, m, 0:T], in0=x_res[:, m, 0:T], in1=ps_ap, op=ALU.add),
                         reads=[Bx[m], Bps], writes=[Bx[m]])
                proj_chunks(U["abo"][i], T, lambda kc: (yc[:, kc, 0:T] if kc < 8 else ysc[:, kc - 8, 0:T]), Byc + Bysc, 16, 256, 128, consume, fine=(i == 0))

        def ssd_state_T():
            S.ops("pe", [lambda e, j=j: e.transpose(psF[:, 4 + j // 4, (j % 4) * 128:(j % 4 + 1) * 128], in_=hst[:, j, :], identity=identf[:])
                         for j in range(8)], reads=[Bhst, Bconst], writes=[BpsF[4], BpsF[5]])
            S.op("act", lambda e: e.activation(out=hT[:].rearrange("p (b n) -> p b n", b=2), in_=psF[:, 4:6, :], func=AF.Copy),
                 reads=[BpsF[4], BpsF[5]], writes=[BhT])

        def mixer_c(T):
            hn, Bhn = BF8[0], BBF8[0]
            gg, Bgg = BF8[1], BBF8[1]
            xbb, Bxbb = BF8[2], BBF8[2]
            yl, Byl = BF8[3], BBF8[3]
            rms_norm(T, [x_res[:, j, 0:T] for j in range(8)], Bx, "norm_mix", 8, [hn[:, j, 0:T] for j in range(8)], Bhn)
            rhs_fn = lambda kc: hn[:, kc, 0:T]
            def gate_unit(i):
                def consume(oc, ps_ap, Bps, i=i):
                    j = 4 * i + oc
                    S.op("act", lambda e: e.activation(out=gg[:, j, 0:T], in_=ps_ap, func=AF.Gelu_apprx_tanh), reads=[Bps], writes=[Bgg[j]])
                proj_chunks(U["lin"][i], T, rhs_fn, Bhn, 8, 512, 128, consume)
            for i in range(2):
                def consume(oc, ps_ap, Bps, i=i):
                    j = 4 * i + oc
                    acc, ia = conv_chunk(T, ps_ap, Bps, 4, "lc", 8, j, Bcar["lc"][j], "lru_conv_w", 0, "lru_conv_b", j)
                    S.op("act", lambda e: e.activation(out=YG[:, j, 0:T], in_=acc[:, 0:T], func=AF.Copy), reads=[BFT[ia]], writes=[BYG[j]])
                    S.op("pool", lambda e: e.tensor_copy(out=xbb[:, j, 0:T], in_=acc[:, 0:T]), reads=[BFT[ia]], writes=[Bxbb[j]])
                proj_chunks(U["lin"][2 + i], T, rhs_fn, Bhn, 8, 512, 128, consume, fine=(i == 0))
            def head_chain(h, slot, Bslot):
                for oc in range(2):
                    j = 2 * h + oc
                    bR, bI = next_bank(), next_bank()
                    for mat, bb in ((0, bR), (1, bI)):
                        S.ops("pe", [lambda e, kc=kc, mat=mat, bb=bb, h=h, oc=oc: e.matmul(
                            psF[:, bb, 0:T], lhsT=slot[:, ((mat * 2 + h % 2) * 2 + kc) * 256 + oc * 128:((mat * 2 + h % 2) * 2 + kc) * 256 + oc * 128 + 128],
                            rhs=xbb[:, 2 * h + kc, 0:T], start=(kc == 0), stop=(kc == 1)) for kc in range(2)],
                            reads=[Bslot, Bxbb[2 * h], Bxbb[2 * h + 1]], writes=[BpsF[bb]])
                    ir, ii, ia_, im, iu = [next_ft() for _ in range(5)]
                    r_, i_, a_, m_, u_ = [F32T[k][:, 0:T] for k in (ir, ii, ia_, im, iu)]
                    S.op("act", lambda e, r_=r_, bR=bR, j=j: e.activation(out=r_, in_=psF[:, bR, 0:T], func=AF.Tanh, bias=hb[:, j:j + 1], scale=0.5),
                         reads=[BpsF[bR], Bconst], writes=[BFT[ir]])
                    S.op("act", lambda e, i_=i_, bI=bI, j=j: e.activation(out=i_, in_=psF[:, bI, 0:T], func=AF.Tanh, bias=hb[:, 8 + j:9 + j], scale=0.5),
                         reads=[BpsF[bI], Bconst], writes=[BFT[ii]])
                    S.op("act", lambda e, a_=a_, r_=r_, j=j: e.activation(out=a_, in_=r_, func=AF.Exp, scale=clh[:, j:j + 1], bias=clh[:, j:j + 1]),
                         reads=[BFT[ir], Bconst], writes=[BFT[ia_]])
                    S.op("pool", lambda e, m_=m_, a_=a_: e.tensor_tensor(out=m_, in0=a_, in1=a_, op=ALU.mult),
                         reads=[BFT[ia_]], writes=[BFT[im]])
                    S.op("dve", lambda e, m_=m_: e.tensor_scalar(out=m_, in0=m_, scalar1=-0.25, scalar2=0.25, op0=ALU.mult, op1=ALU.add),
                         reads=[BFT[im]], writes=[BFT[im]])
                    S.op("act", lambda e, m_=m_: e.activation(out=m_, in_=m_, func=AF.Sqrt), reads=[BFT[im]], writes=[BFT[im]])
                    S.op("dve", lambda e, u_=u_, i_=i_, j=j: e.scalar_tensor_tensor(out=u_, in0=i_, scalar=1.0, in1=YG[:, j, 0:T],
                                                                                  op0=ALU.add, op1=ALU.mult),
                         reads=[BFT[ii], BYG[j]], writes=[BFT[iu]])
                    S.op("dve", lambda e, u_=u_, m_=m_: e.tensor_tensor(out=u_, in0=u_, in1=m_, op=ALU.mult),
                         reads=[BFT[iu], BFT[im]], writes=[BFT[iu]])
                    S.op("dve", lambda e, a_=a_, u_=u_, j=j: e.tensor_tensor_scan(out=YG[:, j, 0:T], data0=a_, data1=u_, initial=hl[:, j:j + 1],
                                                                               op0=ALU.mult, op1=ALU.add),
                         reads=[BFT[ia_], BFT[iu], Bhl[j]], writes=[BYG[j]])
                    S.op("pool", lambda e, j=j: e.tensor_copy(out=hl[:, j:j + 1], in_=YG[:, j, T - 1:T]), reads=[BYG[j]], writes=[Bhl[j]])

            def yl_ops(js):
                for j in js:
                    S.op("dve", lambda e, j=j: e.tensor_tensor(out=yl[:, j, 0:T], in0=YG[:, j, 0:T], in1=gg[:, j, 0:T], op=ALU.mult),
                         reads=[BYG[j], Bgg[j]], writes=[Byl[j]])

            sl, Bsl = w_acquire(U["ax"][0])
            head_chain(0, sl, Bsl)
            head_chain(1, sl, Bsl)
            gate_unit(0)
            yl_ops(range(0, 4))
            sl, Bsl = w_acquire(U["ax"][1])
            head_chain(2, sl, Bsl)
            head_chain(3, sl, Bsl)
            gate_unit(1)
            yl_ops(range(4, 8))
            w_release()
            for i in range(2):
                def consume(oc, ps_ap, Bps, i=i):
                    m = 4 * i + oc
                    S.op("dve", lambda e: e.tensor_tensor(out=x_res[:, m, 0:T], in0=x_res[:, m, 0:T], in1=ps_ap, op=ALU.add),
                         reads=[Bx[m], Bps], writes=[Bx[m]])
                proj_chunks(U["lout"][i], T, lambda kc: yl[:, kc, 0:T], Byl, 8, 512, 128, consume, fine=(i == 0))

        allcar = Bcar["ssd_conv"] + Bcar["sc"] + Bcar["lc"] + Bcar["f"][0] + Bcar["f"][1]

        def car_rows_src(name, ap):
            return flat_rows(ap)

        def init_state(kind):
            if kind == "p":
                S.op("pool", lambda e: e.memset(CAR[:], 0.0), writes=allcar + Bhl)
                S.op("pool", lambda e: e.memset(hl[:], 0.0), writes=Bhl)
                S.op("pool", lambda e: e.memset(hst[:], 0.0), writes=[Bhst])
                S.op("pool", lambda e: e.memset(hT[:], 0.0), writes=[BhT])
            else:
                ems = []
                for gi, (name, r) in enumerate(CAR_SPECS):
                    ems.append((rows[0:r, gi, :], flat_rows(st_in[name])))
                S.dma("sp", "strow", lambda e: [e.dma_start(out=a, in_=b) for a, b in ems], writes=[Brows], n=len(ems))
                for gi, (name, r) in enumerate(CAR_SPECS):
                    b = next_bank()
                    S.ops("pe", [lambda e, gi=gi, r=r, b=b: e.transpose(psF[:, b, 0:r], in_=rows[0:r, gi, :], identity=identf[0:r, 0:r])],
                          reads=[Brows, Bconst], writes=[BpsF[b]])
                    if name == "l":
                        S.op("dve", lambda e, b=b: e.tensor_copy(out=hl[:], in_=psF[:, b, 0:8]), reads=[BpsF[b]], writes=Bhl)
                    else:
                        S.op("dve", lambda e, b=b, r=r, name=name: e.tensor_copy(out=CAR[:, coff[name]:coff[name] + r], in_=psF[:, b, 0:r]),
                             reads=[BpsF[b]], writes=allcar)
                S.dma("sp", "sth", lambda e: e.dma_start(out=hst[0:64, :, :], in_=st_ssd.rearrange("(j two) p n -> two p j n", two=2)[0]),
                      writes=[Bhst])
                S.dma("sp", "sth", lambda e: e.dma_start(out=hst[64:128, :, :], in_=st_ssd.rearrange("(j two) p n -> two p j n", two=2)[1]),
                      reads=[Bhst], writes=[Bhst])
                ssd_state_T()

        def out_state(si):
            S.op("pool", lambda e: e.tensor_copy(out=CAR[:, coff["l"]:coff["l"] + 8], in_=hl[:]), reads=Bhl, writes=Bhl)
            for gi, (name, r) in enumerate(CAR_SPECS):
                b = next_bank()
                S.ops("pe", [lambda e, name=name, r=r, b=b: e.transpose(psF[0:r, b, 0:128], in_=CAR[:, coff[name]:coff[name] + r], identity=identf[:])],
                      reads=allcar + Bhl + [Bconst], writes=[BpsF[b]])
                S.op("dve", lambda e, gi=gi, r=r, b=b: e.tensor_copy(out=rows[0:r, gi, :], in_=psF[0:r, b, 0:128]), reads=[BpsF[b]], writes=[Brows])
            ems = []
            for gi, (name, r) in enumerate(CAR_SPECS):
                if name == "f":
                    for l in range(2):
                        ems.append((flat_rows(st_out["f"][l, si]), rows[44 * l:44 * l + 44, gi, :]))
                else:
                    ems.append((flat_rows(st_out[name][si]), rows[0:r, gi, :]))
            S.dma("sp", "ostrow", lambda e: [e.dma_start(out=a, in_=b) for a, b in ems], reads=[Brows], writes=[Brows], n=len(ems))
            o3 = o_ssd[si].rearrange("(j two) p n -> two p j n", two=2)
            S.dma("sp", "osth", lambda e: [e.dma_start(out=o3[0], in_=hst[0:64, :, :]), e.dma_start(out=o3[1], in_=hst[64:128, :, :])],
                  reads=[Bhst], writes=[Bhst], n=2)

        def x_load(tl):
            T_ = tl["T"]
            src_ = xp[tl["idx"]] if tl["kind"] == "p" else xs_in[0]
            ntok_ = min(128, T_)
            for tb in range(max(1, T_ // 128)):
                r0 = tl["t0"] + tb * 128
                S.dma("sp", "xin%d" % tb, lambda e, tb=tb, r0=r0: e.dma_start(out=xin[tb][0:ntok_, :], in_=src_[r0:r0 + ntok_, :]),
                      writes=[Bxin[tb]])

        for tli, tl in enumerate(tiles if STAGE >= 2 else []):
            T = tl["T"]
            src = xp[tl["idx"]] if tl["kind"] == "p" else xs_in[0]
            dst = yp[tl["idx"]] if tl["kind"] == "p" else ys[0]
            if tli == 0:
                x_load(tl)
            if tl["first"]:
                init_state(tl["kind"])
            NTB = max(1, T // 128)
            ntok = min(128, T)
            for tb in range(NTB):
                q = tb
                for jh in range(2):
                    b = next_bank()
                    S.ops("pe", [lambda e, jj=jj, q=q, b=b, jh=jh: e.transpose(
                        psF[:, b, jj * 128:jj * 128 + ntok], in_=xin[q][0:ntok, (4 * jh + jj) * 128:(4 * jh + jj + 1) * 128], identity=identf[0:ntok, 0:ntok])
                        for jj in range(4)], reads=[Bxin[q], Bconst], writes=[BpsF[b]])
                    en = evac_eng()
                    S.op(en, copy_op(en, x_res[:, 4 * jh:4 * jh + 4, tb * 128:tb * 128 + ntok],
                                     psF[:, b, :].rearrange("p (j t) -> p j t", j=4)[:, :, 0:ntok]),
                         reads=[BpsF[b]], writes=Bx[4 * jh:4 * jh + 4])
            if tli + 1 < len(tiles) and not (LAZY and tli == 0):
                x_load(tiles[tli + 1])
            if STAGE >= 3:
                mixer_ab(T)
            if STAGE >= 4:
                ffn(0, T)
            if STAGE >= 5:
                mixer_c(T)
            if STAGE >= 6:
                ffn(1, T)
            if LAZY and tli == 0 and len(tiles) > 1:
                x_load(tiles[1])
            rms_norm(T, [x_res[:, j, 0:T] for j in range(8)], Bx, "norm_final", 0, [YG[:, j, 0:T] for j in range(8)], BYG)
            for tb in range(NTB):
                q = tb % 2
                for jh in range(2):
                    b = next_bank()
                    S.ops("pe", [lambda e, jj=jj, b=b, jh=jh, tb=tb: e.transpose(
                        psF[0:ntok, b, jj * 128:(jj + 1) * 128], in_=YG[:, 4 * jh + jj, tb * 128:tb * 128 + ntok], identity=identf[:])
                        for jj in range(4)], reads=BYG[4 * jh:4 * jh + 4] + [Bconst], writes=[BpsF[b]])
                    en = evac_eng()
                    S.op(en, copy_op(en, yout[q][0:ntok, jh * 512:(jh + 1) * 512], psF[0:ntok, b, :]), reads=[BpsF[b]], writes=[Byout[q]])
                r0 = tl["t0"] + tb * 128
                S.dma("sp", "yout%d" % q, lambda e, q=q, r0=r0: e.dma_start(out=dst[r0:r0 + ntok, :], in_=yout[q][0:ntok, :]),
                      reads=[Byout[q]], writes=[Byout[q]])
            if tl["last"]:
                out_state(tl["si"])
        S.emit_all()
    return nc


_IN_ORDER = ["x_prompt", "x_sample", "state_ssd_conv", "state_ssd", "state_sconv", "state_lru_conv", "state_lru", "state_ffn_conv"]


def run(inputs, NPS, LP, LS, ncores):
    nc = build_nc(NPS, LP, LS)
    f = lambda a: np.ascontiguousarray(np.asarray(a, dtype=np.float32))
    in_maps = []
    for c in range(ncores):
        m = {"xp": f(inputs["x_prompt"][c * NPS:(c + 1) * NPS]), "xs": f(inputs["x_sample"][c:c + 1]),
             "st_ssd_conv": f(inputs["state_ssd_conv"][c]), "st_ssd": f(inputs["state_ssd"][c]), "st_sc": f(inputs["state_sconv"][c]),
             "st_lc": f(inputs["state_lru_conv"][c]), "st_l": f(inputs["state_lru"][c]), "st_f": f(inputs["state_ffn_conv"][:, c])}
        for k in W_SHAPES:
            m[k] = f(inputs[k])
        in_maps.append(m)
    res = run_bass_kernel_spmd(nc, in_maps, core_ids=list(range(ncores)))
    R = res.results
    cat = lambda k, sl: np.concatenate([r[k][sl] for r in R], axis=0)
    P, Sm = slice(0, NPS), slice(NPS, NPS + 1)
    y_p = cat("yp", slice(None))
    y_s = cat("ys", slice(None))
    outs = [y_p, y_s]
    for sl in (P, Sm):
        outs += [cat("o_ssd_conv", sl), cat("o_ssd", sl), cat("o_sc", sl), cat("o_lc", sl), cat("o_l", sl),
                 np.concatenate([r["o_f"][:, sl] for r in R], axis=1)]
    return tuple(np.ascontiguousarray(o.astype(np.float32)) for o in outs)


def kernel(**inputs):
    return run(inputs, 2, 4096, 64, NCORES)
```

```python
import numpy as np
from contextlib import ExitStack
import concourse.bass as bass
import concourse.mybir as mybir
from concourse.alu_op_type import AluOpType as ALU
from concourse.bass_utils import run_bass_kernel_spmd

F32 = mybir.dt.float32
BF16 = mybir.dt.bfloat16
AF = mybir.ActivationFunctionType
NCORES = 8
D = 1024
DFF = 2816
NJF = 22
EPS = 1e-6
USZ = 4096
import os
STAGE = int(os.environ.get('KSTAGE', '9'))
SUB = int(os.environ.get('KSUB', '9'))
LAZY = os.environ.get('KLAZY', '1') == '1'
LZQ = os.environ.get('KLZQ', 'act')


class Buf:
    __slots__ = ("name", "w", "rl", "x")

    def __init__(self, name):
        self.name = name
        self.w = None
        self.rl = []
        self.x = False


class Rec:
    def __init__(self):
        self.calls = []

    def __getattr__(self, name):
        def m(*a, **k):
            self.calls.append((name, a, k))
            return None
        return m


def _free(ap):
    n = 1
    for d in ap.shape[1:]:
        n *= d
    return n


STALLDBG = None


class Sched:
    ENGS = ("pe", "dve", "act", "pool", "sp")
    WIN = {"pe": int(os.environ.get("KWPE", "40")), "dve": int(os.environ.get("KWV", "24")), "act": int(os.environ.get("KWV", "24")),
           "pool": int(os.environ.get("KWV", "24")), "sp": 1}
    TBL = float(os.environ.get("KTBL", "1300"))
    SCL = {e: float(os.environ.get("KS_" + e, "1")) for e in ("pe", "dve", "act", "pool", "sp")}

    def __init__(self, nc, stack):
        self.nc = nc
        self.stack = stack
        self.nodes = []
        self.nbuf = 0

    def buf(self, name=None):
        self.nbuf += 1
        return Buf(name or ("b%d" % self.nbuf))

    def bufs(self, n, name="b"):
        return [self.buf("%s%d" % (name, i)) for i in range(n)]

    def _est(self, en, calls):
        t = 0.0
        for name, a, k in calls:
            if name == "dma_start":
                o = k["out"]
                t = max(t, 2000.0 + o.shape[0] * _free(o) * (2 if o.dtype == BF16 else 4) / 160.0)
            elif en == "pe":
                if name == "matmul":
                    rhs = k["rhs"]
                    t += (_free(rhs) * 0.45 + 14) * (4.0 if rhs.dtype == F32 else 1.0)
                else:
                    t += 220.0 if k["in_"].dtype == F32 else 70.0
            elif name == "dma_start":
                o = k["out"]
                t = max(t, 2000.0 + o.shape[0] * _free(o) * (2 if o.dtype == BF16 else 4) / 160.0)
            else:
                o = k.get("out", a[0] if a else None)
                n = _free(o) if o is not None else 64
                if en == "dve":
                    t += 130 + n * (2.1 if name == "tensor_tensor_scan" else (6.0 if name == "reciprocal" else 1.04))
                elif en == "act":
                    t += 230 + n * 0.83
                else:
                    t += 300 + n * 1.35
        return t

    def _add(self, en, calls, reads, writes, dkey=None):
        nid = len(self.nodes)
        deps = {}

        def need(d, sem):
            if d is None:
                return
            deps[d] = deps.get(d, False) or sem
        for b in reads:
            need(b.w, True)
            if b.x:
                for r in b.rl:
                    if self.nodes[r]["eng"] != en or self.nodes[r]["dkey"]:
                        need(r, True)
        for b in writes:
            if b.w is not None:
                need(b.w, True)
            for r in b.rl:
                need(r, True)
        for b in reads:
            b.rl.append(nid)
        for b in writes:
            b.w = nid
            b.rl = []
        aset = None
        if en == "act":
            fn = calls[0][2].get("func")
            aset = {AF.Exp: "exp", AF.Ln: "ln", AF.Silu: "silu", AF.Gelu_apprx_tanh: "gelu", AF.Sigmoid: "sig", AF.Sqrt: "sqrt",
                    AF.Tanh: "exp"}.get(fn)
        self.nodes.append(dict(eng=en, calls=calls, deps=list(deps.items()), dkey=dkey, ndma=len(calls) if dkey else 0,
                               dur=self._est(en, calls) * self.SCL[en], aset=aset))
        return nid

    def op(self, en, emit, reads=(), writes=()):
        self.ops(en, [emit], reads, writes)

    def ops(self, en, emits, reads=(), writes=()):
        rec = Rec()
        for e in emits:
            e(rec)
        assert len(rec.calls) >= 1
        self._add(en, rec.calls, reads, writes)

    def dma(self, qn, key, emit, reads=(), writes=(), n=None):
        rec = Rec()
        emit(rec)
        assert len(rec.calls) >= 1
        self._add(qn, rec.calls, reads, writes, dkey=key)

    def _schedule(self):
        nodes = self.nodes
        q = {e: [] for e in self.ENGS}
        for i, nd in enumerate(nodes):
            q[nd["eng"]].append(i)
        head = {e: 0 for e in self.ENGS}
        emitted = [False] * len(nodes)
        fin = [0.0] * len(nodes)
        etime = {e: 0.0 for e in self.ENGS}
        order = {e: [] for e in self.ENGS}
        remaining = len(nodes)
        cur_set = [None]
        nswitch = [0]
        self.nswitch = nswitch
        NOSCHED = os.environ.get("KNOSCHED", "0") == "1"
        while remaining:
            best = None
            for e in self.ENGS:
                lst = q[e]
                h = head[e]
                while h < len(lst) and emitted[lst[h]]:
                    h += 1
                head[e] = h
                if h >= len(lst):
                    continue
                cand = None
                cnt = 0
                i = h
                W = 1 if NOSCHED else self.WIN[e]
                while i < len(lst) and cnt < W:
                    nid = lst[i]
                    i += 1
                    if emitted[nid]:
                        continue
                    cnt += 1
                    ok = True
                    rdy = 0.0
                    for d, sem in nodes[nid]["deps"]:
                        if not emitted[d]:
                            ok = False
                            break
                        lat = 60.0 if not sem else (150.0 if nodes[d]["eng"] == e and not nodes[d]["dkey"] else 300.0)
                        fd = fin[d] + lat if sem else 0.0
                        if fd > rdy:
                            rdy = fd
                    if not ok:
                        continue
                    pen = 0.0
                    if e == "act" and nodes[nid]["aset"] is not None and nodes[nid]["aset"] != cur_set[0]:
                        pen = self.TBL
                    eff = max(rdy, etime[e]) + pen
                    if cand is None or eff < cand[0] - 1e-9:
                        cand = (eff, nid, pen)
                    if eff <= etime[e]:
                        break
                if cand is None:
                    continue
                start = cand[0]
                if best is None or start < best[0]:
                    best = (start, cand[1], e)
            assert best is not None, "scheduler deadlock"
            start, nid, e = best
            if STALLDBG is not None:
                crit = None
                cf = -1.0
                for d, sem in nodes[nid]["deps"]:
                    if sem and fin[d] > cf:
                        cf = fin[d]
                        crit = d
                STALLDBG.append((e, nid, start, etime[e], crit))
            emitted[nid] = True
            nd = nodes[nid]
            if e == "act" and nd["aset"] is not None:
                if nd["aset"] != cur_set[0]:
                    nswitch[0] += 1
                cur_set[0] = nd["aset"]
            if nd["dkey"]:
                etime[e] = start + 60.0
                fin[nid] = start + nd["dur"]
            else:
                fin[nid] = start + nd["dur"]
                etime[e] = fin[nid]
            order[e].append(nid)
            remaining -= 1
        self.makespan = max(fin) if fin else 0.0
        return order

    def emit_all(self, final_q="sp"):
        nc = self.nc
        nodes = self.nodes
        order = self._schedule()
        sems = {e: self.stack.enter_context(nc.semaphore("s_" + e)) for e in self.ENGS}
        dsems = {}
        ev = [None] * len(nodes)
        cnt = {e: 0 for e in self.ENGS}
        dcnt = {}
        for e in self.ENGS:
            for nid in order[e]:
                nd = nodes[nid]
                if nd["dkey"]:
                    k = nd["dkey"]
                    if k not in dsems:
                        dsems[k] = self.stack.enter_context(nc.semaphore("d_" + k))
                        dcnt[k] = 0
                    dcnt[k] += 16 * nd["ndma"]
                    ev[nid] = ("d_" + k, dsems[k], dcnt[k])
                else:
                    cnt[e] += 1
                    ev[nid] = ("s_" + e, sems[e], cnt[e])
        progs = {}
        for e in self.ENGS:
            prog = []
            waited = {}
            for nid in order[e]:
                nd = nodes[nid]
                for d, sem in nd["deps"]:
                    if not sem:
                        continue
                    sname, sh, val = ev[d]
                    if waited.get(sname, 0) >= val:
                        continue
                    waited[sname] = val
                    prog.append(("w", sh, val))
                prog.append(("i", nd["calls"], ev[nid][1], 16 if nd["dkey"] else 1, bool(nd["dkey"])))
            progs[e] = prog
        waited = {}
        fin_waits = []
        for k, sh in dsems.items():
            fin_waits.append(("w", sh, dcnt[k]))
        for e in ("pe", "dve", "act", "pool"):
            if cnt[e]:
                fin_waits.append(("w", sems[e], cnt[e]))
        progs[final_q] = progs[final_q] + fin_waits
        with nc.Block() as block:
            for n, attr in (("sp", "sync"), ("pe", "tensor"), ("dve", "vector"), ("act", "scalar"), ("pool", "gpsimd")):
                prog = progs[n]
                if not prog:
                    continue

                def body(eng, prog=prog):
                    for it in prog:
                        if it[0] == "w":
                            eng.wait_ge(it[1], it[2])
                        else:
                            _, calls, sem, inc, all_inc = it
                            for idx, (name, a, k) in enumerate(calls):
                                inst = getattr(eng, name)(*a, **k)
                                if all_inc or idx == len(calls) - 1:
                                    inst.then_inc(sem, inc)
                getattr(block, attr)(body)


VEC_SPECS = [
    ("norm_mix", 16), ("norm_ffn", 16), ("norm_final", 8), ("ssd_norm", 8), ("sc_conv_b", 8),
    ("lru_conv_b", 8), ("lru_lambda", 8), ("lru_ba", 8), ("lru_bx", 8), ("ssd_conv_w", 48),
    ("ssd_conv_b", 12), ("sc_conv_w", 24), ("lru_conv_w", 32), ("ffn_conv_w", 132), ("ffn_conv_b", 44),
]
W_SHAPES = {
    "norm_mix": [2, D], "norm_ffn": [2, D], "norm_final": [D], "ab_w_in": [D, 5648], "ssd_conv_w": [4, 1536],
    "ssd_conv_b": [1536], "ssd_dt_bias": [16], "ssd_a_log": [16], "ssd_d": [16], "ssd_norm": [D],
    "sc_conv_w": [3, D], "sc_conv_b": [D], "ab_w_out": [2048, D], "lru_w_in": [D, 2048], "lru_conv_w": [4, D],
    "lru_conv_b": [D], "lru_wa": [4, 256, 256], "lru_ba": [4, 256], "lru_wx": [4, 256, 256], "lru_bx": [4, 256],
    "lru_lambda": [D], "lru_w_out": [D, D], "ffn_w_in": [2, D, 2 * DFF], "ffn_conv_w": [2, 3, DFF],
    "ffn_conv_b": [2, DFF], "ffn_w_out": [2, DFF, D],
}
CAR_SPECS = [("ssd_conv", 36), ("sc", 16), ("lc", 24), ("l", 8), ("f", 88)]


def build_nc(NPS, LP, LS):
    nc = bass.Bass("TRN2", target_bir_lowering=False)
    NS = NPS + 1
    din = {}

    def DI(name, shape):
        din[name] = nc.dram_tensor(name, list(shape), F32, kind="ExternalInput").ap()
        return din[name]

    def DO(name, shape):
        return nc.dram_tensor(name, list(shape), F32, kind="ExternalOutput").ap()

    xp = DI("xp", [NPS, LP, D])
    xs_in = DI("xs", [1, LS, D])
    st_in = {"ssd_conv": DI("st_ssd_conv", [3, 1536]), "sc": DI("st_sc", [2, D]), "lc": DI("st_lc", [3, D]),
             "l": DI("st_l", [D]), "f": DI("st_f", [2, 2, DFF])}
    st_ssd = DI("st_ssd", [16, 64, 128])
    Wd = {k: DI(k, v) for k, v in W_SHAPES.items()}
    yp = DO("yp", [NPS, LP, D])
    ys = DO("ys", [1, LS, D])
    st_out = {"ssd_conv": DO("o_ssd_conv", [NS, 3, 1536]), "sc": DO("o_sc", [NS, 2, D]), "lc": DO("o_lc", [NS, 3, D]),
              "l": DO("o_l", [NS, D]), "f": DO("o_f", [2, NS, 2, DFF])}
    o_ssd = DO("o_ssd", [NS, 16, 64, 128])

    units = []

    def add_unit(nk, ncols, pieces, tag):
        units.append(dict(nk=nk, ncols=ncols, pieces=pieces, tag=tag))
        return len(units) - 1

    def colsrc(w2d, c0, cw):
        return lambda k0, nk: w2d[k0 * 128:(k0 + nk) * 128, c0:c0 + cw].rearrange("(k p) n -> p k n", p=128)

    abin = Wd["ab_w_in"]
    U = {}
    U["dt"] = add_unit(8, 48, [(colsrc(abin, 2560, 16), 0, 16), (colsrc(abin, 2560, 16), 32, 16)], "dt")
    U["xbc"] = [add_unit(8, 512, [(colsrc(abin, 1024 + 512 * i, 512), 0, 512)], "xbc") for i in range(3)]
    U["z"] = [add_unit(8, 512, [(colsrc(abin, 512 * i, 512), 0, 512)], "z") for i in range(2)]
    U["sc"] = [add_unit(8, 384, [(colsrc(abin, 3600 + 128 * j, 128), 0, 128), (colsrc(abin, 4624 + 128 * j, 128), 128, 128),
                                 (colsrc(abin, 2576 + 128 * j, 128), 256, 128)], "sc") for j in range(8)]
    U["abo"] = [add_unit(16, 256, [(colsrc(Wd["ab_w_out"], 256 * i, 256), 0, 256)], "abo") for i in range(4)]
    U["fin"] = []
    U["fout"] = []
    for l in range(2):
        wi = Wd["ffn_w_in"][l]
        U["fin"].append([add_unit(8, 512, [(colsrc(wi, 256 * i, 128), 0, 128), (colsrc(wi, DFF + 256 * i, 128), 128, 128),
                                           (colsrc(wi, 256 * i + 128, 128), 256, 128), (colsrc(wi, DFF + 256 * i + 128, 128), 384, 128)],
                                  "fin") for i in range(11)])
        U["fout"].append([add_unit(22, 128, [(colsrc(Wd["ffn_w_out"][l], 128 * m, 128), 0, 128)], "fout") for m in range(8)])
    U["lin"] = [add_unit(8, 512, [(colsrc(Wd["lru_w_in"], 512 * i, 512), 0, 512)], "lin") for i in range(4)]
    wa2 = Wd["lru_wa"].rearrange("h k n -> (h k) n")
    wx2 = Wd["lru_wx"].rearrange("h k n -> (h k) n")
    U["ax"] = [add_unit(8, 256, [("ax", wa2, wx2, hb)], "ax") for hb in (0, 2)]
    U["lout"] = [add_unit(8, 512, [(colsrc(Wd["lru_w_out"], 512 * i, 512), 0, 512)], "lout") for i in range(2)]
    NU = len(units)
    WS = nc.dram_tensor("WS", [NU, 128, USZ], BF16, kind="Internal").ap()

    tile_seq = []
    if STAGE >= 3:
        tile_seq += [U["dt"]]
        if SUB >= 2:
            tile_seq += U["xbc"] + U["z"]
        if SUB >= 5:
            tile_seq += U["sc"]
        if SUB >= 6:
            tile_seq += U["abo"]
    if STAGE >= 4:
        tile_seq += U["fin"][0] + U["fout"][0]
    if STAGE >= 5:
        tile_seq += [U["lin"][2], U["lin"][3], U["ax"][0], U["lin"][0], U["ax"][1], U["lin"][1]] + U["lout"]
    if STAGE >= 6:
        tile_seq += U["fin"][1] + U["fout"][1]

    seqs = [("p", i, LP) for i in range(NPS)] + [("s", 0, LS)]
    tiles = []
    for si, (kind, idx, L) in enumerate(seqs):
        T = min(512, L)
        assert L % T == 0
        for ti in range(L // T):
            tiles.append(dict(si=si, kind=kind, idx=idx, t0=ti * T, T=T, first=(ti == 0), last=(ti == L // T - 1)))
    full_seq = tile_seq * len(tiles)

    with ExitStack() as st:
        S = Sched(nc, st)

        def sb(name, shape, dt):
            return st.enter_context(nc.sbuf_tensor(name, shape, dt))

        def psum(name, shape, dt):
            return st.enter_context(nc.psum_tensor(name, shape, dt))

        NRING = int(os.environ.get('KNRING', '3'))
        ring = [sb("ring%d" % i, [128, USZ], BF16) for i in range(NRING)]
        Bring = S.bufs(NRING, "ring")
        x_res = sb("x_res", [128, 8, 512], F32)
        Bx = S.bufs(8, "xres")
        xin = [sb("xin%d" % i, [128, D], F32) for i in range(4)]
        Bxin = S.bufs(4, "xin")
        yout = [sb("yout%d" % i, [128, D], F32) for i in range(2)]
        Byout = S.bufs(2, "yout")
        BF8 = [sb("bf8_%d" % i, [128, 8, 512], BF16) for i in range(5)]
        BBF8 = [S.bufs(8, "bf8_%d_" % i) for i in range(5)]
        NFT = int(os.environ.get('KNFT', '11'))
        F32T = [sb("f32t%d" % i, [128, 516], F32) for i in range(NFT)]
        BFT = S.bufs(NFT, "f32t")
        BC = sb("BC", [128, 4, 512], BF16)
        BBC = S.bufs(4, "BC")
        YG = sb("YG", [128, 8, 512], F32)
        BYG = S.bufs(8, "YG")
        sqb = [sb("sqb%d" % i, [128, 512], BF16) for i in range(4)]
        Bsqb = S.bufs(4, "sqb")
        t_e = sb("t_e", [48, 512], F32)
        t_dt = sb("t_dt", [48, 512], F32)
        t_ln = t_e
        t_dA = sb("t_dA", [48, 512], F32)
        t_At = sb("t_At", [48, 512], F32)
        t_w = t_dA
        LT = sb("LT", [48, 512], F32)
        Bte, Btdt, Btln, BtdA, BtAt, Btw, BLT = S.bufs(7, "ssdsm")
        Btln = Bte
        Btw = BtdA
        RH = [sb("RH%d" % i, [48, 1024], F32) for i in range(2)]
        BRH = S.bufs(2, "RH")
        segm = sb("segm", [128, 1024], F32)
        Bsegm = S.buf("segm")
        Lb = sb("Lb", [128, 1024], BF16)
        BLb = S.buf("Lb")
        Mb = [sb("Mb%d" % i, [128, 1024], BF16) for i in range(2)]
        BMb = S.bufs(2, "Mb")
        xpad = sb("xpad", [128, 2048], BF16)
        Bxpad = S.buf("xpad")
        Xdd = sb("Xdd", [128, 1024], BF16)
        BXdd = S.buf("Xdd")
        Btm = sb("Btm", [128, 256], BF16)
        BBtm = S.buf("Btm")
        CBs = sb("CBs", [128, 256], BF16)
        BCBs = S.buf("CBs")
        wtm = sb("wtm", [128, 16], F32)
        Bwtm = S.buf("wtm")
        cd = sb("cd", [128, 8, 4], F32)
        Bcd = S.buf("cd")
        hst = sb("hst", [128, 8, 128], F32)
        Bhst = S.buf("hst")
        hT = sb("hT", [128, 1024], BF16)
        BhT = S.buf("hT")
        hl = sb("hl", [128, 8], F32)
        Bhl = S.bufs(8, "hl")
        rstd = sb("rstd", [128, 512], F32)
        Brstd = S.buf("rstd")
        identf = sb("identf", [128, 128], F32)
        identb = sb("identb", [128, 128], BF16)
        ones_bf = sb("ones_bf", [128, 128], BF16)
        negmask = sb("negmask", [128, 128], F32)
        rmask = sb("rmask", [48, 512], F32)
        SelHP = sb("SelHP", [16, 8, 128], F32)
        NVEC = sum(r for _, r in VEC_SPECS)
        CV = sb("CV", [128, 384], F32)
        CAR = sb("CAR", [128, 176], F32)
        rows = sb("rows", [128, 5, 128], F32)
        Brows = S.buf("rows")
        A48 = sb("A48", [48, 1], F32)
        dtb48 = sb("dtb48", [48, 1], F32)
        dsk = sb("dsk", [128, 8], F32)
        clc = sb("clc", [128, 8], F32)
        cl2 = sb("cl2", [128, 8], F32)
        clh = sb("clh", [128, 8], F32)
        hb = sb("hb", [128, 16], F32)
        Bconst = S.buf("const")
        BCV = S.buf("CV")

        psF = psum("psF", [128, 6, 512], F32)
        BpsF = S.bufs(6, "psF")
        psX = psum("psX", [128, 1024], BF16)
        BpsX = S.buf("psX")
        psM = psum("psM", [128, 512], F32)
        BpsM = S.buf("psM")
        psB = psM[:, 384:512].bitcast(BF16)
        print("psB", psB.shape, psB.ap, psB.offset)
        for b_ in BpsF + [BpsX, BpsM]:
            b_.x = True
        BpsB = BpsM

        voff = {}
        o = 0
        for name, r in VEC_SPECS:
            voff[name] = o
            o += r

        def cvc(name, idx):
            c = voff[name] + idx
            return CV[:, c:c + 1]

        coff = {}
        o = 0
        for name, r in CAR_SPECS:
            coff[name] = o
            o += r
        Bcar = {"ssd_conv": S.bufs(12, "c_xbc"), "sc": S.bufs(8, "c_sc"), "lc": S.bufs(8, "c_lc"), "l": Bhl,
                "f": [S.bufs(22, "c_f0_"), S.bufs(22, "c_f1_")]}

        def car_cols(name, nk, nj, j, base=0):
            c0 = coff[name] + base + j
            return CAR[:, c0:c0 + (nk - 1) * nj + 1:nj]

        bank_ctr = [0]

        def next_bank():
            b = bank_ctr[0] % 6
            bank_ctr[0] += 1
            return b

        ft_ctr = [0]

        def next_ft():
            i = ft_ctr[0] % NFT
            ft_ctr[0] += 1
            return i

        evac_ctr = [0]

        def evac_eng():
            evac_ctr[0] += 1
            return "act" if evac_ctr[0] % 2 else "dve"

        def copy_op(en, out, in_):
            if en == "act":
                return lambda e: e.activation(out=out, in_=in_, func=AF.Copy)
            return lambda e: e.tensor_copy(out=out, in_=in_)

        S.op("pool", lambda e: e.memset(identf[:], 1.0), writes=[Bconst])
        S.op("pool", lambda e: e.affine_select(out=identf[:], in_=identf[:], pattern=[[-1, 128]], compare_op=ALU.is_equal,
                                               fill=0.0, base=0, channel_multiplier=1), reads=[Bconst], writes=[Bconst])
        S.op("pool", lambda e: e.tensor_copy(out=identb[:], in_=identf[:]), reads=[Bconst], writes=[Bconst])
        S.op("pool", lambda e: e.memset(ones_bf[:], 1.0), writes=[Bconst])
        S.op("pool", lambda e: e.memset(negmask[:], 0.0), writes=[Bconst])
        S.op("pool", lambda e: e.affine_select(out=negmask[:], in_=negmask[:], pattern=[[1, 128]], compare_op=ALU.is_ge,
                                               fill=-1.0e5, base=0, channel_multiplier=-1), reads=[Bconst], writes=[Bconst])
        S.op("pool", lambda e: e.memset(rmask[:], 1.0), writes=[Bconst])
        S.op("pool", lambda e: e.memset(rmask[:, 0:512:128], 0.0), reads=[Bconst], writes=[Bconst])
        S.op("pool", lambda e: e.memset(SelHP[:], 1.0), writes=[Bconst])
        S.op("pool", lambda e: e.affine_select(out=SelHP[:], in_=SelHP[:], pattern=[[-2, 8], [-1, 2], [0, 64]],
                                               compare_op=ALU.is_equal, fill=0.0, base=0, channel_multiplier=1),
             reads=[Bconst], writes=[Bconst])
        S.op("pool", lambda e: e.memset(LT[:], 1.0), writes=[BLT])
        rh_cl = [0]

        def init_RH(CL):
            if rh_cl[0] == CL:
                return
            rh_cl[0] = CL
            for hh in range(2):
                S.op("pool", lambda e, hh=hh: e.memset(RH[hh][:], 0.0), writes=[BRH[hh]])
                S.op("pool", lambda e, hh=hh: e.memset(RH[hh][32:48, 0:8 * CL], 1.0), reads=[BRH[hh]], writes=[BRH[hh]])
                S.op("pool", lambda e, hh=hh: e.affine_select(
                    out=RH[hh][32:48, 0:8 * CL].rearrange("p (h l) -> p h l", h=8),
                    in_=RH[hh][32:48, 0:8 * CL].rearrange("p (h l) -> p h l", h=8),
                    pattern=[[-1, 8], [0, CL]], compare_op=ALU.is_equal, fill=0.0, base=-8 * hh, channel_multiplier=1),
                    reads=[BRH[hh]], writes=[BRH[hh]])
        S.op("pool", lambda e: e.memset(xpad[:], 0.0), writes=[Bxpad])
        S.op("pool", lambda e: e.memset(CAR[:], 0.0), writes=[Bconst])

        Bvst = S.buf("vst")
        vst = [x_res[:, 0, 0:128], x_res[:, 1, 0:128], x_res[:, 2, 0:128]]

        def flat_rows(ap):
            n = len(ap.shape)
            if n == 1:
                return ap.rearrange("(r c) -> r c", c=128)
            if n == 2:
                return ap.rearrange("a (r c) -> (a r) c", c=128)
            return ap.rearrange("a b (r c) -> (a b r) c", c=128)

        S.op("pool", lambda e: e.memset(x_res[:, 0:3, 0:128], 0.0), writes=[Bvst])
        ndm = 0
        emits = []
        for name, r in VEC_SPECS:
            src = flat_rows(Wd[name])
            o0 = voff[name]
            done = 0
            while done < r:
                g = (o0 + done) // 128
                p0 = (o0 + done) % 128
                n = min(r - done, 128 - p0)
                emits.append((x_res[p0:p0 + n, g, 0:128], src[done:done + n, :]))
                done += n
        S.dma("sp", "vec", lambda e: [e.dma_start(out=a, in_=b) for a, b in emits], reads=[Bvst], writes=[Bvst], n=len(emits))
        for g in range(3):
            S.ops("pe", [lambda e, g=g: e.transpose(psF[:, g, 0:128], in_=vst[g], identity=identf[:])],
                  reads=[Bvst, Bconst], writes=[BpsF[g]])
            S.op("dve", lambda e, g=g: e.tensor_copy(out=CV[:, g * 128:(g + 1) * 128], in_=psF[:, g, 0:128]),
                 reads=[BpsF[g]], writes=[BCV])
        S.op("pool", lambda e: e.memset(A48[:], 0.0), writes=[Bconst])
        S.op("pool", lambda e: e.memset(dtb48[:], 0.0), reads=[Bconst], writes=[Bconst])
        al = Wd["ssd_a_log"].rearrange("(h o) -> h o", o=1)
        db = Wd["ssd_dt_bias"].rearrange("(h o) -> h o", o=1)
        S.dma("sp", "c1", lambda e: [e.dma_start(out=A48[0:16, :], in_=al), e.dma_start(out=A48[32:48, :], in_=al),
                                     e.dma_start(out=dtb48[0:16, :], in_=db), e.dma_start(out=dtb48[32:48, :], in_=db)],
              reads=[Bconst], writes=[Bconst], n=4)
        S.op("act", lambda e: e.activation(out=A48[:], in_=A48[:], func=AF.Exp), reads=[Bconst], writes=[Bconst])
        S.op("dve", lambda e: e.tensor_scalar(out=A48[:], in0=A48[:], scalar1=-1.0, scalar2=None, op0=ALU.mult),
             reads=[Bconst], writes=[Bconst])
        dsrc = Wd["ssd_d"]
        S.dma("sp", "c2", lambda e: [
            e.dma_start(out=dsk[0:64, :], in_=bass.AP(dsrc.tensor, 0, [[0, 64], [2, 8]]), allow_slow_non_contiguous=True),
            e.dma_start(out=dsk[64:128, :], in_=bass.AP(dsrc.tensor, 1, [[0, 64], [2, 8]]), allow_slow_non_contiguous=True)],
            reads=[Bconst], writes=[Bconst], n=2)
        lamc = CV[:, voff["lru_lambda"]:voff["lru_lambda"] + 8]
        S.op("act", lambda e: e.activation(out=clc[:], in_=lamc, func=AF.Exp, scale=-1.0), reads=[BCV], writes=[Bconst])
        S.op("act", lambda e: e.activation(out=clc[:], in_=clc[:], func=AF.Ln, bias=1.0), reads=[Bconst], writes=[Bconst])
        S.op("dve", lambda e: e.tensor_scalar(out=cl2[:], in0=clc[:], scalar1=-16.0, scalar2=None, op0=ALU.mult),
             reads=[Bconst], writes=[Bconst])
        S.op("dve", lambda e: e.tensor_scalar(out=clh[:], in0=clc[:], scalar1=-4.0, scalar2=None, op0=ALU.mult),
             reads=[Bconst], writes=[Bconst])
        S.op("dve", lambda e: e.tensor_scalar(out=clc[:], in0=clc[:], scalar1=-8.0, scalar2=None, op0=ALU.mult),
             reads=[Bconst], writes=[Bconst])
        S.op("dve", lambda e: e.tensor_scalar(out=hb[:, 0:8], in0=CV[:, voff["lru_ba"]:voff["lru_ba"] + 8], scalar1=0.5, scalar2=None,
                                              op0=ALU.mult), reads=[BCV], writes=[Bconst])
        S.op("dve", lambda e: e.tensor_scalar(out=hb[:, 8:16], in0=CV[:, voff["lru_bx"]:voff["lru_bx"] + 8], scalar1=0.5, scalar2=None,
                                              op0=ALU.mult), reads=[BCV], writes=[Bconst])

        BWS = S.bufs(NU, "WS")
        NSTG = 4
        stg32 = [x_res[:, 0:4, :].rearrange("p a b -> p (a b)"), x_res[:, 4:8, :].rearrange("p a b -> p (a b)"),
                 YG[:, 0:4, :].rearrange("p a b -> p (a b)"), YG[:, 4:8, :].rearrange("p a b -> p (a b)")]
        stg16 = [BF8[0][:, 0:4, :].rearrange("p a b -> p (a b)"), BF8[0][:, 4:8, :].rearrange("p a b -> p (a b)"),
                 BF8[1][:, 0:4, :].rearrange("p a b -> p (a b)"), BF8[1][:, 4:8, :].rearrange("p a b -> p (a b)")]
        Bs32 = [Bvst, S.buf("s32b"), S.buf("s32c"), S.buf("s32d")]
        Bs16 = S.bufs(4, "s16")
        ceng = ["dve", "act", "pool"]
        jobs = []
        for u, un in enumerate(units if STAGE >= 1 else []):
            nk, ncols = un["nk"], un["ncols"]
            hk = nk // 2
            for half in range(2):
                jobs.append((u, un, hk, ncols, half))
        DEPTH = 2

        def cv_in(i):
            u, un, hk, ncols, half = jobs[i]
            s = i % NSTG
            n_el = hk * ncols
            s32 = stg32[s][:, 0:n_el].rearrange("p (k n) -> p k n", k=hk)
            if un["tag"] == "ax":
                _, a2, x2 = un["pieces"][0]
                srcm = a2 if half == 0 else x2
                pcs = [(s32, srcm.rearrange("(k p) n -> p k n", p=128))]
            else:
                pcs = [(s32[:, :, c0:c0 + cw], fn(half * hk, hk)) for fn, c0, cw in un["pieces"]]
            if un["tag"] == "dt":
                S.op("pool", lambda e: e.memset(s32, 0.0), reads=[Bs32[s]], writes=[Bs32[s]])
            S.dma("sp", "cvi%d" % s, lambda e: [e.dma_start(out=a_, in_=b_, allow_slow_non_contiguous=True) for a_, b_ in pcs],
                  reads=[Bs32[s]], writes=[Bs32[s]])

        def cv_out(i):
            u, un, hk, ncols, half = jobs[i]
            s = i % NSTG
            n_el = hk * ncols
            en = ceng[i % 3]
            S.op(en, copy_op(en, stg16[s][:, 0:n_el], stg32[s][:, 0:n_el]), reads=[Bs32[s]], writes=[Bs16[s]])
            S.dma("sp", "cvo%d" % s, lambda e: e.dma_start(out=WS[u, :, half * n_el:(half + 1) * n_el], in_=stg16[s][:, 0:n_el]),
                  reads=[Bs16[s]], writes=[BWS[u]])

        for i in range((len(jobs) + DEPTH) if not LAZY else 0):
            if i < len(jobs):
                cv_in(i)
            if i >= DEPTH:
                cv_out(i - DEPTH)
        def inherit(dsts, srcs):
            for b in dsts:
                for sb_ in srcs:
                    if sb_.w is not None:
                        b.rl.append(sb_.w)
                    b.rl.extend(sb_.rl)
        if not LAZY:
            inherit(Bx, Bs32[0:2])
            inherit(BYG, Bs32[2:4])
            inherit(BBF8[0], Bs16[0:2])
            inherit(BBF8[1], Bs16[2:4])

        wstate = dict(issued=0, cur=0)

        lazy_ctr = [0]
        lz_eng = ["act", "dve"]
        lz_stg = [(yout[0], Byout[0]), (xin[0], Bxin[0]), (xin[1], Bxin[1]), (yout[1], Byout[1]), (xin[2], Bxin[2]), (xin[3], Bxin[3])]

        def w_issue():
            i = wstate["issued"]
            if i >= len(full_seq):
                return
            u = full_seq[i]
            s = i % NRING
            un = units[u]
            nk, ncols = un["nk"], un["ncols"]
            n_el = nk * ncols
            if LAZY and i < len(tile_seq):
                kper = max(1, min(nk, 1024 // ncols))
                k0 = 0
                while k0 < nk:
                    kn = min(kper, nk - k0)
                    q = lazy_ctr[0] % len(lz_stg)
                    ne = kn * ncols
                    stg_t, stg_b = lz_stg[q]
                    s32 = stg_t[:, 0:ne].rearrange("p (k n) -> p k n", k=kn)
                    if un["tag"] == "ax":
                        _, a2, x2, hb = un["pieces"][0]
                        srcm = a2 if k0 < 4 else x2
                        kk = (k0 % 4) + hb * 2
                        pcs = [(s32, srcm[kk * 128:(kk + kn) * 128, :].rearrange("(k p) n -> p k n", p=128))]
                    else:
                        pcs = [(s32[:, :, c0:c0 + cw], fn(k0, kn)) for fn, c0, cw in un["pieces"]]
                    if un["tag"] == "dt":
                        S.op("pool", lambda e, s32=s32: e.memset(s32, 0.0), reads=[stg_b], writes=[stg_b])
                    S.dma("sp", "lz%d" % q, lambda e, pcs=pcs: [e.dma_start(out=a_, in_=b_, allow_slow_non_contiguous=True) for a_, b_ in pcs],
                          reads=[stg_b], writes=[stg_b])
                    en = lz_eng[lazy_ctr[0] % 2]
                    S.op(en, copy_op(en, ring[s][:, k0 * ncols:k0 * ncols + ne], stg_t[:, 0:ne]), reads=[stg_b], writes=[Bring[s]])
                    lazy_ctr[0] += 1
                    k0 += kn
                S.dma(LZQ, "lzo%d" % s, lambda e, u=u, s=s, n_el=n_el: e.dma_start(out=WS[u, :, 0:n_el], in_=ring[s][:, 0:n_el]),
                      reads=[Bring[s]], writes=[BWS[u]])
            else:
                S.dma("sp", "w%d" % s, lambda e, u=u, s=s, n_el=n_el: e.dma_start(out=ring[s][:, 0:n_el], in_=WS[u, :, 0:n_el]),
                      reads=[BWS[u]], writes=[Bring[s]])
            wstate["issued"] += 1

        def w_acquire(u_expected):
            i = wstate["cur"]
            assert full_seq[i] == u_expected, (i, full_seq[i], u_expected)
            while wstate["issued"] < min(i + NRING, len(full_seq)):
                w_issue()
            wstate["cur"] += 1
            s = i % NRING
            return ring[s], Bring[s]

        def w_release():
            while wstate["issued"] < min(wstate["cur"] + NRING - 1, len(full_seq)):
                w_issue()

        def rms_norm(T, src_aps, src_bufs, gname, gbase, dst_aps, dst_bufs):
            b = next_bank()
            for j in range(8):
                q = j % 4
                if False:
                    S.op("pool", lambda e, j=j, q=q: e.tensor_tensor(out=sqb[q][:, 0:T], in0=src_aps[j], in1=src_aps[j], op=ALU.mult),
                         reads=[src_bufs[j]], writes=[Bsqb[q]])
                else:
                    S.op("act", lambda e, j=j, q=q: e.activation(out=sqb[q][:, 0:T], in_=src_aps[j], func=AF.Square),
                         reads=[src_bufs[j]], writes=[Bsqb[q]])
                S.ops("pe", [lambda e, j=j, q=q, b=b: e.matmul(psF[:, b, 0:T], lhsT=ones_bf[:], rhs=sqb[q][:, 0:T],
                                                             start=(j == 0), stop=(j == 7))],
                      reads=[Bsqb[q], Bconst], writes=[BpsF[b]])
            S.op("act", lambda e, b=b: e.activation(out=rstd[:, 0:T], in_=psF[:, b, 0:T], func=AF.Sqrt, bias=EPS, scale=1.0 / D),
                 reads=[BpsF[b]], writes=[Brstd])
            S.op("dve", lambda e: e.reciprocal(out=rstd[:, 0:T], in_=rstd[:, 0:T]), reads=[Brstd], writes=[Brstd])
            for j in range(8):
                if False:
                    it = next_ft()
                    S.op("pool", lambda e, j=j, it=it: e.tensor_scalar(out=F32T[it][:, 0:T], in0=src_aps[j], scalar1=cvc(gname, gbase + j),
                                                                      scalar2=None, op0=ALU.mult), reads=[src_bufs[j], BCV], writes=[BFT[it]])
                    S.op("pool", lambda e, j=j, it=it: e.tensor_tensor(out=dst_aps[j], in0=F32T[it][:, 0:T], in1=rstd[:, 0:T], op=ALU.mult),
                         reads=[BFT[it], Brstd], writes=[dst_bufs[j]])
                else:
                    S.op("dve", lambda e, j=j: e.scalar_tensor_tensor(out=dst_aps[j], in0=src_aps[j], scalar=cvc(gname, gbase + j),
                                                                     in1=rstd[:, 0:T], op0=ALU.mult, op1=ALU.mult),
                         reads=[src_bufs[j], Brstd, BCV], writes=[dst_bufs[j]])

        def proj_chunks(u, T, rhs_fn, rhs_bufs, nk, ncols, M, consume, fine=False):
            slot, Bslot = w_acquire(u)
            noc = max(1, ncols // 128)
            rb = list(rhs_bufs)
            for oc in range(noc):
                b = next_bank()
                ems = []
                for kc in range(nk):
                    c0 = kc * ncols + oc * 128
                    ems.append(lambda e, kc=kc, c0=c0, b=b: e.matmul(psF[0:M, b, 0:T], lhsT=slot[:, c0:c0 + M], rhs=rhs_fn(kc),
                                                                      start=(kc == 0), stop=(kc == nk - 1)))
                if fine and oc == 0 and len(rb) == nk:
                    for kc in range(nk):
                        S.ops("pe", [ems[kc]], reads=[Bslot, rb[kc]], writes=[BpsF[b]])
                else:
                    S.ops("pe", ems, reads=[Bslot] + rb, writes=[BpsF[b]])
                consume(oc, psF[0:M, b, 0:T], BpsF[b])
            w_release()

        def conv_chunk(T, ps_ap, Bps, K, cname, nj, j, carbuf, wname, wbase, bname, bidx, in_mul=None, car_base=0, wstride=None):
            H = K - 1
            it = next_ft()
            tmp = F32T[it]
            cc = car_cols(cname, H, nj, j, car_base)
            S.op("pool", lambda e: e.tensor_copy(out=tmp[:, 0:H], in_=cc), reads=[carbuf], writes=[BFT[it]])
            if in_mul is None:
                S.op("act", lambda e: e.activation(out=tmp[:, H:H + T], in_=ps_ap, func=AF.Copy), reads=[Bps], writes=[BFT[it]])
            else:
                g_ap, g_buf = in_mul
                S.op("dve", lambda e: e.tensor_tensor(out=tmp[:, H:H + T], in0=g_ap, in1=ps_ap, op=ALU.mult),
                     reads=[Bps, g_buf], writes=[BFT[it]])
            S.op("pool", lambda e: e.tensor_copy(out=cc, in_=tmp[:, T:T + H]), reads=[BFT[it]], writes=[carbuf])
            ia = next_ft()
            acc = F32T[ia]
            ws = wstride if wstride is not None else nj
            S.op("dve", lambda e: e.tensor_scalar(out=acc[:, 0:T], in0=tmp[:, 0:T], scalar1=cvc(wname, wbase + j),
                                                  scalar2=cvc(bname, bidx), op0=ALU.mult, op1=ALU.add),
                 reads=[BFT[it], BCV], writes=[BFT[ia]])
            for k in range(1, K):
                S.op("dve", lambda e, k=k: e.scalar_tensor_tensor(out=acc[:, 0:T], in0=tmp[:, k:k + T], scalar=cvc(wname, wbase + k * ws + j),
                                                                 in1=acc[:, 0:T], op0=ALU.mult, op1=ALU.add),
                     reads=[BFT[it], BFT[ia], BCV], writes=[BFT[ia]])
            return acc, ia

        def ffn(l, T):
            hn, Bhn = BF8[0], BBF8[0]
            rms_norm(T, [x_res[:, j, 0:T] for j in range(8)], Bx, "norm_ffn", 8 * l, [hn[:, j, 0:T] for j in range(8)], Bhn)

            def a_ap(j):
                return BF8[1 + j // 8][:, j % 8, 0:T], BBF8[1 + j // 8][j % 8]
            for i in range(11):
                stt = {}

                def consume(oc, ps_ap, Bps, i=i, stt=stt):
                    j = 2 * i + oc // 2
                    if oc % 2 == 0:
                        acc, ia = conv_chunk(T, ps_ap, Bps, 3, "f", 22, j, Bcar["f"][l][j], "ffn_conv_w", l * 66, "ffn_conv_b", l * 22 + j,
                                             car_base=l * 44)
                        S.op("act", lambda e: e.activation(out=acc[:, 0:T], in_=acc[:, 0:T], func=AF.Gelu_apprx_tanh),
                             reads=[BFT[ia]], writes=[BFT[ia]])
                        stt["g"] = (acc, ia)
                    else:
                        acc, ia = stt["g"]
                        da, db_ = a_ap(j)
                        S.op("dve", lambda e: e.tensor_tensor(out=da, in0=acc[:, 0:T], in1=ps_ap, op=ALU.mult),
                             reads=[BFT[ia], Bps], writes=[db_])
                proj_chunks(U["fin"][l][i], T, lambda kc: hn[:, kc, 0:T], Bhn, 8, 512, 128, consume, fine=(i == 0))
            allA = [a_ap(j)[1] for j in range(22)]
            for m in range(8):
                def consume(oc, ps_ap, Bps, m=m):
                    S.op("dve", lambda e: e.tensor_tensor(out=x_res[:, m, 0:T], in0=x_res[:, m, 0:T], in1=ps_ap, op=ALU.add),
                         reads=[Bx[m], Bps], writes=[Bx[m]])
                proj_chunks(U["fout"][l][m], T, lambda kc: a_ap(kc)[0], allA, 22, 128, 128, consume, fine=(m == 0))

        def mixer_ab(T):
            CL = min(128, T)
            NCH = T // CL
            init_RH(CL)
            hn, Bhn = BF8[0], BBF8[0]
            zs, Bzs = BF8[1], BBF8[1]
            xs_, Bxs = BF8[2], BBF8[2]
            yc, Byc = BF8[3], BBF8[3]
            EBt, BEB = BF8[4], BBF8[4]
            ysc, Bysc = BF8[4], BBF8[4]
            rms_norm(T, [x_res[:, j, 0:T] for j in range(8)], Bx, "norm_mix", 0, [hn[:, j, 0:T] for j in range(8)], Bhn)
            rhs_fn = lambda kc: hn[:, kc, 0:T]

            def consume_dt(oc, ps_ap, Bps):
                S.op("act", lambda e: e.activation(out=t_e[:, 0:T], in_=ps_ap, func=AF.Exp, bias=dtb48[:, 0:1]),
                     reads=[Bps, Bconst], writes=[Bte])
                S.op("act", lambda e: e.activation(out=t_dt[:, 0:T], in_=t_e[:, 0:T], func=AF.Ln, bias=1.0), reads=[Bte], writes=[Btdt])
                S.op("act", lambda e: e.activation(out=t_ln[:, 0:T], in_=t_dt[:, 0:T], func=AF.Ln), reads=[Btdt], writes=[Btln])
                S.op("dve", lambda e: e.tensor_scalar(out=t_dA[:, 0:T], in0=t_dt[:, 0:T], scalar1=A48[:, 0:1], scalar2=None, op0=ALU.mult),
                     reads=[Btdt, Bconst], writes=[BtdA])
                S.op("dve", lambda e: e.tensor_tensor_scan(out=t_At[:, 0:T], data0=rmask[:, 0:T], data1=t_dA[:, 0:T], initial=0.0,
                                                          op0=ALU.mult, op1=ALU.add), reads=[BtdA, Bconst], writes=[BtAt])
                S.op("dve", lambda e: e.tensor_tensor(out=LT[32:48, 0:T], in0=t_ln[32:48, 0:T], in1=t_At[32:48, 0:T], op=ALU.subtract),
                     reads=[Btln, BtAt], writes=[BLT])
                At3 = t_At[0:16, 0:T].rearrange("p (c l) -> p c l", c=NCH)
                S.op("dve", lambda e: e.tensor_tensor(out=t_w[0:16, 0:T].rearrange("p (c l) -> p c l", c=NCH),
                                                      in0=At3[:, :, CL - 1:CL].broadcast_to([16, NCH, CL]), in1=At3, op=ALU.subtract),
                     reads=[BtAt], writes=[Btw])
                S.op("act", lambda e: e.activation(out=t_w[0:16, 0:T], in_=t_w[0:16, 0:T], func=AF.Exp), reads=[Btw], writes=[Btw])
                S.op("dve", lambda e: e.tensor_tensor(out=t_w[0:16, 0:T], in0=t_w[0:16, 0:T], in1=t_dt[0:16, 0:T], op=ALU.mult),
                     reads=[Btw, Btdt], writes=[Btw])
            proj_chunks(U["dt"], T, rhs_fn, Bhn, 8, 48, 48, consume_dt, fine=True)

            if SUB < 2:
                return
            for i in range(3):
                def consume(oc, ps_ap, Bps, i=i):
                    j = 4 * i + oc
                    acc, ia = conv_chunk(T, ps_ap, Bps, 4, "ssd_conv", 12, j, Bcar["ssd_conv"][j], "ssd_conv_w", 0, "ssd_conv_b", j)
                    if j < 8:
                        dst, dbf = xs_[:, j, 0:T], Bxs[j]
                    else:
                        dst, dbf = BC[:, j - 8, 0:T], BBC[j - 8]
                    S.op("act", lambda e: e.activation(out=dst, in_=acc[:, 0:T], func=AF.Silu), reads=[BFT[ia]], writes=[dbf])
                proj_chunks(U["xbc"][i], T, rhs_fn, Bhn, 8, 512, 128, consume)
            for i in range(2):
                def consume(oc, ps_ap, Bps, i=i):
                    j = 4 * i + oc
                    S.op("act", lambda e: e.activation(out=zs[:, j, 0:T], in_=ps_ap, func=AF.Silu), reads=[Bps], writes=[Bzs[j]])
                proj_chunks(U["z"][i], T, rhs_fn, Bhn, 8, 512, 128, consume)

            if SUB < 3:
                return
            for j in range(8):
                b = next_bank()
                S.ops("pe", [lambda e, j=j, b=b: e.matmul(psF[:, b, 0:T], lhsT=SelHP[0:16, j, :], rhs=t_At[0:16, 0:T], start=True, stop=True)],
                      reads=[BtAt, Bconst], writes=[BpsF[b]])
                S.op("act", lambda e, j=j, b=b: e.activation(out=EBt[:, j, 0:T], in_=psF[:, b, 0:T], func=AF.Exp),
                     reads=[BpsF[b]], writes=[BEB[j]])
                S.op("act", lambda e, j=j, b=b: e.activation(out=cd[:, j, 0:NCH], in_=psF[:, b, CL - 1:T:CL], func=AF.Exp),
                     reads=[BpsF[b]], writes=[Bcd])
            for c in range(NCH):
                tk = slice(c * CL, (c + 1) * CL)
                S.ops("pe", [lambda e, j=j: e.transpose(psX[0:CL, j * 128:(j + 1) * 128], in_=xs_[:, j, tk], identity=identb[:])
                             for j in range(8)], reads=Bxs + [Bconst], writes=[BpsX])
                S.ops("pe", [lambda e, g=g: e.transpose(psB[0:CL, g * 128:(g + 1) * 128], in_=BC[:, g, tk], identity=identb[:])
                             for g in range(2)], reads=[BBC[0], BBC[1], Bconst], writes=[BpsB])
                S.ops("pe", [lambda e: e.transpose(psM[0:CL, 256:272], in_=t_w[0:16, tk], identity=identf[0:16, 0:16])] +
                      [lambda e, g=g: e.matmul(psM[0:CL, g * 128:g * 128 + CL], lhsT=BC[:, g, tk], rhs=BC[:, 2 + g, tk], start=True, stop=True)
                       for g in range(2)], reads=[Btw, Bconst] + BBC, writes=[BpsM])
                S.op("act", lambda e: e.activation(out=wtm[0:CL, :], in_=psM[0:CL, 256:272], func=AF.Copy), reads=[BpsM], writes=[Bwtm])
                S.op("act", lambda e: e.activation(
                    out=bass.AP(xpad[:].tensor, xpad[:].offset, [[xpad[:].ap[0][0], CL], [256, 8], [192, 2], [1, 64]]),
                    in_=psX[0:CL, :].rearrange("p (j e d) -> p j e d", j=8, e=2), func=AF.Copy), reads=[BpsX], writes=[Bxpad])
                S.op("dve", lambda e: e.tensor_tensor(out=Xdd[0:CL, :].rearrange("p (h d) -> p h d", h=16),
                                                      in0=psX[0:CL, :].rearrange("p (h d) -> p h d", h=16),
                                                      in1=wtm[0:CL, :].unsqueeze(2).broadcast_to([CL, 16, 64]), op=ALU.mult),
                     reads=[BpsX, Bwtm], writes=[BXdd])
                S.op("act", lambda e: e.activation(out=Btm[0:CL, :], in_=psB[0:CL, :], func=AF.Copy), reads=[BpsB], writes=[BBtm])
                S.op("act", lambda e: e.activation(out=CBs[0:CL, :], in_=psM[0:CL, 0:256], func=AF.Copy), reads=[BpsM], writes=[BCBs])
                nq = (8 * CL) // 512
                for hh in range(2):
                    S.op("pool", lambda e, hh=hh: e.affine_select(
                        out=RH[hh][0:16, 0:8 * CL].rearrange("p (h l) -> p h l", h=8),
                        in_=t_At[0:16, tk].unsqueeze(1).broadcast_to([16, 8, CL]),
                        pattern=[[-1, 8], [0, CL]], compare_op=ALU.is_equal, fill=0.0, base=-8 * hh, channel_multiplier=1),
                        reads=[BtAt], writes=[BRH[hh]])
                    for q in range(nq):
                        S.ops("pe", [lambda e, hh=hh, q=q: e.matmul(psF[0:CL, 2 * hh + q, :], lhsT=LT[0:48, tk],
                                                                     rhs=RH[hh][0:48, q * 512:(q + 1) * 512], start=True, stop=True)],
                              reads=[BLT, BRH[hh]], writes=[BpsF[2 * hh + q]])
                    seg_ap = psF[0:CL, 2 * hh:2 * hh + nq, :].rearrange("p q (h l) -> p (q h) l", l=CL)
                    S.op("dve", lambda e, seg_ap=seg_ap: e.tensor_tensor(
                        out=segm[0:CL, 0:8 * CL].rearrange("p (h l) -> p h l", h=8), in0=seg_ap,
                        in1=negmask[0:CL, 0:CL].unsqueeze(1).broadcast_to([CL, 8, CL]), op=ALU.add),
                        reads=[BpsF[2 * hh + q] for q in range(nq)] + [Bconst], writes=[Bsegm])
                    S.op("act", lambda e: e.activation(out=Lb[0:CL, 0:8 * CL], in_=segm[0:CL, 0:8 * CL], func=AF.Exp),
                         reads=[Bsegm], writes=[BLb])
                    S.op("dve", lambda e, hh=hh: e.tensor_tensor(
                        out=Mb[hh][0:CL, 0:8 * CL].rearrange("p (h l) -> p h l", h=8),
                        in0=Lb[0:CL, 0:8 * CL].rearrange("p (h l) -> p h l", h=8),
                        in1=CBs[0:CL, hh * 128:hh * 128 + CL].unsqueeze(1).broadcast_to([CL, 8, CL]), op=ALU.mult),
                        reads=[BLb, BCBs], writes=[BMb[hh]])
                for jg in range(2):
                    bY, bO = 2 * jg, 2 * jg + 1
                    ems = []
                    for jj in range(4):
                        j = 4 * jg + jj
                        for e2 in range(2):
                            ems.append(lambda e, jj=jj, j=j, e2=e2, jg=jg, bY=bY: e.matmul(
                                psF[:, bY, jj * 128:jj * 128 + CL], lhsT=xpad[0:CL, j * 256 + e2 * 128:j * 256 + e2 * 128 + 128],
                                rhs=Mb[jg][0:CL, (2 * jj + e2) * CL:(2 * jj + e2 + 1) * CL], start=(e2 == 0), stop=(e2 == 1)))
                    S.ops("pe", ems, reads=[Bxpad, BMb[jg]], writes=[BpsF[bY]])
                    S.ops("pe", [lambda e, jj=jj, jg=jg, bO=bO: e.matmul(
                        psF[:, bO, jj * 128:jj * 128 + CL], lhsT=hT[:, (4 * jg + jj) * 128:(4 * jg + jj + 1) * 128], rhs=BC[:, 2 + jg, tk],
                        start=True, stop=True) for jj in range(4)], reads=[BhT, BBC[2 + jg]], writes=[BpsF[bO]])
                    it = next_ft()
                    tmp = F32T[it]
                    t3 = tmp[:, 0:4 * CL].rearrange("p (j l) -> p j l", j=4)
                    pO = psF[:, bO, :].rearrange("p (j l) -> p j l", j=4)[:, :, 0:CL]
                    pY = psF[:, bY, :].rearrange("p (j l) -> p j l", j=4)[:, :, 0:CL]
                    S.op("dve", lambda e, t3=t3, pO=pO, jg=jg: e.tensor_tensor(out=t3, in0=pO, in1=EBt[:, 4 * jg:4 * jg + 4, tk], op=ALU.mult),
                         reads=[BpsF[bO]] + BEB[4 * jg:4 * jg + 4], writes=[BFT[it]])
                    S.op("dve", lambda e, t3=t3, pY=pY, jg=jg: e.tensor_tensor(out=YG[:, 4 * jg:4 * jg + 4, tk], in0=pY, in1=t3, op=ALU.add),
                         reads=[BpsF[bY], BFT[it]], writes=BYG[4 * jg:4 * jg + 4])
                S.ops("pe", [lambda e, j=j: e.matmul(psF[:, 4 + j // 4, (j % 4) * 128:(j % 4 + 1) * 128], lhsT=Xdd[0:CL, j * 128:(j + 1) * 128],
                                                     rhs=Btm[0:CL, (j // 4) * 128:(j // 4 + 1) * 128], start=True, stop=True)
                             for j in range(8)], reads=[BXdd, BBtm], writes=[BpsF[4], BpsF[5]])
                S.op("dve", lambda e, c=c: e.tensor_tensor(out=hst[:], in0=hst[:], in1=cd[:, :, c:c + 1].broadcast_to([128, 8, 128]), op=ALU.mult),
                     reads=[Bhst, Bcd], writes=[Bhst])
                S.op("dve", lambda e: e.tensor_tensor(out=hst[:], in0=hst[:], in1=psF[:, 4:6, :].rearrange("p b (j n) -> p (b j) n", n=128),
                                                      op=ALU.add), reads=[Bhst, BpsF[4], BpsF[5]], writes=[Bhst])
                ssd_state_T()
            if SUB < 4:
                return
            for j in range(8):
                S.op("dve", lambda e, j=j: e.scalar_tensor_tensor(out=YG[:, j, 0:T], in0=xs_[:, j, 0:T], scalar=dsk[:, j:j + 1], in1=YG[:, j, 0:T],
                                                                 op0=ALU.mult, op1=ALU.add), reads=[Bxs[j], BYG[j], Bconst], writes=[BYG[j]])
                S.op("dve", lambda e, j=j: e.tensor_tensor(out=YG[:, j, 0:T], in0=YG[:, j, 0:T], in1=zs[:, j, 0:T], op=ALU.mult),
                     reads=[BYG[j], Bzs[j]], writes=[BYG[j]])
            rms_norm(T, [YG[:, j, 0:T] for j in range(8)], BYG, "ssd_norm", 0, [yc[:, j, 0:T] for j in range(8)], Byc)

            if SUB < 5:
                return
            for j in range(8):
                stt = {}

                def consume(oc, ps_ap, Bps, j=j, stt=stt):
                    if oc == 0:
                        ig = next_ft()
                        S.op("act", lambda e: e.activation(out=F32T[ig][:, 0:T], in_=ps_ap, func=AF.Copy), reads=[Bps], writes=[BFT[ig]])
                        stt["g"] = ig
                    elif oc == 1:
                        ig = stt["g"]
                        acc, ia = conv_chunk(T, ps_ap, Bps, 3, "sc", 8, j, Bcar["sc"][j], "sc_conv_w", 0, "sc_conv_b", j,
                                             in_mul=(F32T[ig][:, 0:T], BFT[ig]))
                        stt["u"] = (acc, ia)
                    else:
                        acc, ia = stt["u"]
                        S.op("dve", lambda e: e.tensor_tensor(out=ysc[:, j, 0:T], in0=acc[:, 0:T], in1=ps_ap, op=ALU.mult),
                             reads=[BFT[ia], Bps], writes=[Bysc[j]])
                proj_chunks(U["sc"][j], T, rhs_fn, Bhn, 8, 384, 128, consume)
            if SUB < 6:
                return
            for i in range(4):
                def consume(oc, ps_ap, Bps, i=i):
                    m = 2 * i + oc
                    S.op("dve", lambda e: e.tensor_tensor(out=x_res[:, m, 0:T], in0=x_res[:, m, 0:T], in1=ps_ap, op=ALU.add),
                         reads=[Bx[m], Bps], writes=[Bx[m]])
                proj_chunks(U["abo"][i], T, lambda kc: (yc[:, kc, 0:T] if kc < 8 else ysc[:, kc - 8, 0:T]), Byc + Bysc, 16, 256, 128, consume, fine=(i == 0))

        def ssd_state_T():
            S.ops("pe", [lambda e, j=j: e.transpose(psF[:, 4 + j // 4, (j % 4) * 128:(j % 4 + 1) * 128], in_=hst[:, j, :], identity=identf[:])
                         for j in range(8)], reads=[Bhst, Bconst], writes=[BpsF[4], BpsF[5]])
            S.op("act", lambda e: e.activation(out=hT[:].rearrange("p (b n) -> p b n", b=2), in_=psF[:, 4:6, :], func=AF.Copy),
                 reads=[BpsF[4], BpsF[5]], writes=[BhT])

        def mixer_c(T):
            hn, Bhn = BF8[0], BBF8[0]
            gg, Bgg = BF8[1], BBF8[1]
            xbb, Bxbb = BF8[2], BBF8[2]
            yl, Byl = BF8[3], BBF8[3]
            rms_norm(T, [x_res[:, j, 0:T] for j in range(8)], Bx, "norm_mix", 8, [hn[:, j, 0:T] for j in range(8)], Bhn)
            rhs_fn = lambda kc: hn[:, kc, 0:T]
            def gate_unit(i):
                def consume(oc, ps_ap, Bps, i=i):
                    j = 4 * i + oc
                    S.op("act", lambda e: e.activation(out=gg[:, j, 0:T], in_=ps_ap, func=AF.Gelu_apprx_tanh), reads=[Bps], writes=[Bgg[j]])
                proj_chunks(U["lin"][i], T, rhs_fn, Bhn, 8, 512, 128, consume)
            for i in range(2):
                def consume(oc, ps_ap, Bps, i=i):
                    j = 4 * i + oc
                    acc, ia = conv_chunk(T, ps_ap, Bps, 4, "lc", 8, j, Bcar["lc"][j], "lru_conv_w", 0, "lru_conv_b", j)
                    S.op("act", lambda e: e.activation(out=YG[:, j, 0:T], in_=acc[:, 0:T], func=AF.Copy), reads=[BFT[ia]], writes=[BYG[j]])
                    S.op("pool", lambda e: e.tensor_copy(out=xbb[:, j, 0:T], in_=acc[:, 0:T]), reads=[BFT[ia]], writes=[Bxbb[j]])
                proj_chunks(U["lin"][2 + i], T, rhs_fn, Bhn, 8, 512, 128, consume, fine=(i == 0))
            def head_chain(h, slot, Bslot):
                for oc in range(2):
                    j = 2 * h + oc
                    bR, bI = next_bank(), next_bank()
                    for mat, bb in ((0, bR), (1, bI)):
                        S.ops("pe", [lambda e, kc=kc, mat=mat, bb=bb, h=h, oc=oc: e.matmul(
                            psF[:, bb, 0:T], lhsT=slot[:, ((mat * 2 + h % 2) * 2 + kc) * 256 + oc * 128:((mat * 2 + h % 2) * 2 + kc) * 256 + oc * 128 + 128],
                            rhs=xbb[:, 2 * h + kc, 0:T], start=(kc == 0), stop=(kc == 1)) for kc in range(2)],
                            reads=[Bslot, Bxbb[2 * h], Bxbb[2 * h + 1]], writes=[BpsF[bb]])
                    ir, ii, ia_, im, iu = [next_ft() for _ in range(5)]
                    r_, i_, a_, m_, u_ = [F32T[k][:, 0:T] for k in (ir, ii, ia_, im, iu)]
                    S.op("act", lambda e, r_=r_, bR=bR, j=j: e.activation(out=r_, in_=psF[:, bR, 0:T], func=AF.Tanh, bias=hb[:, j:j + 1], scale=0.5),
                         reads=[BpsF[bR], Bconst], writes=[BFT[ir]])
                    S.op("act", lambda e, i_=i_, bI=bI, j=j: e.activation(out=i_, in_=psF[:, bI, 0:T], func=AF.Tanh, bias=hb[:, 8 + j:9 + j], scale=0.5),
                         reads=[BpsF[bI], Bconst], writes=[BFT[ii]])
                    S.op("act", lambda e, a_=a_, r_=r_, j=j: e.activation(out=a_, in_=r_, func=AF.Exp, scale=clh[:, j:j + 1], bias=clh[:, j:j + 1]),
                         reads=[BFT[ir], Bconst], writes=[BFT[ia_]])
                    S.op("pool", lambda e, m_=m_, a_=a_: e.tensor_tensor(out=m_, in0=a_, in1=a_, op=ALU.mult),
                         reads=[BFT[ia_]], writes=[BFT[im]])
                    S.op("dve", lambda e, m_=m_: e.tensor_scalar(out=m_, in0=m_, scalar1=-0.25, scalar2=0.25, op0=ALU.mult, op1=ALU.add),
                         reads=[BFT[im]], writes=[BFT[im]])
                    S.op("act", lambda e, m_=m_: e.activation(out=m_, in_=m_, func=AF.Sqrt), reads=[BFT[im]], writes=[BFT[im]])
                    S.op("dve", lambda e, u_=u_, i_=i_, j=j: e.scalar_tensor_tensor(out=u_, in0=i_, scalar=1.0, in1=YG[:, j, 0:T],
                                                                                  op0=ALU.add, op1=ALU.mult),
                         reads=[BFT[ii], BYG[j]], writes=[BFT[iu]])
                    S.op("dve", lambda e, u_=u_, m_=m_: e.tensor_tensor(out=u_, in0=u_, in1=m_, op=ALU.mult),
                         reads=[BFT[iu], BFT[im]], writes=[BFT[iu]])
                    S.op("dve", lambda e, a_=a_, u_=u_, j=j: e.tensor_tensor_scan(out=YG[:, j, 0:T], data0=a_, data1=u_, initial=hl[:, j:j + 1],
                                                                               op0=ALU.mult, op1=ALU.add),
                         reads=[BFT[ia_], BFT[iu], Bhl[j]], writes=[BYG[j]])
                    S.op("pool", lambda e, j=j: e.tensor_copy(out=hl[:, j:j + 1], in_=YG[:, j, T - 1:T]), reads=[BYG[j]], writes=[Bhl[j]])

            def yl_ops(js):
                for j in js:
                    S.op("dve", lambda e, j=j: e.tensor_tensor(out=yl[:, j, 0:T], in0=YG[:, j, 0:T], in1=gg[:, j, 0:T], op=ALU.mult),
                         reads=[BYG[j], Bgg[j]], writes=[Byl[j]])

            sl, Bsl = w_acquire(U["ax"][0])
            head_chain(0, sl, Bsl)
            head_chain(1, sl, Bsl)
            gate_unit(0)
            yl_ops(range(0, 4))
            sl, Bsl = w_acquire(U["ax"][1])
            head_chain(2, sl, Bsl)
            head_chain(3, sl, Bsl)
            gate_unit(1)
            yl_ops(range(4, 8))
            w_release()
            for i in range(2):
                def consume(oc, ps_ap, Bps, i=i):
                    m = 4 * i + oc
                    S.op("dve", lambda e: e.tensor_tensor(out=x_res[:, m, 0:T], in0=x_res[:, m, 0:T], in1=ps_ap, op=ALU.add),
                         reads=[Bx[m], Bps], writes=[Bx[m]])
                proj_chunks(U["lout"][i], T, lambda kc: yl[:, kc, 0:T], Byl, 8, 512, 128, consume, fine=(i == 0))

        allcar = Bcar["ssd_conv"] + Bcar["sc"] + Bcar["lc"] + Bcar["f"][0] + Bcar["f"][1]

        def car_rows_src(name, ap):
            return flat_rows(ap)

        def init_state(kind):
            if kind == "p":
                S.op("pool", lambda e: e.memset(CAR[:], 0.0), writes=allcar + Bhl)
                S.op("pool", lambda e: e.memset(hl[:], 0.0), writes=Bhl)
                S.op("pool", lambda e: e.memset(hst[:], 0.0), writes=[Bhst])
                S.op("pool", lambda e: e.memset(hT[:], 0.0), writes=[BhT])
            else:
                ems = []
                for gi, (name, r) in enumerate(CAR_SPECS):
                    ems.append((rows[0:r, gi, :], flat_rows(st_in[name])))
                S.dma("sp", "strow", lambda e: [e.dma_start(out=a, in_=b) for a, b in ems], writes=[Brows], n=len(ems))
                for gi, (name, r) in enumerate(CAR_SPECS):
                    b = next_bank()
                    S.ops("pe", [lambda e, gi=gi, r=r, b=b: e.transpose(psF[:, b, 0:r], in_=rows[0:r, gi, :], identity=identf[0:r, 0:r])],
                          reads=[Brows, Bconst], writes=[BpsF[b]])
                    if name == "l":
                        S.op("dve", lambda e, b=b: e.tensor_copy(out=hl[:], in_=psF[:, b, 0:8]), reads=[BpsF[b]], writes=Bhl)
                    else:
                        S.op("dve", lambda e, b=b, r=r, name=name: e.tensor_copy(out=CAR[:, coff[name]:coff[name] + r], in_=psF[:, b, 0:r]),
                             reads=[BpsF[b]], writes=allcar)
                S.dma("sp", "sth", lambda e: e.dma_start(out=hst[0:64, :, :], in_=st_ssd.rearrange("(j two) p n -> two p j n", two=2)[0]),
                      writes=[Bhst])
                S.dma("sp", "sth", lambda e: e.dma_start(out=hst[64:128, :, :], in_=st_ssd.rearrange("(j two) p n -> two p j n", two=2)[1]),
                      reads=[Bhst], writes=[Bhst])
                ssd_state_T()

        def out_state(si):
            S.op("pool", lambda e: e.tensor_copy(out=CAR[:, coff["l"]:coff["l"] + 8], in_=hl[:]), reads=Bhl, writes=Bhl)
            for gi, (name, r) in enumerate(CAR_SPECS):
                b = next_bank()
                S.ops("pe", [lambda e, name=name, r=r, b=b: e.transpose(psF[0:r, b, 0:128], in_=CAR[:, coff[name]:coff[name] + r], identity=identf[:])],
                      reads=allcar + Bhl + [Bconst], writes=[BpsF[b]])
                S.op("dve", lambda e, gi=gi, r=r, b=b: e.tensor_copy(out=rows[0:r, gi, :], in_=psF[0:r, b, 0:128]), reads=[BpsF[b]], writes=[Brows])
            ems = []
            for gi, (name, r) in enumerate(CAR_SPECS):
                if name == "f":
                    for l in range(2):
                        ems.append((flat_rows(st_out["f"][l, si]), rows[44 * l:44 * l + 44, gi, :]))
                else:
                    ems.append((flat_rows(st_out[name][si]), rows[0:r, gi, :]))
            S.dma("sp", "ostrow", lambda e: [e.dma_start(out=a, in_=b) for a, b in ems], reads=[Brows], writes=[Brows], n=len(ems))
            o3 = o_ssd[si].rearrange("(j two) p n -> two p j n", two=2)
            S.dma("sp", "osth", lambda e: [e.dma_start(out=o3[0], in_=hst[0:64, :, :]), e.dma_start(out=o3[1], in_=hst[64:128, :, :])],
                  reads=[Bhst], writes=[Bhst], n=2)

        def x_load(tl):
            T_ = tl["T"]
            src_ = xp[tl["idx"]] if tl["kind"] == "p" else xs_in[0]
            ntok_ = min(128, T_)
            for tb in range(max(1, T_ // 128)):
                r0 = tl["t0"] + tb * 128
                S.dma("sp", "xin%d" % tb, lambda e, tb=tb, r0=r0: e.dma_start(out=xin[tb][0:ntok_, :], in_=src_[r0:r0 + ntok_, :]),
                      writes=[Bxin[tb]])

        for tli, tl in enumerate(tiles if STAGE >= 2 else []):
            T = tl["T"]
            src = xp[tl["idx"]] if tl["kind"] == "p" else xs_in[0]
            dst = yp[tl["idx"]] if tl["kind"] == "p" else ys[0]
            if tli == 0:
                x_load(tl)
            if tl["first"]:
                init_state(tl["kind"])
            NTB = max(1, T // 128)
            ntok = min(128, T)
            for tb in range(NTB):
                q = tb
                for jh in range(2):
                    b = next_bank()
                    S.ops("pe", [lambda e, jj=jj, q=q, b=b, jh=jh: e.transpose(
                        psF[:, b, jj * 128:jj * 128 + ntok], in_=xin[q][0:ntok, (4 * jh + jj) * 128:(4 * jh + jj + 1) * 128], identity=identf[0:ntok, 0:ntok])
                        for jj in range(4)], reads=[Bxin[q], Bconst], writes=[BpsF[b]])
                    en = evac_eng()
                    S.op(en, copy_op(en, x_res[:, 4 * jh:4 * jh + 4, tb * 128:tb * 128 + ntok],
                                     psF[:, b, :].rearrange("p (j t) -> p j t", j=4)[:, :, 0:ntok]),
                         reads=[BpsF[b]], writes=Bx[4 * jh:4 * jh + 4])
            if tli + 1 < len(tiles) and not (LAZY and tli == 0):
                x_load(tiles[tli + 1])
            if STAGE >= 3:
                mixer_ab(T)
            if STAGE >= 4:
                ffn(0, T)
            if STAGE >= 5:
                mixer_c(T)
            if STAGE >= 6:
                ffn(1, T)
            if LAZY and tli == 0 and len(tiles) > 1:
                x_load(tiles[1])
            rms_norm(T, [x_res[:, j, 0:T] for j in range(8)], Bx, "norm_final", 0, [YG[:, j, 0:T] for j in range(8)], BYG)
            for tb in range(NTB):
                q = tb % 2
                for jh in range(2):
                    b = next_bank()
                    S.ops("pe", [lambda e, jj=jj, b=b, jh=jh, tb=tb: e.transpose(
                        psF[0:ntok, b, jj * 128:(jj + 1) * 128], in_=YG[:, 4 * jh + jj, tb * 128:tb * 128 + ntok], identity=identf[:])
                        for jj in range(4)], reads=BYG[4 * jh:4 * jh + 4] + [Bconst], writes=[BpsF[b]])
                    en = evac_eng()
                    S.op(en, copy_op(en, yout[q][0:ntok, jh * 512:(jh + 1) * 512], psF[0:ntok, b, :]), reads=[BpsF[b]], writes=[Byout[q]])
                r0 = tl["t0"] + tb * 128
                S.dma("sp", "yout%d" % q, lambda e, q=q, r0=r0: e.dma_start(out=dst[r0:r0 + ntok, :], in_=yout[q][0:ntok, :]),
                      reads=[Byout[q]], writes=[Byout[q]])
            if tl["last"]:
                out_state(tl["si"])
        S.emit_all()
    return nc


_IN_ORDER = ["x_prompt", "x_sample", "state_ssd_conv", "state_ssd", "state_sconv", "state_lru_conv", "state_lru", "state_ffn_conv"]


def run(inputs, NPS, LP, LS, ncores):
    nc = build_nc(NPS, LP, LS)
    f = lambda a: np.ascontiguousarray(np.asarray(a, dtype=np.float32))
    in_maps = []
    for c in range(ncores):
        m = {"xp": f(inputs["x_prompt"][c * NPS:(c + 1) * NPS]), "xs": f(inputs["x_sample"][c:c + 1]),
             "st_ssd_conv": f(inputs["state_ssd_conv"][c]), "st_ssd": f(inputs["state_ssd"][c]), "st_sc": f(inputs["state_sconv"][c]),
             "st_lc": f(inputs["state_lru_conv"][c]), "st_l": f(inputs["state_lru"][c]), "st_f": f(inputs["state_ffn_conv"][:, c])}
        for k in W_SHAPES:
            m[k] = f(inputs[k])
        in_maps.append(m)
    res = run_bass_kernel_spmd(nc, in_maps, core_ids=list(range(ncores)))
    R = res.results
    cat = lambda k, sl: np.concatenate([r[k][sl] for r in R], axis=0)
    P, Sm = slice(0, NPS), slice(NPS, NPS + 1)
    y_p = cat("yp", slice(None))
    y_s = cat("ys", slice(None))
    outs = [y_p, y_s]
    for sl in (P, Sm):
        outs += [cat("o_ssd_conv", sl), cat("o_ssd", sl), cat("o_sc", sl), cat("o_lc", sl), cat("o_l", sl),
                 np.concatenate([r["o_f"][:, sl] for r in R], axis=1)]
    return tuple(np.ascontiguousarray(o.astype(np.float32)) for o in outs)


def kernel(**inputs):
    return run(inputs, 2, 4096, 64, NCORES)
```
